# Optimizing a Trainium2 kernel written in Bass

```python
import jax
import jax.numpy as jnp
from jax import lax
import numpy as np

D_MODEL = 1024
BATCH = 16
SEQ = 2048
DEPTH = 2
DEC_BATCH = 8
DEC_SEQ = 64
PAST_LEN = 2048

CHUNK = 64
N_META = 16
EPS = 1e-6

R_HEADS = 4
R_DK = 64
R_DV = 128
ROPE_BASE = 10000.0
M_HEADS = 4
M_DH = 128
CONV_W = 4
G_HEADS = 4
G_DK = 64
G_DV = 128
G_RANK = 16
G_NORMALIZER = 16.0

R_QK = R_HEADS * R_DK
R_V = R_HEADS * R_DV
M_W = M_HEADS * M_DH
G_QK = G_HEADS * G_DK
G_V = G_HEADS * G_DV
D_FF = ((-(-8 * D_MODEL // 3)) + 255) // 256 * 256
IN_WIDTHS = (R_QK, R_QK, R_V, R_V, M_W, M_W, M_HEADS, M_HEADS,
             G_QK, G_QK, G_V, G_V, G_RANK, D_MODEL, D_MODEL, D_MODEL)
D_IN = sum(IN_WIDTHS)

kernel_name = 'hybrid_streaming_retention_mlstm_gla_step'


def _rmsnorm(x, g):
    xf = x.astype(jnp.float32)
    y = xf * lax.rsqrt(jnp.mean(xf * xf, axis=-1, keepdims=True) + EPS)
    return (y * g.astype(jnp.float32)).astype(x.dtype)


def _head_norm(o):
    return o * lax.rsqrt(jnp.mean(o * o, axis=-1, keepdims=True) + EPS)


def _rotary(x, pos):
    half = x.shape[-1] // 2
    inv = 1.0 / (ROPE_BASE ** jnp.linspace(0.0, 1.0, half, dtype=jnp.float32))
    ang = pos[:, None] * inv[None, :]
    cos = jnp.cos(ang)[None, :, None, :]
    sin = jnp.sin(ang)[None, :, None, :]
    x1, x2 = x[..., :half], x[..., half:]
    return jnp.concatenate([x1 * cos - x2 * sin, x1 * sin + x2 * cos], axis=-1)


def _causal_conv(x, buf, w, b):
    T = x.shape[1]
    xc = jnp.concatenate([buf.astype(x.dtype), x], axis=1)
    y = b.astype(x.dtype) + sum(xc[:, j:j + T] * w[j].astype(x.dtype) for j in range(CONV_W))
    return y, xc[:, xc.shape[1] - (CONV_W - 1):]


def _gated_linear_chunk(q, k, v, log_a, s0, chunk):
    B, T, H, dk = q.shape
    dv = v.shape[-1]
    n = T // chunk
    q = q.reshape(B, n, chunk, H, dk)
    k = k.reshape(B, n, chunk, H, dk)
    v = v.reshape(B, n, chunk, H, dv)
    b = jnp.cumsum(log_a.reshape(B, n, chunk, H, dk), axis=2)
    b_last = b[:, :, -1]
    q_in = q * jnp.exp(b)
    k_in = k * jnp.exp(-b)
    k_st = k * jnp.exp(b_last[:, :, None] - b)
    mask = jnp.tril(jnp.ones((chunk, chunk), dtype=bool))
    scores = jnp.where(mask, jnp.einsum('bnthc,bnshc->bnhts', q_in, k_in), 0.0)
    o = jnp.einsum('bnhts,bnshv->bnthv', scores, v)
    ds = jnp.einsum('bnshc,bnshv->bnhcv', k_st, v)

    def step(s, inp):
        dec, d = inp
        return dec[..., None] * s + d, s

    s_fin, s_prev = lax.scan(step, s0, (jnp.moveaxis(jnp.exp(b_last), 1, 0), jnp.moveaxis(ds, 1, 0)))
    s_prev = jnp.moveaxis(s_prev, 0, 1)
    o = o + jnp.einsum('bnthc,bnhcv->bnthv', q_in, s_prev)
    return o.reshape(B, T, H, dv), s_fin


def _mlstm_chunk(q, k, v, ig, lf, state, chunk):
    c0, n0, m0 = state
    B, T, H, d = q.shape
    n = T // chunk
    q, k, v = (a.reshape(B, n, chunk, H, d) for a in (q, k, v))
    ig = ig.reshape(B, n, chunk, H)
    b = jnp.cumsum(lf.reshape(B, n, chunk, H), axis=2)
    a = ig - b
    g = lax.cummax(a, axis=2)
    b_last = b[:, :, -1]
    m_loc = b_last + g[:, :, -1]
    w_st = jnp.exp(a + (b_last - m_loc)[:, :, None])
    dc = jnp.einsum('bnsh,bnshk,bnshv->bnhkv', w_st, k, v)
    dn = jnp.einsum('bnsh,bnshk->bnhk', w_st, k)

    def step(carry, inp):
        c, nn, m = carry
        bl, ml, dci, dni = inp
        m_new = jnp.maximum(bl + m, ml)
        s_old = jnp.exp(bl + m - m_new)
        s_new = jnp.exp(ml - m_new)
        c_new = s_old[..., None, None] * c + s_new[..., None, None] * dci
        n_new = s_old[..., None] * nn + s_new[..., None] * dni
        return (c_new, n_new, m_new), (c, nn, m)

    mv = lambda t: jnp.moveaxis(t, 1, 0)
    (c_f, n_f, m_f), (c_p, n_p, m_p) = lax.scan(step, (c0, n0, m0), (mv(b_last), mv(m_loc), mv(dc), mv(dn)))
    c_p, n_p, m_p = (jnp.moveaxis(t, 0, 1) for t in (c_p, n_p, m_p))
    m_t = b + jnp.maximum(m_p[:, :, None], g)
    w_inter = jnp.exp(b + m_p[:, :, None] - m_t)
    log_d = jnp.swapaxes(b - m_t, 2, 3)[..., :, None] + jnp.swapaxes(a, 2, 3)[..., None, :]
    mask = jnp.tril(jnp.ones((chunk, chunk), dtype=bool))
    dmat = jnp.exp(jnp.where(mask, log_d, -jnp.inf))
    scores = jnp.einsum('bnthd,bnshd->bnhts', q, k) * dmat
    num = (jnp.einsum('bnhts,bnshv->bnthv', scores, v)
           + w_inter[..., None] * jnp.einsum('bnthk,bnhkv->bnthv', q, c_p))
    den = jnp.swapaxes(scores.sum(-1), 2, 3) + w_inter * jnp.einsum('bnthk,bnhk->bnth', q, n_p)
    h = num / jnp.maximum(jnp.abs(den), jnp.exp(-m_t))[..., None]
    return h.reshape(B, T, H, d), (c_f, n_f, m_f)


def _over_segments(fn, arrays, state, segs):
    outs, start = [], 0
    for length in segs:
        part = [a[:, start:start + length] for a in arrays]
        o, state = fn(*part, state, min(CHUNK, length))
        outs.append(o)
        start += length
    return jnp.concatenate(outs, axis=1), state


def _layer(x, pos, segs, st, w):
    s_ret, c_m, n_m, m_m, conv_buf, s_gla = st
    (norm1, w_in, b_i, b_f, conv_w, conv_b, w_mq, w_mk, w_mv, m_skip, w_a2, b_a,
     w_br_ret, w_br_mlstm, w_br_gla, w_out, norm2, w_ffn_in, w_ffn_out) = w
    f32 = jnp.float32
    B, T, _ = x.shape
    h = _rmsnorm(x, norm1)
    offs = tuple(int(o) for o in np.cumsum(IN_WIDTHS)[:-1])
    (rq, rk, rv, rg, mx, mz, mi, mf, gq, gk, gv, gr, ga,
     z_ret, z_mlstm, z_gla) = jnp.split(h @ w_in, offs, axis=-1)

    log_gamma = jnp.log(1.0 - 2.0 ** (-5.0 - jnp.arange(R_HEADS, dtype=f32)))
    rq_h = _rotary(rq.astype(f32).reshape(B, T, R_HEADS, R_DK), pos)
    rk_h = _rotary(rk.astype(f32).reshape(B, T, R_HEADS, R_DK), pos) * (R_DK ** -0.5)
    rv_h = rv.astype(f32).reshape(B, T, R_HEADS, R_DV)
    la_r = jnp.broadcast_to(log_gamma[None, None, :, None], rq_h.shape)
    o_r, s_ret = _over_segments(_gated_linear_chunk, (rq_h, rk_h, rv_h, la_r), s_ret.astype(f32), segs)
    o_r = _head_norm(o_r).reshape(B, T, R_V) * jax.nn.silu(rg.astype(f32))
    p_ret = o_r.astype(x.dtype) @ w_br_ret

    c_pre, conv_new = _causal_conv(mx, conv_buf, conv_w, conv_b)
    c = jax.nn.silu(c_pre.astype(f32))
    c_h = c.reshape(B, T, M_HEADS, M_DH)
    mq = jnp.einsum('bthd,hde->bthe', c_h, w_mq.astype(f32))
    mk = jnp.einsum('bthd,hde->bthe', c_h, w_mk.astype(f32)) * (M_DH ** -0.5)
    mvv = jnp.einsum('bthd,hde->bthe', mx.astype(f32).reshape(B, T, M_HEADS, M_DH), w_mv.astype(f32))
    ig = mi.astype(f32) + b_i.astype(f32)
    lf = jax.nn.log_sigmoid(mf.astype(f32) + b_f.astype(f32))
    h_m, (c_m, n_m, m_m) = _over_segments(
        _mlstm_chunk, (mq, mk, mvv, ig, lf),
        (c_m.astype(f32), n_m.astype(f32), m_m.astype(f32)), segs)
    o_m = jax.nn.sigmoid(mz.astype(f32)) * (_head_norm(h_m).reshape(B, T, M_W) + m_skip.astype(f32) * c)
    p_mlstm = o_m.astype(x.dtype) @ w_br_mlstm

    gq_h = gq.astype(f32).reshape(B, T, G_HEADS, G_DK) * (G_DK ** -0.5)
    gk_h = gk.astype(f32).reshape(B, T, G_HEADS, G_DK)
    gv_h = gv.astype(f32).reshape(B, T, G_HEADS, G_DV)
    la_g = (jax.nn.log_sigmoid((ga @ w_a2 + b_a).astype(f32)) / G_NORMALIZER).reshape(B, T, G_HEADS, G_DK)
    o_g, s_gla = _over_segments(_gated_linear_chunk, (gq_h, gk_h, gv_h, la_g), s_gla.astype(f32), segs)
    o_g = _head_norm(o_g).reshape(B, T, G_V) * jax.nn.silu(gr.astype(f32))
    p_gla = o_g.astype(x.dtype) @ w_br_gla

    sg = lambda z: jax.nn.sigmoid(z.astype(f32)).astype(x.dtype)
    mix = sg(z_ret) * p_ret + sg(z_mlstm) * p_mlstm + sg(z_gla) * p_gla
    x = x + mix @ w_out

    h2 = _rmsnorm(x, norm2)
    u_g, u_v = jnp.split(h2 @ w_ffn_in, 2, axis=-1)
    x = x + (jax.nn.silu(u_g) * u_v) @ w_ffn_out
    return x, (s_ret, c_m, n_m, m_m, conv_new, s_gla)


def _trunk(x, pos, segs, states, weights, norm_f):
    new = []
    for l in range(DEPTH):
        x, st = _layer(x, pos, segs, tuple(s[l] for s in states), tuple(w[l] for w in weights))
        new.append(st)
    stacked = tuple(jnp.stack([st[i] for st in new]) for i in range(6))
    return _rmsnorm(x, norm_f), stacked


def setup_inputs(seed: int = 0) -> dict:
    key = jax.random.key(seed)
    ks = jax.random.split(key, 32)
    nrm = lambda k, shape, s: jax.random.normal(k, shape, jnp.float32) * s
    return {
        'x_prompt': nrm(ks[0], (BATCH, SEQ, D_MODEL), 1.0),
        'x_sample': nrm(ks[1], (DEC_BATCH, DEC_SEQ, D_MODEL), 1.0),
        'state_ret': nrm(ks[2], (DEPTH, DEC_BATCH, R_HEADS, R_DK, R_DV), 0.3),
        'state_mlstm_c': nrm(ks[3], (DEPTH, DEC_BATCH, M_HEADS, M_DH, M_DH), 0.3),
        'state_mlstm_n': nrm(ks[4], (DEPTH, DEC_BATCH, M_HEADS, M_DH), 0.3),
        'state_mlstm_m': 1.0 + nrm(ks[5], (DEPTH, DEC_BATCH, M_HEADS), 0.5),
        'state_mlstm_conv': nrm(ks[6], (DEPTH, DEC_BATCH, CONV_W - 1, M_W), 1.0),
        'state_gla': nrm(ks[7], (DEPTH, DEC_BATCH, G_HEADS, G_DK, G_DV), 0.3),
        'meta_tokens': nrm(ks[8], (N_META, D_MODEL), 1.0),
        'norm1': 1.0 + nrm(ks[9], (DEPTH, D_MODEL), 0.01),
        'w_in': nrm(ks[10], (DEPTH, D_MODEL, D_IN), D_MODEL ** -0.5),
        'b_mlstm_i': nrm(ks[11], (DEPTH, M_HEADS), 0.1),
        'b_mlstm_f': jnp.linspace(3.0, 6.0, M_HEADS, dtype=jnp.float32)[None] + nrm(ks[12], (DEPTH, M_HEADS), 0.1),
        'conv_w': nrm(ks[13], (DEPTH, CONV_W, M_W), CONV_W ** -0.5),
        'conv_b': nrm(ks[14], (DEPTH, M_W), 0.01),
        'w_mq': nrm(ks[15], (DEPTH, M_HEADS, M_DH, M_DH), M_DH ** -0.5),
        'w_mk': nrm(ks[16], (DEPTH, M_HEADS, M_DH, M_DH), M_DH ** -0.5),
        'w_mv': nrm(ks[17], (DEPTH, M_HEADS, M_DH, M_DH), M_DH ** -0.5),
        'm_skip': 1.0 + nrm(ks[18], (DEPTH, M_W), 0.1),
        'w_gla_a2': nrm(ks[19], (DEPTH, G_RANK, G_QK), G_RANK ** -0.5),
        'b_gla_a': nrm(ks[20], (DEPTH, G_QK), 0.1),
        'w_br_ret': nrm(ks[21], (DEPTH, R_V, D_MODEL), R_V ** -0.5),
        'w_br_mlstm': nrm(ks[22], (DEPTH, M_W, D_MODEL), M_W ** -0.5),
        'w_br_gla': nrm(ks[23], (DEPTH, G_V, D_MODEL), G_V ** -0.5),
        'w_out': nrm(ks[24], (DEPTH, D_MODEL, D_MODEL), D_MODEL ** -0.5),
        'norm2': 1.0 + nrm(ks[25], (DEPTH, D_MODEL), 0.01),
        'w_ffn_in': nrm(ks[26], (DEPTH, D_MODEL, 2 * D_FF), D_MODEL ** -0.5),
        'w_ffn_out': nrm(ks[27], (DEPTH, D_FF, D_MODEL), D_FF ** -0.5),
        'norm_f': 1.0 + nrm(ks[28], (D_MODEL,), 0.01),
    }


def reference(x_prompt, x_sample, state_ret, state_mlstm_c, state_mlstm_n, state_mlstm_m,
              state_mlstm_conv, state_gla, meta_tokens, norm1, w_in, b_mlstm_i, b_mlstm_f,
              conv_w, conv_b, w_mq, w_mk, w_mv, m_skip, w_gla_a2, b_gla_a, w_br_ret,
              w_br_mlstm, w_br_gla, w_out, norm2, w_ffn_in, w_ffn_out, norm_f):
    f32 = jnp.float32
    weights = (norm1, w_in, b_mlstm_i, b_mlstm_f, conv_w, conv_b, w_mq, w_mk, w_mv, m_skip,
               w_gla_a2, b_gla_a, w_br_ret, w_br_mlstm, w_br_gla, w_out, norm2, w_ffn_in, w_ffn_out)

    B, S, _ = x_prompt.shape
    meta = jnp.broadcast_to(meta_tokens.astype(x_prompt.dtype)[None], (B, N_META, D_MODEL))
    xp = jnp.concatenate([meta, x_prompt], axis=1)
    pos_p = jnp.arange(N_META + S, dtype=f32)
    zeros = (jnp.zeros((DEPTH, B, R_HEADS, R_DK, R_DV), f32),
             jnp.zeros((DEPTH, B, M_HEADS, M_DH, M_DH), f32),
             jnp.zeros((DEPTH, B, M_HEADS, M_DH), f32),
             jnp.zeros((DEPTH, B, M_HEADS), f32),
             jnp.zeros((DEPTH, B, CONV_W - 1, M_W), x_prompt.dtype),
             jnp.zeros((DEPTH, B, G_HEADS, G_DK, G_DV), f32))
    yp, (p_ret, p_c, p_n, p_m, p_conv, p_gla) = _trunk(xp, pos_p, (N_META, S), zeros, weights, norm_f)
    y_prompt = yp[:, N_META:]

    T = x_sample.shape[1]
    pos_s = (N_META + PAST_LEN) + jnp.arange(T, dtype=f32)
    y_sample, (s_ret, s_c, s_n, s_m, s_conv, s_gla) = _trunk(
        x_sample, pos_s, (T,),
        (state_ret, state_mlstm_c, state_mlstm_n, state_mlstm_m, state_mlstm_conv, state_gla),
        weights, norm_f)
    return (y_prompt, y_sample, p_ret, p_c, p_n, p_m, p_conv, p_gla, s_ret, s_c, s_n, s_m, s_conv, s_gla)
```

```python
import numpy as np
import concourse.bass as bass
import concourse.mybir as mybir
from concourse.bass_utils import run_bass_kernel_spmd

F32 = mybir.dt.float32
BF16 = mybir.dt.bfloat16
ALU = mybir.AluOpType
AF = mybir.ActivationFunctionType

D = 1024
KC = 8
SEQ = 2048
NMETA = 16
DSEQ = 64
DEPTH = 2
DFF = 2816
EPS = 1e-6
OFF = dict(rq=0, rk=256, rv=512, rg=1024, mx=1536, mz=2048, mi=2560, mf=2564, gq=2568, gk=2824,
           gv=3080, gr=3592, ga=4104, zr=4120, zm=5144, zg=6168)
DIN = 7192
NPOS = NMETA + SEQ + DSEQ
LENS = (16, 64, 128)
GAM = [1.0 - 2.0 ** (-5.0 - h) for h in range(4)]

C_ESC, C_QSC = 3116, 3117
C_ID, C_MASK, C_DR, C_GQ, C_KS, C_SEL, C_RM, C_RA, C_RMS, C_RAS, C_ONE, C_END = (
    0, 128, 256, 768, 1280, 1292, 1804, 2316, 2828, 2908, 2988, 3120)
V_N1, V_N2, V_CW, V_CB, V_SK, V_BA, V_BI, V_BF, V_NF, V_END = 0, 8, 16, 32, 36, 40, 44, 45, 46, 64


def job_parts(kind, idx):
    h = idx
    W = 'w_in'
    if kind == 'ret':
        return [([(W, OFF['rq'] + 64 * h, 64), ('w_rs', 64 * h, 64)], 0, 8, 128),
                ([(W, OFF['rk'] + 64 * h, 64), ('w_rs', 256 + 64 * h, 64)], 0, 8, 128),
                ([(W, OFF['rg'] + 128 * h, 128)], 0, 8, 128)]
    if kind == 'retv':
        return [([(W, OFF['rv'], 512)], 0, 8, 128)]
    if kind == 'glav':
        return [([(W, OFF['gv'], 512)], 0, 8, 128)]
    if kind == 'gla':
        return [([(W, OFF['gq'] + 64 * h, 64), (W, OFF['gk'] + 64 * h, 64)], 0, 8, 128),
                ([(W, OFF['gr'] + 128 * h, 128)], 0, 8, 128)]
    if kind == 'ml':
        return [([(W, OFF['mx'] + 128 * h, 128)], 0, 8, 128), ([(W, OFF['mz'] + 128 * h, 128)], 0, 8, 128),
                ([('w_mq', 0, 128)], 128 * h, 1, 128), ([('w_mk', 0, 128)], 128 * h, 1, 128),
                ([('w_mv', 0, 128)], 128 * h, 1, 128)]
    if kind == 'merge':
        ft = idx
        return [([(W, OFF['zr'] + 128 * ft, 128)], 0, 8, 128), ([(W, OFF['zm'] + 128 * ft, 128)], 0, 8, 128),
                ([(W, OFF['zg'] + 128 * ft, 128)], 0, 8, 128), ([('w_br0', 128 * ft, 128)], 0, 4, 128),
                ([('w_br1', 128 * ft, 128)], 0, 4, 128), ([('w_br2', 128 * ft, 128)], 0, 4, 128)]
    if kind == 'mergeb':
        ft, b = idx // 3, idx % 3
        zoff = (OFF['zr'], OFF['zm'], OFF['zg'])[b]
        return [([(W, zoff + 128 * ft, 128)], 0, 8, 128), ([('w_br%d' % b, 128 * ft, 128)], 0, 4, 128)]
    if kind == 'out':
        return [([('w_out', 128 * idx, 128)], 0, 8, 128)]
    if kind == 'ffi':
        return [([('w_fi', 128 * idx, 128)], 0, 8, 128), ([('w_fi', DFF + 128 * idx, 128)], 0, 8, 128)]
    if kind == 'ffo':
        half, ft = idx // 8, idx % 8
        return [([('w_fo', 128 * ft, 128)], half * 1408, 11, 128)]
    if kind == 'gaw':
        return [([(W, OFF['ga'], 16)], 0, 8, 128)]
    if kind == 'gwt':
        return [([(W, OFF['mi'], 8)], 0, 8, 128)]
    if kind == 'wa2':
        cols = []
        for hh in range(4):
            cols += [('w_a2', 64 * hh, 64), ('w_a2', 64 * hh, 64)]
        return [(cols, 0, 1, 16)]
    raise KeyError(kind)


def part_ncols(part):
    return sum(c[2] for c in part[0])


JOB_ORDER = ([('retv', 0), ('glav', 0)] + [('ret', h) for h in range(4)] + [('gwt', 0)] + [('ml', h) for h in range(4)] + [('gaw', 0), ('wa2', 0)]
             + [('gla', h) for h in range(4)] + [('mergeb', f) for f in range(24)] + [('out', f) for f in range(8)]
             + [('ffi', j) for j in range(22)] + [('ffo', i) for i in range(16)])
JOB_OFF = {}
_o = 0
for _k in JOB_ORDER:
    _t = sum(pt[2] * part_ncols(pt) for pt in job_parts(*_k))
    JOB_OFF[_k] = (_o, _t)
    _o += _t
WTOT = _o


def pack_weights(srcs):
    wp = np.zeros((DEPTH, 128, WTOT), np.float32)
    for key in JOB_ORDER:
        off, _ = JOB_OFF[key]
        for part in job_parts(*key):
            cols, r0, nk, npart = part
            ncols = part_ncols(part)
            blk = np.concatenate([srcs[sn][:, r0:r0 + nk * npart, c0:c0 + nc_] for (sn, c0, nc_) in cols], axis=2)
            blk = blk.reshape(DEPTH, nk, npart, ncols).transpose(0, 2, 1, 3)
            wp[:, 0:npart, off:off + nk * ncols] = blk.reshape(DEPTH, npart, nk * ncols)
            off += nk * ncols
    return wp


class Rg:
    __slots__ = ("w", "r", "excl")

    def __init__(self, excl=False):
        self.w = None
        self.r = {}
        self.excl = excl


class Sched:
    def __init__(self, nc):
        self.nc = nc
        self.eng = {'pe': nc.tensor, 'act': nc.scalar, 'dve': nc.vector, 'pool': nc.gpsimd, 'sp': nc.sync}
        self.ops = []
        self.cnt = {}
        self.known = {e: {} for e in self.eng}
        self.clock = {}
        self.waited = set()
        self.slots = {'sp': ['dsp%d' % i for i in range(8)], 'pool': ['dpl%d' % i for i in range(6)]}
        self.rr = {'sp': 0, 'pool': 0}
        self.trail = False
        self.mix = False
        self.dummy_fn = None
        self.dummy = nc.alloc_sbuf_tensor("sched_dummy", [1, 16], F32)

    def _deps(self, reads, writes):
        deps = []
        for r in reads:
            if r.w is not None:
                deps.append((r.w[0], r.w[1], True))
            if r.excl:
                for e, i in r.r.items():
                    deps.append((e, i, False))
        for w in writes:
            if w.w is not None:
                deps.append((w.w[0], w.w[1], False))
            for e, i in w.r.items():
                deps.append((e, i, False))
        return deps

    def _resolve(self, E, issuer, deps):
        kn = self.known[issuer]
        need = {}
        for (Fe, i, raw) in deps:
            if Fe == E and (E == 'pe' or not raw):
                continue
            if kn.get(Fe, -1) >= i:
                continue
            if need.get(Fe, -1) < i:
                need[Fe] = i
        waits = []
        for Fe, i in need.items():
            if kn.get(Fe, -1) >= i:
                continue
            waits.append((Fe, i))
            self.waited.add((Fe, i))
            for G, j in self.clock[(Fe, i)].items():
                if kn.get(G, -1) < j:
                    kn[G] = j
            if kn.get(Fe, -1) < i:
                kn[Fe] = i
        return waits

    def _mark(self, key, reads, writes):
        for r in reads:
            if r.r.get(key[0], -1) < key[1]:
                r.r[key[0]] = key[1]
        for w in writes:
            w.w = key
            w.r = {}

    def op(self, E, fn, reads=(), writes=()):
        n = self.cnt.get(E, 0)
        self.cnt[E] = n + 1
        lhs = reads[0].w[0] if (len(reads) > 0 and reads[0].w is not None) else None
        waits = self._resolve(E, E, self._deps(reads, writes))
        waits.sort(key=lambda w: 1 if w[0] == lhs else 0)
        self.clock[(E, n)] = dict(self.known[E])
        self.ops.append(('c', E, n, fn, waits, self.mix))
        self._mark((E, n), reads, writes)

    def dma(self, issuer, out, in_, reads=(), writes=(), **kw):
        sl = self.slots[issuer]
        Dq = sl[self.rr[issuer] % len(sl)]
        self.rr[issuer] += 1
        n = self.cnt.get(Dq, 0)
        self.cnt[Dq] = n + 1
        deps = self._deps(reads, writes)
        if n > 0:
            deps.append((Dq, n - 1, True))
        waits = self._resolve(Dq, issuer, deps)
        self.clock[(Dq, n)] = dict(self.known[issuer])
        self.ops.append(('d', issuer, Dq, n, out, in_, kw, waits))
        self._mark((Dq, n), reads, writes)

    def emit(self):
        nc = self.nc
        val = {}
        c = {}
        for o in self.ops:
            if o[0] == 'c':
                E, n = o[1], o[2]
                if (E, n) in self.waited:
                    c[E] = c.get(E, 0) + 1
                val[(E, n)] = c.get(E, 0)
        names = list(self.eng) + [s for v in self.slots.values() for s in v]
        sems = {}
        for nm in names:
            sems[nm] = nc.semaphore("sem_" + nm).__enter__()

        def v(Fe, i):
            if Fe in self.eng:
                return val[(Fe, i)]
            return 16 * (i + 1)

        for o in self.ops:
            if o[0] == 'c':
                _, E, n, fn, waits, mixf = o
                h = self.eng[E]
                if E == 'pe' and mixf and self.dummy_fn is not None and any(Fe in ('dve', 'act') for (Fe, i) in waits):
                    for _ in range(NDUMMY):
                        self.dummy_fn()
                for (Fe, i) in waits:
                    h.wait_ge(sems[Fe], v(Fe, i))
                ins = fn()
                if (E, n) in self.waited:
                    if E == 'dve' and self.trail:
                        ins = nc.vector.memset(self.dummy[0:1, 0:1], 0.0)
                    elif E == 'act' and self.trail:
                        ins = nc.scalar.copy(self.dummy[0:1, 2:3], self.dummy[0:1, 1:2])
                    ins.then_inc(sems[E], 1)
            else:
                _, issuer, Dq, n, out, in_, kw, waits = o
                h = self.eng[issuer]
                for (Fe, i) in waits:
                    h.wait_ge(sems[Fe], v(Fe, i))
                h.dma_start(out=out, in_=in_, **kw).then_inc(sems[Dq], 16)
        for Dq in self.slots['sp'] + self.slots['pool']:
            if self.cnt.get(Dq, 0) > 0:
                nc.sync.wait_ge(sems[Dq], 16 * self.cnt[Dq])
        self.max_sem = dict(c)


class Pool:
    def __init__(self, nc, name, n, shape, dt):
        self.bufs = [nc.alloc_sbuf_tensor("%s%d" % (name, i), shape, dt) for i in range(n)]
        self.rg = [Rg() for _ in range(n)]
        self.i = 0

    def get(self):
        k = self.i % len(self.bufs)
        self.i += 1
        return self.bufs[k], self.rg[k]


def make_groups():
    def fr(seq, i, pre=None, post=None):
        chunks = []
        import os
        CH = int(os.environ.get('MK_CH', '128'))
        for j in range(512 // CH):
            chunks.append(dict(off=j * CH, c=CH, stream='main', pos0=NMETA + i * 512 + j * CH, li=LENS.index(CH),
                               pre=None, post=None))
        chunks[0]['pre'] = pre
        chunks[-1]['post'] = post
        return dict(kind='fr', n=512, seq=seq, tok0=i * 512, chunks=chunks,
                    segs=[dict(off=0, n=512, stream='main')])
    sp = dict(kind='sp', n=80, chunks=[
        dict(off=0, c=16, stream='main', pos0=0, li=0, pre=('zero',), post=('save', 'scr', 1)),
        dict(off=16, c=64, stream='smp', pos0=NMETA + SEQ, li=1, pre=('load', 'in', 0), post=('save', 'out', 2))],
        segs=[dict(off=0, n=16, stream='main'), dict(off=16, n=64, stream='smp')])
    ST = ('save', 'scr', 0)
    LD = ('load', 'scr', 0)
    groups = [
        [sp, fr(0, 0, None, ST)],
        [fr(0, 1, LD, None), fr(0, 2, None, ST)],
        [fr(0, 3, LD, ('save', 'out', 0)), fr(1, 0, ('load', 'scr', 1), ST)],
        [fr(1, 1, LD, None), fr(1, 2, None, ST)],
        [fr(1, 3, LD, ('save', 'out', 1))],
    ]
    for g in groups:
        col = 0
        gch = 0
        for T in g:
            T['col'] = col
            col += T['n']
            for ch in T['chunks']:
                ch['g'] = gch
                gch += 1
    return groups


def build_program(ngroups=5, stages=('ret', 'ml', 'gla', 'merge', 'ffn'), depth=DEPTH):
    nc = bass.Bass("TRN2", target_bir_lowering=False)
    S = Sched(nc)

    def din(name, shape):
        return nc.dram_tensor(name, list(shape), F32, kind="ExternalInput").ap()

    def dout(name, shape):
        return nc.dram_tensor(name, list(shape), F32, kind="ExternalOutput").ap()

    def dscr(name, shape):
        return nc.dram_tensor(name, list(shape), F32, kind="Internal").ap()

    xp = din("xp", [2, SEQ, D])
    xs = din("xs", [DSEQ, D])
    meta = din("meta", [NMETA, D])
    KSH = dict(ret=[64, 4, 128], gla=[64, 4, 128], ml=[128, 4, 129], m=[4, 1], conv=[128, 4, 3])
    st_in = {k: din("sti_" + k, [1, 2] + s) for k, s in KSH.items()}
    st_out = {k: dout("sto_" + k, [3, 2] + s) for k, s in KSH.items()}
    st_scr = {k: dscr("sts_" + k, [2, 2] + s) for k, s in KSH.items()}
    stt_ = dict(scr=st_scr, out=st_out)
    stt_['in'] = st_in
    scr_rg = {}

    def drg(kind, idx, l, k, h):
        key = (kind, idx, l, k, h)
        if key not in scr_rg:
            scr_rg[key] = Rg()
        return scr_rg[key]

    wpack_d = din("wpack", [DEPTH, 128, WTOT])
    w_in = w_rs = w_mq = w_mk = w_mv = w_a2 = w_out = w_fi = w_fo = None
    w_br = [None, None, None]
    cst_d = din("cst", [128, C_END])
    vec_d = din("vecs", [128, DEPTH, V_END])
    cs_d = din("cs", [64, 2, NPOS])
    yp = dout("yp", [2, SEQ, D])
    ys = dout("ys", [DSEQ, D])

    sb = nc.alloc_sbuf_tensor
    NT = 1024
    DBG = 'dbg' in stages
    dbg_d = dout("dbg", [128, 4096]) if DBG else None
    dbg_state = {'col': 0, 'names': []}

    def dump(name, src, np_, ncol, R):
        if not DBG or dbg_state['col'] + ncol > 4096:
            return
        t, tr_ = tmpf.get()
        cp(t[0:np_, 0:ncol], src, R, [tr_])
        c0 = dbg_state['col']
        S.dma('sp', dbg_d[0:np_, c0:c0 + ncol], t[0:np_, 0:ncol], [tr_], ())
        dbg_state['names'].append((name, np_, c0, ncol))
        dbg_state['col'] += ncol
    build_program.dbg_state = dbg_state
    xT = sb("xT", [128, KC, NT], F32)
    hT = sb("hT", [128, KC, NT], BF16)
    ob = sb("ob", [128, 12, NT], BF16)
    mixb = sb("mixb", [128, KC, NT], BF16)
    cst = sb("cst_sb", [128, C_END], F32)
    vec = sb("vec_sb", [128, DEPTH, V_END], F32)
    nvec = sb("nvec", [128, DEPTH, V_END], F32)
    identb = sb("identb", [128, 128], BF16)
    onesb = sb("onesb", [128, 128], BF16)
    R_cst, R_vec, R_nvec, R_idb, R_oneb = Rg(), Rg(), Rg(), Rg(), Rg()
    Rx = [[Rg() for _ in range(3)] for _ in range(KC)]
    Rh = [[Rg() for _ in range(3)] for _ in range(KC)]
    Ro = [[Rg() for _ in range(3)] for _ in range(12)]
    Rm = [[Rg() for _ in range(3)] for _ in range(KC)]

    STB = {}
    STR = {}
    for s in ('main', 'smp'):
        STB[s] = dict(ret=sb("st_ret_" + s, [128, 4, 128], F32), gla=sb("st_gla_" + s, [128, 4, 128], F32),
                      ml=sb("st_ml_" + s, [128, 4, 144], F32), m=sb("st_m_" + s, [4, 16], F32),
                      conv=sb("st_conv_" + s, [128, 4, 4], F32))
        STR[s] = {k: [Rg() for _ in range(4)] for k in ('ret', 'gla', 'ml', 'conv')}
        STR[s]['m'] = [Rg()]
    s16 = Pool(nc, "s16", 3, [128, 160], BF16)

    WSL = 4608
    warena = sb("warena", [128, 3 * WSL], BF16)
    wlive = []
    DENSE_SLOTS = [(i * WSL, (i + 1) * WSL) for i in range(3)]
    A_SLOTS = [(0, 4096), (4096, 8192)]
    B_SLOTS = [(8192, 8192 + 2432), (8192 + 2432, 8192 + 4864)]
    dctr = [0, 0]
    gaw = sb("gaw", [128, KC, 16], BF16)
    wa2 = sb("wa2", [16, 512], BF16)
    gwt = sb("gwt", [128, KC, 8], BF16)
    R_gaw, R_wa2, R_gwt = Rg(), Rg(), Rg()
    gaT = sb("gaT", [16, NT], BF16)
    R_gaT = [Rg() for _ in range(3)]
    tmpf = Pool(nc, "tmpf", 7, [128, 528], F32)
    gatep = Pool(nc, "gatep", 2, [128, 512], F32)
    tmpb = Pool(nc, "tmpb", 8, [128, 512], BF16)
    mxfp = Pool(nc, "mxfp", 2, [128, 608], F32)
    ctp = Pool(nc, "ctp", 2, [128, 512], F32)
    cfp = Pool(nc, "cfp", 8, [128, 144], F32)
    cbp = Pool(nc, "cbp", 10, [128, 160], BF16)
    vaugp = Pool(nc, "vaugp", 3, [128, 160], BF16)
    csp = Pool(nc, "csp", 12, [128, 16], F32)
    scr8k = sb("scr8k", [128, 2048], F32)
    R_s0, R_s1 = Rg(), Rg()
    rows = scr8k[0:4, :].rearrange("p (k t) -> p k t", t=512)
    colsb = sb("colsb", [128, 8, 16], F32)
    R_cols = [Rg() for _ in range(8)]
    scb = sb("scb", [128, 2, 4, 8], F32)
    R_scb = [[Rg() for _ in range(4)] for _ in range(2)]
    sct = sb("sct", [4, 16], F32)
    R_sct = Rg()
    XINS = [(scr8k[:, 0:1024], R_s0), (scr8k[:, 1024:2048], R_s1)]
    xctr = [0]
    tab = scr8k[0:64, 1024:2048].rearrange("p (k t) -> p k t", t=512)
    tab2 = scr8k[:, 1024:1536]
    R_tab = R_s1
    NPF = 6
    psf = [nc.psum_tensor("psf%d" % i, [128, 512], F32).__enter__() for i in range(NPF)]
    psb = [nc.psum_tensor("psb%d" % i, [128, 1024], BF16).__enter__() for i in range(2)]
    psfr = [Rg(True) for _ in range(NPF)]
    psbr = [Rg(True) for _ in range(2)]
    pctr = [0, 0]

    def PF():
        k = pctr[0] % NPF
        pctr[0] += 1
        return psf[k], psfr[k]

    def PB():
        k = pctr[1] % 2
        pctr[1] += 1
        return psb[k], psbr[k]

    def mm(out, lhsT, rhs, start, stop, R, W):
        S.op('pe', lambda: nc.tensor.matmul(out, lhsT, rhs, start=start, stop=stop), R, W)

    def tr(out, in_, ident, R, W):
        S.op('pe', lambda: nc.tensor.transpose(out, in_, ident), R, W)

    def act(out, in_, func, R, W, bias=None, scale=None, accum=None):
        kw = {}
        if bias is not None:
            kw['bias'] = bias
        if scale is not None:
            kw['scale'] = scale
        if accum is not None:
            kw['accum_out'] = accum
        S.op('act', lambda: nc.scalar.activation(out=out, in_=in_, func=func, **kw), R, W)

    def tt(out, a, b, op, R, W, eng='dve'):
        h = S.eng[eng]
        S.op(eng, lambda: h.tensor_tensor(out, a, b, op), R, W)

    def ts(out, a, s1, s2, op0, op1, R, W, eng='dve'):
        h = S.eng[eng]
        if op1 is None:
            S.op(eng, lambda: h.tensor_scalar(out, a, s1, None, op0), R, W)
        else:
            S.op(eng, lambda: h.tensor_scalar(out, a, s1, s2, op0, op1), R, W)

    def stt(out, in0, scalar, in1, op0, op1, R, W, eng='dve'):
        h = S.eng[eng]
        S.op(eng, lambda: h.scalar_tensor_tensor(out, in0, scalar, in1, op0, op1), R, W)

    def amul(out, in_, sc, R, W):
        S.op('act', lambda: nc.scalar.mul(out, in_, sc), R, W)

    def cp(out, in_, R, W, eng='dve'):
        if eng == 'act':
            S.op('act', lambda: nc.scalar.copy(out, in_), R, W)
        else:
            h = S.eng[eng]
            S.op(eng, lambda: h.tensor_copy(out, in_), R, W)

    def mset(ap, v, W, eng='dve'):
        h = S.eng[eng]
        S.op(eng, lambda: h.memset(ap, v), (), W)

    def scan(out, d0, d1, init, op0, op1, R, W):
        S.op('dve', lambda: nc.vector.tensor_tensor_scan(out, d0, d1, init, op0, op1), R, W)

    def recip(out, in_, R, W):
        S.op('dve', lambda: nc.vector.reciprocal(out, in_), R, W)

    S.dma('sp', cst[:], cst_d, (), [R_cst])
    S.dma('sp', vec[:], vec_d, (), [R_vec])
    S.dma('pool', identb[:], cst_d[:, C_ID:C_ID + 128], (), [R_idb])
    S.dma('pool', onesb[:], cst_d[:, C_ONE:C_ONE + 128], (), [R_oneb])
    ts(nvec[:], vec[:], -1.0, None, ALU.mult, None, [R_vec], [R_nvec])
    epsb = sb("epsb", [128, 16], F32)
    R_eps = Rg()
    mset(epsb[:], EPS, [R_eps])
    foldb = sb("foldb", [128, 64], BF16)
    selb = sb("selb", [128, 64], BF16)
    R_fold, R_sel = Rg(), Rg()
    S.dma('pool', foldb[0:64, :], cst_d[0:64, C_ID:C_ID + 64], (), [R_fold])
    S.dma('pool', foldb[64:128, :], cst_d[64:128, C_ID + 64:C_ID + 128], [R_fold], [R_fold])
    S.dma('pool', selb[:], cst_d[:, C_ID + 64:C_ID + 128], (), [R_sel])
    identf = cst[:, C_ID:C_ID + 128]
    maskT = cst[:, C_MASK:C_MASK + 128]
    for k in range(3):
        mset(vaugp.bufs[k][:, 128:129], 1.0, [vaugp.rg[k]])

    def wjobp(kind, idx, l, slot=None):
        off, tot = JOB_OFF[(kind, idx)]
        if slot is None:
            if tot <= WSL // 2:
                q = dctr[1] % 6
                dctr[1] += 1
                slot = (q * (WSL // 2), (q + 1) * (WSL // 2))
            else:
                slot = DENSE_SLOTS[dctr[0] % 3]
                dctr[0] += 1
        s0, s1 = slot
        assert tot <= s1 - s0
        old_rgs = [rg_ for (a, b, rg_) in wlive if a < s0 + tot and b > s0]
        wlive[:] = [(a, b, rg_) for (a, b, rg_) in wlive if not (a < s0 + tot and b > s0)]
        rg = Rg()
        wlive.append((s0, s0 + tot, rg))
        S.dma('pool', warena[:, s0:s0 + tot], wpack_d[l][:, off:off + tot], (), [rg] + old_rgs)
        res = []
        o = s0
        for part in job_parts(kind, idx):
            nk, ncols = part[2], part_ncols(part)
            res.append((warena[:, o:o + nk * ncols].rearrange("p (k n) -> p k n", n=ncols), rg))
            o += nk * ncols
        return res

    def wsmall(dst_ap, kind, l, rg, npart=128):
        off, tot = JOB_OFF[(kind, 0)]
        S.dma('pool', dst_ap, wpack_d[l][0:npart, off:off + tot], (), [rg])

    def wjob(parts):
        raise RuntimeError('unused')

    def st_action(a, l, stream, kind, h):
        if a is None:
            return
        buf = STB[stream][kind]
        rg = STR[stream][kind][h if kind != 'm' else 0]
        np_ = 64 if kind in ('ret', 'gla') else (4 if kind == 'm' else 128)
        wd = dict(ret=128, gla=128, ml=129, conv=3)
        sv = buf[0:np_, h, 0:wd[kind]] if kind != 'm' else buf[0:4, 0:1]
        if a[0] == 'zero':
            mset(sv, 0.0, [rg])
            return
        which, idx = a[1], a[2]
        dt_ = stt_[which][kind]
        dv = dt_[idx, l][:, h, :] if kind != 'm' else dt_[idx, l]
        drg_ = drg(which, idx, l, kind, h)
        if a[0] == 'load':
            S.dma('sp', sv, dv, [drg_] if which == 'scr' else (), [rg])
        else:
            S.dma('sp', dv, sv, [rg], [drg_])

    def refresh16(stream, kind, h):
        buf = STB[stream][kind]
        np_ = 64 if kind in ('ret', 'gla') else 128
        w = 128 if kind in ('ret', 'gla') else 129
        t16, r16 = s16.get()
        cp(t16[0:np_, 0:w], buf[0:np_, h, 0:w], [STR[stream][kind][h]], [r16], eng='act')
        return t16, r16

    def rmsnorm(tiles, l, vcol, dst_bf=True):
        for ti, T in enumerate(tiles):
            n = T['n']
            cs_ = slice(T['col'], T['col'] + n)
            ps, pr = PF()
            for kc in range(KC):
                sq, sr = tmpb.get()
                act(sq[:, :n], xT[:, kc, cs_], AF.Square, [Rx[kc][ti]], [sr])
                mm(ps[:, :n], onesb[:], sq[:, :n], kc == 0, kc == KC - 1, [R_oneb, sr], [pr])
            lnv, lr = tmpf.get()
            act(lnv[:, :n], ps[:, :n], AF.Ln, [pr, R_eps], [lr], bias=epsb[:, 0:1], scale=1.0 / D)
            rs, rr = gatep.get()
            act(rs[:, :n], lnv[:, :n], AF.Exp, [lr], [rr], scale=-0.5)
            if dst_bf:
                for kc in range(KC):
                    stt(hT[:, kc, cs_], xT[:, kc, cs_], vec[:, l, vcol + kc:vcol + kc + 1], rs[:, :n],
                        ALU.mult, ALU.mult, [Rx[kc][ti], R_vec, rr], [Rh[kc][ti]])
            else:
                yield_final(T, ti, rs, rr, n, cs_, l, vcol)

    def yield_final(T, ti, rs, rr, n, cs_, l, vcol):
        nb = (n + 127) // 128
        for b in range(nb):
            c0 = b * 128
            cn = min(128, n - c0)
            pa, par = PF()
            pb, pbr = PF()
            for kc in range(KC):
                yt, yr = tmpf.get()
                stt(yt[:, :cn], xT[:, kc, T['col'] + c0:T['col'] + c0 + cn], vec[:, l, vcol + kc:vcol + kc + 1],
                    rs[:, c0:c0 + cn], ALU.mult, ALU.mult, [Rx[kc][ti], R_vec, rr], [yr])
                pp, ppr = (pa, par) if kc < 4 else (pb, pbr)
                tr(pp[0:cn, (kc % 4) * 128:(kc % 4) * 128 + 128], yt[:, :cn], identf, [yr, R_cst], [ppr])
            xin, R_xin = XINS[xctr[0] % 2]
            xctr[0] += 1
            cp(xin[0:cn, 0:512], pa[0:cn, :], [par], [R_xin])
            cp(xin[0:cn, 512:1024], pb[0:cn, :], [pbr], [R_xin], eng='act')
            if T['kind'] == 'sp':
                S.dma('sp', ys[:, :], xin[16:80, :], [R_xin], ())
            else:
                r0 = T['tok0'] + c0
                S.dma('sp', yp[T['seq'], r0:r0 + cn, :], xin[0:cn, :], [R_xin], ())

    def load_x(tiles):
        for ti, T in enumerate(tiles):
            n = T['n']
            nb = (n + 127) // 128
            for b in range(nb):
                c0 = b * 128
                cn = min(128, n - c0)
                xin, R_xin = XINS[xctr[0] % 2]
                xctr[0] += 1
                if T['kind'] == 'sp':
                    S.dma('sp', xin[0:16, :], meta, (), [R_xin])
                    S.dma('sp', xin[16:80, :], xs, [R_xin], [R_xin])
                    xin_r = [R_xin]
                else:
                    r0 = T['tok0'] + c0
                    S.dma('sp', xin[0:cn, :], xp[T['seq'], r0:r0 + cn, :], (), [R_xin])
                    xin_r = [R_xin]
                for half in range(2):
                    pp, ppr = PF()
                    for q in range(4):
                        kc = half * 4 + q
                        tr(pp[:, q * 128:q * 128 + cn], xin[0:cn, kc * 128:(kc + 1) * 128], identf[0:cn, 0:cn],
                           xin_r + [R_cst], [ppr])
                    dst = xT[:, half * 4:half * 4 + 4, T['col'] + c0:T['col'] + c0 + cn]
                    src = pp[:, :].rearrange("p (q t) -> p q t", t=128)[:, :, 0:cn]
                    cp(dst, src, [ppr], [Rx[half * 4 + q][ti] for q in range(4)], eng='dve' if half == 0 else 'act')

    def headnorm_gate(src, srcR, c, gate_ap, gateR, oi, ti, colsl, skip=None):
        junk, jr = cfp.get()
        ssb, ssr = csp.get()
        act(junk[0:c, 0:128], src, AF.Square, srcR, [jr, ssr], accum=ssb[0:c, 0:1])
        act(ssb[0:c, 1:2], ssb[0:c, 0:1], AF.Ln, [ssr, R_eps], [ssr], bias=epsb[0:c, 0:1], scale=1.0 / 128)
        act(ssb[0:c, 2:3], ssb[0:c, 1:2], AF.Exp, [ssr], [ssr], scale=-0.5)
        on, onr = cbp.get()
        ts(on[0:c, 0:128], src, ssb[0:c, 2:3], None, ALU.mult, None, srcR + [ssr], [onr])
        pt, ptr = PB()
        tr(pt[:, 0:c], on[0:c, 0:128], identb[0:c, 0:c], [onr, R_idb], [ptr])
        if skip is None:
            tt(ob[:, oi, colsl], pt[:, 0:c], gate_ap, ALU.mult, [ptr] + gateR, [Ro[oi][ti]])
        else:
            cT_ap, cR, sk_ap = skip
            t2, t2r = cfp.get()
            stt(t2[:, 0:c], cT_ap, sk_ap, pt[:, 0:c], ALU.mult, ALU.add, [ptr, R_vec] + cR, [t2r])
            tt(ob[:, oi, colsl], t2[:, 0:c], gate_ap, ALU.mult, [t2r] + gateR, [Ro[oi][ti]])

    def proj_fm(ps, pr, m0, m1, wview, wr, ti, T):
        n = T['n']
        cs_ = slice(T['col'], T['col'] + n)
        for kc in range(KC):
            mm(ps[m0:m1, :n], wview[:, kc, :], hT[:, kc, cs_], kc == 0, kc == KC - 1, [wr, Rh[kc][ti]], [pr])

    def vtok(ch, T, ti, wview, wr, dst, dstr, ncol=128):
        c = ch['c']
        c0 = T['col'] + ch['off']
        ps, pr = PF()
        for kc in range(KC):
            mm(ps[0:c, 0:ncol], hT[:, kc, c0:c0 + c], wview[:, kc, :], kc == 0, kc == KC - 1, [Rh[kc][ti], wr], [pr])
        cp(dst[0:c, 0:ncol], ps[0:c, 0:ncol], [pr], [dstr], eng='act')

    def load_tab2(T):
        n = T['n']
        segs = [(0, 16, 0), (16, 64, NMETA + SEQ)] if T['kind'] == 'sp' else [(0, n, NMETA + T['tok0'])]
        first = True
        for (c0, ln, p0) in segs:
            for half in range(2):
                S.dma('sp', tab2[64 * half:64 * half + 64, c0:c0 + ln], cs_d[:, half, p0:p0 + ln],
                      () if first else [R_tab], [R_tab])
                first = False

    def load_tab(T):
        n = T['n']
        if T['kind'] == 'sp':
            S.dma('sp', tab[:, :, 0:16], cs_d[:, :, 0:16], (), [R_tab])
            S.dma('sp', tab[:, :, 16:80], cs_d[:, :, NMETA + SEQ:NMETA + SEQ + 64], [R_tab], [R_tab])
        else:
            p0 = NMETA + T['tok0']
            S.dma('sp', tab[:, :, 0:n], cs_d[:, :, p0:p0 + n], (), [R_tab])

    def branch_ret(l, tiles):
        for h in range(4):
            Wl = None
            parts = wjob([(Wl[:, OFF['rq'] + 64 * h:OFF['rq'] + 64 * h + 64], KC, 64),
                          (w_rs[l][:, 64 * h:64 * h + 64], KC, 64),
                          (Wl[:, OFF['rk'] + 64 * h:OFF['rk'] + 64 * h + 64], KC, 64),
                          (w_rs[l][:, 256 + 64 * h:256 + 64 * h + 64], KC, 64),
                          (Wl[:, OFF['rv'] + 128 * h:OFF['rv'] + 128 * h + 128], KC, 128),
                          (Wl[:, OFF['rg'] + 128 * h:OFF['rg'] + 128 * h + 128], KC, 128)])
            (wq, rq), (wqs, rqs), (wk, rk), (wks, rks), (wv, rv), (wg, rgt) = parts
            for ti, T in enumerate(tiles):
                n = T['n']
                load_tab(T)
                pg, pgr = PF()
                proj_fm(pg, pgr, 0, 128, wg, rgt, ti, T)
                gate, gr_ = gatep.get()
                act(gate[:, :n], pg[:, :n], AF.Silu, [pgr], [gr_])
                outs = []
                for (wa, ra, wb, rb, isq) in ((wq, rq, wqs, rqs, True), (wk, rk, wks, rks, False)):
                    pa, par = PF()
                    proj_fm(pa, par, 0, 64, wa, ra, ti, T)
                    pb_, pbr_ = PF()
                    proj_fm(pb_, pbr_, 0, 64, wb, rb, ti, T)
                    t1, t1r = tmpf.get()
                    tt(t1[0:64, :n], pa[0:64, :n], tab[:, 0, 0:n], ALU.mult, [par, R_tab], [t1r])
                    t2, t2r = tmpf.get()
                    tt(t2[0:64, :n], pb_[0:64, :n], tab[:, 1, 0:n], ALU.mult, [pbr_, R_tab], [t2r])
                    ob_, obr_ = tmpb.get()
                    if isq:
                        tt(t1[0:64, :n], t1[0:64, :n], t2[0:64, :n], ALU.add, [t1r, t2r], [t1r])
                        for sg in T['segs']:
                            o0, sn = sg['off'], sg['n']
                            if sn >= 128:
                                nch = sn // 128
                                gq = cst[0:64, C_GQ + 128 * h:C_GQ + 128 * h + 128].unsqueeze(1).to_broadcast([64, nch, 128])
                                tt(ob_[0:64, o0:o0 + sn].rearrange("p (c t) -> p c t", t=128),
                                   t1[0:64, o0:o0 + sn].rearrange("p (c t) -> p c t", t=128), gq, ALU.mult,
                                   [t1r, R_cst], [obr_])
                            else:
                                tt(ob_[0:64, o0:o0 + sn], t1[0:64, o0:o0 + sn],
                                   cst[0:64, C_GQ + 128 * h:C_GQ + 128 * h + sn], ALU.mult, [t1r, R_cst], [obr_])
                    else:
                        tt(ob_[0:64, :n], t1[0:64, :n], t2[0:64, :n], ALU.add, [t1r, t2r], [obr_])
                    outs.append((ob_, obr_))
                (qin, qr), (kr, krr) = outs
                if h == 0 and l == 0 and ti == 1:
                    dump('qin', qin[0:64, 0:512], 64, 512, [qr])
                    dump('kr', kr[0:64, 0:512], 64, 512, [krr])
                    dump('gate', gate[:, 0:512], 128, 512, [gr_])
                for ch in T['chunks']:
                    c, o0, st = ch['c'], ch['off'], ch['stream']
                    st_action(ch['pre'], l, st, 'ret', h)
                    Sb = STB[st]['ret']
                    Sr = STR[st]['ret'][h]
                    t16, r16 = refresh16(st, 'ret', h)
                    psc, pscr = PF()
                    mm(psc[0:c, 0:c], kr[0:64, o0:o0 + c], qin[0:64, o0:o0 + c], True, True, [krr, qr], [pscr])
                    P, Pr = cbp.get()
                    tt(P[0:c, 0:c], psc[0:c, 0:c], cst[0:c, C_DR + 128 * h:C_DR + 128 * h + c], ALU.mult,
                       [pscr, R_cst], [Pr])
                    vt, vr = cbp.get()
                    vtok(ch, T, ti, wv, rv, vt, vr)
                    pk, pkr = PB()
                    tr(pk[0:c, 0:64], kr[0:64, o0:o0 + c], identb[0:64, 0:64], [krr, R_idb], [pkr])
                    kst, kstr = cbp.get()
                    ts(kst[0:c, 0:64], pk[0:c, 0:64], cst[0:c, C_KS + 3 * h + ch['li']:C_KS + 3 * h + ch['li'] + 1], None,
                       ALU.mult, None, [pkr, R_cst], [kstr])
                    po, por = PF()
                    if c <= 64:
                        mm(po[0:c, 0:128], P[0:c, 0:c], vt[0:c, 0:128], True, False, [Pr, vr], [por])
                        mm(po[0:c, 0:128], qin[0:64, o0:o0 + c], t16[0:64, 0:128], False, True, [qr, r16], [por])
                    else:
                        mm(po[0:c, 0:128], qin[0:64, o0:o0 + c], t16[0:64, 0:128], True, False, [qr, r16], [por])
                        mm(po[0:c, 0:128], P[0:c, 0:c], vt[0:c, 0:128], False, True, [Pr, vr], [por])
                    pd, pdr = PF()
                    mm(pd[0:64, 0:128], kst[0:c, 0:64], vt[0:c, 0:128], True, True, [kstr, vr], [pdr])
                    if h == DBG_H and l == 0 and ti == 1 and ch['off'] == DBG_OFF:
                        dump('vt', vt[0:128, 0:128], 128, 128, [vr])
                        dump('kst', kst[0:128, 0:64], 128, 64, [kstr])
                        dump('P', P[0:128, 0:128], 128, 128, [Pr])
                        dump('S16', t16[0:64, 0:128], 64, 128, [r16])
                        dump('po', po[0:128, 0:128], 128, 128, [por])
                        dump('pd', pd[0:64, 0:128], 64, 128, [pdr])
                    stt(Sb[0:64, h, :], Sb[0:64, h, :], float(GAM[h] ** c), pd[0:64, 0:128], ALU.mult, ALU.add,
                        [Sr, pdr], [Sr])
                    headnorm_gate(po[0:c, 0:128], [por], c, gate[:, o0:o0 + c], [gr_], h, ti,
                                  slice(T['col'] + o0, T['col'] + o0 + c))
                    st_action(ch['post'], l, st, 'ret', h)
                if h == DBG_H and l == 0 and ti == 1:
                    dump('ob', ob[:, h, T['col']:T['col'] + 512], 128, 512, [Ro[h][ti]])

    def branch_gla(l, tiles):
        import os
        CUT = int(os.environ.get('MK_GLA_CUT', '9'))
        Wl = None
        S.dma('pool', gaw[:], Wl[:, OFF['ga']:OFF['ga'] + 16].rearrange("(k p) n -> p k n", p=128), (), [R_gaw])
        S.dma('pool', wa2[:], w_a2[l], (), [R_wa2])
        for ti, T in enumerate(tiles):
            n = T['n']
            pg, pgr = PF()
            proj_fm(pg, pgr, 0, 16, gaw, R_gaw, ti, T)
            cp(gaT[0:16, T['col']:T['col'] + n], pg[0:16, :n], [pgr], [R_gaT[ti]])
        for h in range(4):
            parts = wjob([(Wl[:, OFF['gq'] + 64 * h:OFF['gq'] + 64 * h + 64], KC, 64),
                          (Wl[:, OFF['gk'] + 64 * h:OFF['gk'] + 64 * h + 64], KC, 64),
                          (Wl[:, OFF['gv'] + 128 * h:OFF['gv'] + 128 * h + 128], KC, 128),
                          (Wl[:, OFF['gr'] + 128 * h:OFF['gr'] + 128 * h + 128], KC, 128)])
            (wq, rq), (wk, rk), (wv, rv), (wg, rgt) = parts
            if CUT < 1:
                continue
            for ti, T in enumerate(tiles):
                n = T['n']
                cs_ = slice(T['col'], T['col'] + n)
                pg, pgr = PF()
                proj_fm(pg, pgr, 0, 128, wg, rgt, ti, T)
                gate, gr_ = gatep.get()
                act(gate[:, :n], pg[:, :n], AF.Silu, [pgr], [gr_])
                if CUT < 2:
                    continue
                pz, pzr = PF()
                mm(pz[0:64, :n], wa2[0:16, 64 * h:64 * h + 64], gaT[0:16, cs_], True, True, [R_wa2, R_gaT[ti]], [pzr])
                e, er = tmpf.get()
                act(e[0:64, :n], pz[0:64, :n], AF.Exp, [pzr, R_nvec], [er], bias=nvec[0:64, l, V_BA + h:V_BA + h + 1],
                    scale=-1.0)
                sp_, spr = tmpf.get()
                act(sp_[0:64, :n], e[0:64, :n], AF.Ln, [er], [spr], bias=1.0)
                if CUT < 3:
                    continue
                bs, bsr = tmpf.get()
                rm = cst[0:64, C_RMS:C_RMS + 80] if T['kind'] == 'sp' else cst[0:64, C_RM:C_RM + 512]
                scan(bs[0:64, :n], rm, sp_[0:64, :n], 0.0, ALU.mult, ALU.add, [spr, R_cst], [bsr])
                if CUT < 4:
                    continue
                eb, ebr = tmpf.get()
                act(eb[0:64, :n], bs[0:64, :n], AF.Exp, [bsr], [ebr], scale=-1.0 / 16)
                enb, enbr = tmpf.get()
                act(enb[0:64, :n], bs[0:64, :n], AF.Exp, [bsr], [enbr], scale=1.0 / 16)
                pq, pqr = PF()
                proj_fm(pq, pqr, 0, 64, wq, rq, ti, T)
                qin, qr = tmpb.get()
                stt(qin[0:64, :n], pq[0:64, :n], 0.125, eb[0:64, :n], ALU.mult, ALU.mult, [pqr, ebr], [qr])
                pk_, pkr_ = PF()
                proj_fm(pk_, pkr_, 0, 64, wk, rk, ti, T)
                kin, kr_ = tmpb.get()
                tt(kin[0:64, :n], pk_[0:64, :n], enb[0:64, :n], ALU.mult, [pkr_, enbr], [kr_])
                if CUT < 6:
                    continue
                for ch in T['chunks']:
                    c, o0, st = ch['c'], ch['off'], ch['stream']
                    st_action(ch['pre'], l, st, 'gla', h)
                    Sb = STB[st]['gla']
                    Sr = STR[st]['gla'][h]
                    t16, r16 = refresh16(st, 'gla', h)
                    vt, vr = cbp.get()
                    vtok(ch, T, ti, wv, rv, vt, vr)
                    pk, pkr = PB()
                    tr(pk[0:c, 0:64], kin[0:64, o0:o0 + c], identb[0:64, 0:64], [kr_, R_idb], [pkr])
                    kt, ktr = cbp.get()
                    cp(kt[0:c, 0:64], pk[0:c, 0:64], [pkr], [ktr])
                    psc, pscr = PF()
                    mm(psc[0:c, 0:c], kin[0:64, o0:o0 + c], qin[0:64, o0:o0 + c], True, True, [kr_, qr], [pscr])
                    P, Pr = cbp.get()
                    tt(P[0:c, 0:c], psc[0:c, 0:c], maskT[0:c, 0:c], ALU.mult, [pscr, R_cst], [Pr])
                    po, por = PF()
                    if c <= 64:
                        mm(po[0:c, 0:128], P[0:c, 0:c], vt[0:c, 0:128], True, False, [Pr, vr], [por])
                        mm(po[0:c, 0:128], qin[0:64, o0:o0 + c], t16[0:64, 0:128], False, True, [qr, r16], [por])
                    else:
                        mm(po[0:c, 0:128], qin[0:64, o0:o0 + c], t16[0:64, 0:128], True, False, [qr, r16], [por])
                        mm(po[0:c, 0:128], P[0:c, 0:c], vt[0:c, 0:128], False, True, [Pr, vr], [por])
                    pd, pdr = PF()
                    mm(pd[0:64, 0:128], kt[0:c, 0:64], vt[0:c, 0:128], True, True, [ktr, vr], [pdr])
                    tt(Sb[0:64, h, :], Sb[0:64, h, :], pd[0:64, 0:128], ALU.add, [Sr, pdr], [Sr])
                    ts(Sb[0:64, h, :], Sb[0:64, h, :], eb[0:64, o0 + c - 1:o0 + c], None, ALU.mult, None, [Sr, ebr], [Sr])
                    headnorm_gate(po[0:c, 0:128], [por], c, gate[:, o0:o0 + c], [gr_], 8 + h, ti,
                                  slice(T['col'] + o0, T['col'] + o0 + c))
                    st_action(ch['post'], l, st, 'gla', h)

    def ml_prepass(l, tiles):
        branch_ml(l, tiles, CUT=3)

    def branch_ml(l, tiles, CUT=9):
        Wl = None
        wsmall(gwt[:].rearrange("p k n -> p (k n)"), 'gwt', l, R_gwt)
        for ti, T in enumerate(tiles):
            n = T['n']
            cs_ = slice(T['col'], T['col'] + n)
            pi_, pir = PF()
            for kc in range(KC):
                mm(pi_[0:4, :n], gwt[:, kc, 0:4], hT[:, kc, cs_], kc == 0, kc == KC - 1, [R_gwt, Rh[kc][ti]], [pir])
            pf_, pfr = PF()
            for kc in range(KC):
                mm(pf_[0:4, :n], gwt[:, kc, 4:8], hT[:, kc, cs_], kc == 0, kc == KC - 1, [R_gwt, Rh[kc][ti]], [pfr])
            a_, ar = tmpf.get()
            ts(a_[0:4, :n], pi_[0:4, :n], vec[0:4, l, V_BI:V_BI + 1], None, ALU.add, None, [pir, R_vec], [ar])
            e, er = tmpf.get()
            act(e[0:4, :n], pf_[0:4, :n], AF.Exp, [pfr, R_nvec], [er], bias=nvec[0:4, l, V_BF:V_BF + 1], scale=-1.0)
            sp_, spr = tmpf.get()
            act(sp_[0:4, :n], e[0:4, :n], AF.Ln, [er], [spr], bias=1.0)
            bs, bsr = tmpf.get()
            sp_tile = T['kind'] == 'sp'
            rm = cst[0:4, C_RMS:C_RMS + 80] if sp_tile else cst[0:4, C_RM:C_RM + 512]
            ra = cst[0:4, C_RAS:C_RAS + 80] if sp_tile else cst[0:4, C_RA:C_RA + 512]
            scan(bs[0:4, :n], rm, sp_[0:4, :n], 0.0, ALU.mult, ALU.add, [spr, R_cst], [bsr])
            tt(a_[0:4, :n], a_[0:4, :n], bs[0:4, :n], ALU.add, [ar, bsr], [ar])
            g_, gr2 = tmpf.get()
            scan(g_[0:4, :n], ra, a_[0:4, :n], -1e30, ALU.add, ALU.max, [ar, R_cst], [gr2])
            act(rows[0:4, 3, :n], a_[0:4, :n], AF.Exp, [ar], [R_s0, R_s1])
            mxt, mxr = tmpf.get()
            tmp_, tmpr = tmpf.get()
            if CUT < 2:
                continue
            for k, ch in enumerate(T['chunks']):
                c, o0, st = ch['c'], ch['off'], ch['stream']
                st_action(ch['pre'], l, st, 'm', 0)
                mrow = STB[st]['m']
                mr = STR[st]['m'][0]
                e1 = o0 + c - 1
                ts(mxt[0:4, o0:o0 + c], g_[0:4, o0:o0 + c], mrow[0:4, 0:1], None, ALU.max, None, [gr2, mr], [mxr])
                act(rows[0:4, 0, o0:o0 + c], mxt[0:4, o0:o0 + c], AF.Exp, [mxr], [R_s0, R_s1], scale=-1.0)
                act(rows[0:4, 1, o0:o0 + c], mxt[0:4, o0:o0 + c], AF.Exp, [mxr, mr], [R_s0, R_s1], scale=-1.0,
                    bias=mrow[0:4, 0:1])
                tt(tmp_[0:4, o0:o0 + c], bs[0:4, o0:o0 + c], mxt[0:4, o0:o0 + c], ALU.subtract, [bsr, mxr], [tmpr])
                act(rows[0:4, 2, o0:o0 + c], tmp_[0:4, o0:o0 + c], AF.Exp, [tmpr], [R_s0, R_s1])
                mmx, mmr = csp.get()
                tt(mmx[0:4, 0:1], mrow[0:4, 0:1], g_[0:4, e1:e1 + 1], ALU.max, [mr, gr2], [mmr])
                act(sct[0:4, 2 * k:2 * k + 1], mmx[0:4, 0:1], AF.Exp, [mmr, mr], [R_sct], scale=-1.0, bias=mrow[0:4, 0:1])
                act(sct[0:4, 2 * k + 1:2 * k + 2], mmx[0:4, 0:1], AF.Exp, [mmr], [R_sct], scale=-1.0)
                tt(mrow[0:4, 0:1], mmx[0:4, 0:1], bs[0:4, e1:e1 + 1], ALU.subtract, [mmr, bsr], [mr])
                st_action(ch['post'], l, st, 'm', 0)
            if CUT < 3:
                continue
            for k, ch in enumerate(T['chunks']):
                c, o0 = ch['c'], ch['off']
                pc, pcr = PF()
                for kind in range(4):
                    mm(pc[0:c, 4 * kind:4 * kind + 4], rows[0:4, kind, o0:o0 + c], identf[0:4, 0:4], True, True,
                       [R_s0, R_s1, R_cst], [pcr])
                cp(colsb[0:c, ch['g'], :], pc[0:c, 0:16], [pcr], [R_cols[ch['g']]])
            nch = len(T['chunks'])
            for h in range(4):
                pc, pcr = PF()
                mm(pc[:, 0:2 * nch], cst[0:4, C_SEL + 128 * h:C_SEL + 128 * h + 128], sct[0:4, 0:2 * nch], True, True,
                   [R_cst, R_sct], [pcr])
                cp(scb[:, ti, h, 0:2 * nch], pc[:, 0:2 * nch], [pcr], [R_scb[ti][h]])
        for h in range(4):
            if CUT < 4:
                continue
            parts = wjob([(Wl[:, OFF['mx'] + 128 * h:OFF['mx'] + 128 * h + 128], KC, 128),
                          (Wl[:, OFF['mz'] + 128 * h:OFF['mz'] + 128 * h + 128], KC, 128),
                          (w_mq[l, h], 1, 128), (w_mk[l, h], 1, 128), (w_mv[l, h], 1, 128)])
            (wx, rx), (wz, rz), (wq, rq), (wk, rk), (wv, rv) = parts
            cw = vec[:, l, V_CW + 4 * h:V_CW + 4 * h + 4]
            for ti, T in enumerate(tiles):
                n = T['n']
                px, pxr = PF()
                proj_fm(px, pxr, 0, 128, wx, rx, ti, T)
                mxf, mxfr = mxfp.get()
                mxb, mxbr = tmpb.get()
                cp(mxb[:, :n], px[:, :n], [pxr], [mxbr], eng='act')
                cT, cTr = ctp.get()
                base = 0
                for sg in T['segs']:
                    o0, sn, st = sg['off'], sg['n'], sg['stream']
                    ch0 = [ch for ch in T['chunks'] if ch['off'] == o0][0]
                    chl = [ch for ch in T['chunks'] if ch['off'] + ch['c'] == o0 + sn][0]
                    st_action(ch0['pre'], l, st, 'conv', h)
                    cvb = STB[st]['conv']
                    cvr = STR[st]['conv'][h]
                    cp(mxf[:, base:base + 3], cvb[:, h, 0:3], [cvr], [mxfr])
                    cp(mxf[:, base + 3:base + 3 + sn], px[:, o0:o0 + sn], [pxr], [mxfr])
                    cp(cvb[:, h, 0:3], mxf[:, base + sn:base + sn + 3], [mxfr], [cvr])
                    st_action(chl['post'], l, st, 'conv', h)
                    ts(cT[:, o0:o0 + sn], mxf[:, base:base + sn], cw[:, 0:1], vec[:, l, V_CB + h:V_CB + h + 1],
                       ALU.mult, ALU.add, [mxfr, R_vec], [cTr])
                    for j in range(1, 4):
                        stt(cT[:, o0:o0 + sn], mxf[:, base + j:base + j + sn], cw[:, j:j + 1], cT[:, o0:o0 + sn],
                            ALU.mult, ALU.add, [mxfr, R_vec, cTr], [cTr])
                    base += sn + 3
                act(cT[:, :n], cT[:, :n], AF.Silu, [cTr], [cTr])
                if CUT < 5:
                    continue
                cb_, cbr_ = tmpb.get()
                cp(cb_[:, :n], cT[:, :n], [cTr], [cbr_])
                pq, pqr = PF()
                mm(pq[:, :n], wq[:, 0, :], cb_[:, :n], True, True, [rq, cbr_], [pqr])
                qb, qbr = tmpb.get()
                cp(qb[:, :n], pq[:, :n], [pqr], [qbr], eng='act')
                pk_, pkr_ = PF()
                mm(pk_[:, :n], wk[:, 0, :], cb_[:, :n], True, True, [rk, cbr_], [pkr_])
                kb, kbr = tmpb.get()
                ts(kb[:, :n], pk_[:, :n], float(128 ** -0.5), None, ALU.mult, None, [pkr_], [kbr])
                pz, pzr = PF()
                proj_fm(pz, pzr, 0, 128, wz, rz, ti, T)
                sig, sgr = gatep.get()
                act(sig[:, :n], pz[:, :n], AF.Sigmoid, [pzr], [sgr])
                if CUT < 6:
                    continue
                for k, ch in enumerate(T['chunks']):
                    c, o0, st, g = ch['c'], ch['off'], ch['stream'], ch['g']
                    st_action(ch['pre'], l, st, 'ml', h)
                    Cb = STB[st]['ml']
                    Cr = STR[st]['ml'][h]
                    t16, r16 = refresh16(st, 'ml', h)
                    col = colsb[0:c, g, :]
                    Rc = R_cols[g]
                    va, var_ = vaugp.get()
                    pv, pvr = PF()
                    mm(pv[0:c, 0:128], mxb[:, o0:o0 + c], wv[:, 0, :], True, True, [mxbr, rv], [pvr])
                    cp(va[0:c, 0:128], pv[0:c, 0:128], [pvr], [var_], eng='act')
                    pk, pkr = PB()
                    tr(pk[0:c, 0:128], kb[:, o0:o0 + c], identb[:, :], [kbr, R_idb], [pkr])
                    kea, kear = cbp.get()
                    ts(kea[0:c, 0:128], pk[0:c, 0:128], col[:, 12 + h:13 + h], None, ALU.mult, None, [pkr, Rc], [kear])
                    psc, pscr = PF()
                    mm(psc[0:c, 0:c], kb[:, o0:o0 + c], qb[:, o0:o0 + c], True, True, [kbr, qbr], [pscr])
                    P, Pr = cbp.get()
                    stt(P[0:c, 0:c], psc[0:c, 0:c], col[:, 12 + h:13 + h], maskT[0:c, 0:c], ALU.mult, ALU.mult,
                        [pscr, Rc, R_cst], [Pr])
                    pA, pAr = PF()
                    mm(pA[0:c, 0:129], P[0:c, 0:c], va[0:c, 0:129], True, True, [Pr, var_], [pAr])
                    pB, pBr = PF()
                    mm(pB[0:c, 0:129], qb[:, o0:o0 + c], t16[:, 0:129], True, True, [qbr, r16], [pBr])
                    tA, tAr = cfp.get()
                    ts(tA[0:c, 0:129], pA[0:c, 0:129], col[:, 0 + h:1 + h], None, ALU.mult, None, [pAr, Rc], [tAr])
                    num, numr = cfp.get()
                    stt(num[0:c, 0:129], pB[0:c, 0:129], col[:, 4 + h:5 + h], tA[0:c, 0:129], ALU.mult, ALU.add,
                        [pBr, Rc, tAr], [numr])
                    sc_, scr_ = csp.get()
                    stt(sc_[0:c, 0:1], num[0:c, 128:129], -1.0, num[0:c, 128:129], ALU.mult, ALU.max, [numr], [scr_])
                    tt(sc_[0:c, 1:2], sc_[0:c, 0:1], col[:, 8 + h:9 + h], ALU.max, [scr_, Rc], [scr_])
                    recip(sc_[0:c, 2:3], sc_[0:c, 1:2], [scr_], [scr_])
                    hh, hhr = cfp.get()
                    ts(hh[0:c, 0:128], num[0:c, 0:128], sc_[0:c, 2:3], None, ALU.mult, None, [numr, scr_], [hhr])
                    pd, pdr = PF()
                    mm(pd[:, 0:129], kea[0:c, 0:128], va[0:c, 0:129], True, True, [kear, var_], [pdr])
                    tc_, tcr = cfp.get()
                    ts(tc_[:, 0:129], pd[:, 0:129], scb[:, ti, h, 2 * k + 1:2 * k + 2], None, ALU.mult, None,
                       [pdr, R_scb[ti][h]], [tcr])
                    stt(Cb[:, h, 0:129], Cb[:, h, 0:129], scb[:, ti, h, 2 * k:2 * k + 1], tc_[:, 0:129], ALU.mult, ALU.add,
                        [Cr, tcr, R_scb[ti][h]], [Cr])
                    headnorm_gate(hh[0:c, 0:128], [hhr], c, sig[:, o0:o0 + c], [sgr], 4 + h, ti,
                                  slice(T['col'] + o0, T['col'] + o0 + c),
                                  skip=(cT[:, o0:o0 + c], [cTr], vec[:, l, V_SK + h:V_SK + h + 1]))
                    st_action(ch['post'], l, st, 'ml', h)


    ssp = Pool(nc, "ssp", 3, [128, 16], F32)
    okp = Pool(nc, "okp", 3, [128, 4, 128], BF16)

    def hn_part1(cx, k, src, srcR, c, scale_ap=None, scaleR=()):
        if 'ss' not in cx:
            cx['ss'] = ssp.get()
            cx['ok'] = okp.get()
            mset(cx['ss'][0][:, 0:16], 1.0, [cx['ss'][1]])
        ss, ssr = cx['ss']
        ok, okr = cx['ok']
        junk, jr = cfp.get()
        if scale_ap is None:
            act(junk[0:c, 0:128], src, AF.Square, srcR, [jr, ssr], accum=ss[0:c, k:k + 1])
            cp(ok[0:c, k, :], src, srcR, [okr], eng='act')
        else:
            act(junk[0:c, 0:128], src, AF.Square, srcR + list(scaleR), [jr, ssr], accum=ss[0:c, k:k + 1], scale=scale_ap)
            ts(ok[0:c, k, :], src, scale_ap, None, ALU.mult, None, srcR + list(scaleR), [okr])

    def hn_finish(it, cx, gate, gateR, oi, skip=None):
        ti, T = it['ti'], it['T']
        chs = T['chunks']
        nch = len(chs)
        ss, ssr = cx['ss']
        ok, okr = cx['ok']
        act(ss[:, 4:4 + nch], ss[:, 0:nch], AF.Ln, [ssr, R_eps], [ssr], bias=epsb[:, 0:1], scale=1.0 / 128)
        act(ss[:, 8:8 + nch], ss[:, 4:4 + nch], AF.Exp, [ssr], [ssr], scale=-0.5)
        for k, ch in enumerate(chs):
            c, o0 = ch['c'], ch['off']
            colsl = slice(T['col'] + o0, T['col'] + o0 + c)
            on, onr = cbp.get()
            amul(on[0:c, 0:128], ok[0:c, k, :], ss[0:c, 8 + k:9 + k], [okr, ssr], [onr])
            pt, ptr = PB()
            tr(pt[:, 0:c], on[0:c, 0:128], identb[0:c, 0:c], [onr, R_idb], [ptr])
            if skip is None:
                tt(ob[:, oi, colsl], pt[:, 0:c], gate[:, o0:o0 + c], ALU.mult, [ptr] + gateR, [Ro[oi][ti]])
            else:
                cT, cR, sk_ap = skip
                t2, t2r = cfp.get()
                stt(t2[:, 0:c], cT[:, o0:o0 + c], sk_ap, pt[:, 0:c], ALU.mult, ALU.add, [ptr, R_vec] + cR, [t2r])
                tt(ob[:, oi, colsl], t2[:, 0:c], gate[:, o0:o0 + c], ALU.mult, [t2r] + gateR, [Ro[oi][ti]])

    ebp = Pool(nc, "ebp", 2, [128, 512], F32)

    def run_to_end(gen):
        try:
            while True:
                next(gen)
        except StopIteration as e:
            return e.value

    def pipeline(items, after_item):
        ctx = run_to_end(items[0]['fns'][0](items[0]))
        for i, it in enumerate(items):
            stage, chA, chB, finish = it['fns']
            nxt = items[i + 1] if i + 1 < len(items) else None
            gen = nxt['fns'][0](nxt) if nxt is not None else None
            nctx = None
            gdone = gen is None
            chs = it['T']['chunks']
            a = chA(it, ctx, chs[0])
            if not gdone and nxt['mode'] == 1:
                nctx = run_to_end(gen)
                gdone = True
            for k, ch in enumerate(chs):
                an = chA(it, ctx, chs[k + 1]) if k + 1 < len(chs) else None
                if not gdone:
                    try:
                        next(gen)
                    except StopIteration as e:
                        gdone = True
                        nctx = e.value
                ctx['nxt_ch'] = chs[k + 1] if k + 1 < len(chs) else None
                chB(it, ctx, ch, a)
                a = an
            finish(it, ctx)
            if not gdone:
                nctx = run_to_end(gen)
            after_item(it)
            ctx = nctx

    vbuf = mixb[:].rearrange("p k n -> p (k n)").rearrange("p (b g n) -> p b g n", b=2, g=8)

    def vregs(b, g):
        k = 4 * b + g // 2
        return [Rm[k][0], Rm[k][1], Rm[k][2]]

    def vview(b, g, h):
        return vbuf[:, b, g, 128 * h:128 * h + 128], vregs(b, g)

    def v_prepass(l, tiles, b, kind):
        (wv, rv), = wjobp(kind, 0, l, slot=A_SLOTS[b])
        for ti, T in enumerate(tiles):
            for ch in T['chunks']:
                c = ch['c']
                c0 = T['col'] + ch['off']
                ps, pr = PF()
                for kc in range(KC):
                    mm(ps[0:c, 0:512], hT[:, kc, c0:c0 + c], wv[:, kc, :], kc == 0, kc == KC - 1, [Rh[kc][ti], rv], [pr])
                cp(vbuf[0:c, b, ch['g'], :], ps[0:c, 0:512], [pr], vregs(b, ch['g']), eng='act')

    def get_t16(cx, st, kind, h):
        t = cx.pop('t16next', None)
        if t is not None:
            return t
        return refresh16(st, kind, h)

    def early_refresh(cx, ch, kind, h):
        nx = cx.get('nxt_ch')
        if nx is not None and nx['pre'] is None and nx['stream'] == ch['stream']:
            cx['t16next'] = refresh16(ch['stream'], kind, h)

    GETJOB = {}

    def mixers(l, tiles, stages):
        nt = len(tiles)
        A_jobs = ([('ret', h) for h in range(4)] if 'ret' in stages else []) + ([('gla', h) for h in range(4)] if 'gla' in stages else [])
        B_jobs = [('ml', h) for h in range(4)] if 'ml' in stages else []
        issued = {}

        def issue(stream, j):
            jobs, slots = (A_jobs, A_SLOTS) if stream == 'A' else (B_jobs, B_SLOTS)
            if j < len(jobs) and (stream, j) not in issued:
                issued[(stream, j)] = wjobp(jobs[j][0], jobs[j][1], l, slot=slots[j % 2])

        def getter(stream, base):
            def get(h):
                return issued[(stream, base + h)]
            return get
        GETJOB['ret'] = getter('A', 0)
        GETJOB['gla'] = getter('A', 4 if 'ret' in stages else 0)
        GETJOB['ml'] = getter('B', 0)
        if 'ml' in stages:
            ml_prepass(l, tiles)
        if 'gla' in stages:
            gla_prepass(l, tiles)
        issue('B', 0)
        issue('B', 1)
        if 'ret' in stages:
            v_prepass(l, tiles, 0, 'retv')
        if 'gla' in stages:
            v_prepass(l, tiles, 1, 'glav')
        issue('A', 0)
        issue('A', 1)
        fns = {}
        if 'ret' in stages:
            fns['ret'] = branch_ret2(l, tiles)
        if 'gla' in stages:
            fns['gla'] = branch_gla2(l, tiles)
        if 'ml' in stages:
            fns['ml'] = branch_ml2(l, tiles)
        A_items = []
        for ji, (br, h) in enumerate(A_jobs):
            for ti, T in enumerate(tiles):
                A_items.append(dict(br=br, h=h, ti=ti, T=T, last=(ti == nt - 1), fns=fns[br], stream='A', j=ji, mode=1))
        B_items = []
        for ji, (br, h) in enumerate(B_jobs):
            for ti, T in enumerate(tiles):
                B_items.append(dict(br=br, h=h, ti=ti, T=T, last=(ti == nt - 1), fns=fns[br], stream='B', j=ji, mode=ML_MODE))
        items = []
        ia = ib = 0
        while ia < len(A_items) or ib < len(B_items):
            for _ in range(2):
                if ia < len(A_items):
                    items.append(A_items[ia])
                    ia += 1
            if ib < len(B_items):
                items.append(B_items[ib])
                ib += 1

        def after_item(it):
            if it['last']:
                issue(it['stream'], it['j'] + 2)
        if items:
            pipeline(items, after_item)

    def make_items(tiles):
        return [dict(h=h, ti=ti, T=T, last=(ti == len(tiles) - 1)) for h in range(4) for ti, T in enumerate(tiles)]

    JM_L = [0]

    def job_mgr(parts_for):
        jobs = {}

        def get(h):
            if h not in jobs and h < 4:
                jobs[h] = wjobp(parts_for, h, JM_L[0])
            return jobs.get(h)

        def after_item(it):
            if it['last']:
                get(it['h'] + 3)
        get(0)
        get(1)
        get(2)
        return get, after_item

    def branch_ret2(l, tiles):
        Wl = None

        def parts_for(h):
            return [(Wl[:, OFF['rq'] + 64 * h:OFF['rq'] + 64 * h + 64], KC, 64),
                    (w_rs[l][:, 64 * h:64 * h + 64], KC, 64),
                    (Wl[:, OFF['rk'] + 64 * h:OFF['rk'] + 64 * h + 64], KC, 64),
                    (w_rs[l][:, 256 + 64 * h:256 + 64 * h + 64], KC, 64),
                    (Wl[:, OFF['rv'] + 128 * h:OFF['rv'] + 128 * h + 128], KC, 128),
                    (Wl[:, OFF['rg'] + 128 * h:OFF['rg'] + 128 * h + 128], KC, 128)]
        get = GETJOB['ret']

        def stage(it):
            h, ti, T = it['h'], it['ti'], it['T']
            (wqq, rqq), (wkk, rkk), (wg, rgt) = get(h)
            n = T['n']
            load_tab2(T)
            pas = []
            for (w2, r2) in ((wqq, rqq), (wkk, rkk)):
                pa, par = PF()
                proj_fm(pa, par, 0, 128, w2, r2, ti, T)
                t, t_r = tmpb.get()
                tt(t[:, :n], pa[:, :n], tab2[:, 0:n], ALU.mult, [par, R_tab], [t_r])
                pas.append((t, t_r))
            yield
            pg, pgr = PF()
            proj_fm(pg, pgr, 0, 128, wg, rgt, ti, T)
            gate, gr_ = gatep.get()
            act(gate[:, :n], pg[:, :n], AF.Silu, [pgr], [gr_])
            yield
            outs = []
            for (t, t_r), isq in zip(pas, (True, False)):
                pf, pfr = PF()
                mm(pf[0:64, :n], foldb[:, 0:64], t[:, :n], True, True, [R_fold, t_r], [pfr])
                ob_, obr_ = tmpb.get()
                if isq:
                    for sg in T['segs']:
                        o0, sn = sg['off'], sg['n']
                        if sn >= 128:
                            nch = sn // 128
                            gq = cst[0:64, C_GQ + 128 * h:C_GQ + 128 * h + 128].unsqueeze(1).to_broadcast([64, nch, 128])
                            tt(ob_[0:64, o0:o0 + sn].rearrange("p (c t) -> p c t", t=128),
                               pf[0:64, o0:o0 + sn].rearrange("p (c t) -> p c t", t=128), gq, ALU.mult,
                               [pfr, R_cst], [obr_])
                        else:
                            tt(ob_[0:64, o0:o0 + sn], pf[0:64, o0:o0 + sn],
                               cst[0:64, C_GQ + 128 * h:C_GQ + 128 * h + sn], ALU.mult, [pfr, R_cst], [obr_])
                else:
                    cp(ob_[0:64, :n], pf[0:64, :n], [pfr], [obr_], eng='act')
                outs.append((ob_, obr_))
            yield
            (qin, qr), (kr, krr) = outs
            return dict(qin=qin, qr=qr, kr=kr, krr=krr, gate=gate, gr=gr_)

        def chA(it, cx, ch):
            h, ti, T = it['h'], it['ti'], it['T']
            c, o0 = ch['c'], ch['off']
            kr, krr, qin, qr = cx['kr'], cx['krr'], cx['qin'], cx['qr']
            psc, pscr = PF()
            mm(psc[0:c, 0:c], kr[0:64, o0:o0 + c], qin[0:64, o0:o0 + c], True, True, [krr, qr], [pscr])
            P, Pr = cbp.get()
            tt(P[0:c, 0:c], psc[0:c, 0:c], cst[0:c, C_DR + 128 * h:C_DR + 128 * h + c], ALU.mult, [pscr, R_cst], [Pr])
            vt, vr = vview(0, ch['g'], h)
            pk, pkr = PB()
            tr(pk[0:c, 0:64], kr[0:64, o0:o0 + c], identb[0:64, 0:64], [krr, R_idb], [pkr])
            kst, kstr = cbp.get()
            ts(kst[0:c, 0:64], pk[0:c, 0:64], cst[0:c, C_KS + 3 * h + ch['li']:C_KS + 3 * h + ch['li'] + 1], None,
               ALU.mult, None, [pkr, R_cst], [kstr])
            return dict(P=P, Pr=Pr, vt=vt, vr=vr, kst=kst, kstr=kstr)

        def chB(it, cx, ch, a):
            h, ti, T = it['h'], it['ti'], it['T']
            c, o0, st = ch['c'], ch['off'], ch['stream']
            qin, qr = cx['qin'], cx['qr']
            st_action(ch['pre'], l, st, 'ret', h)
            Sb = STB[st]['ret']
            Sr = STR[st]['ret'][h]
            t16, r16 = get_t16(cx, st, 'ret', h)
            pd, pdr = PF()
            mm(pd[0:64, 0:128], a['kst'][0:c, 0:64], a['vt'][0:c, 0:128], True, True, [a['kstr']] + a['vr'], [pdr])
            stt(Sb[0:64, h, :], Sb[0:64, h, :], float(GAM[h] ** c), pd[0:64, 0:128], ALU.mult, ALU.add, [Sr, pdr], [Sr])
            early_refresh(cx, ch, 'ret', h)
            po, por = PF()
            if c <= 64:
                mm(po[0:c, 0:128], a['P'][0:c, 0:c], a['vt'][0:c, 0:128], True, False, [a['Pr']] + a['vr'], [por])
                mm(po[0:c, 0:128], qin[0:64, o0:o0 + c], t16[0:64, 0:128], False, True, [qr, r16], [por])
            else:
                mm(po[0:c, 0:128], qin[0:64, o0:o0 + c], t16[0:64, 0:128], True, False, [qr, r16], [por])
                mm(po[0:c, 0:128], a['P'][0:c, 0:c], a['vt'][0:c, 0:128], False, True, [a['Pr']] + a['vr'], [por])
            hn_part1(cx, T['chunks'].index(ch), po[0:c, 0:128], [por], c)
            st_action(ch['post'], l, st, 'ret', h)

        def finish(it, cx):
            hn_finish(it, cx, cx['gate'], [cx['gr']], it['h'])

        return (stage, chA, chB, finish)

    def gla_prepass(l, tiles):
        Wl = None
        wsmall(gaw[:].rearrange("p k n -> p (k n)"), 'gaw', l, R_gaw)
        wsmall(wa2[:], 'wa2', l, R_wa2, npart=16)
        for ti, T in enumerate(tiles):
            n = T['n']
            pg, pgr = PF()
            proj_fm(pg, pgr, 0, 16, gaw, R_gaw, ti, T)
            cp(gaT[0:16, T['col']:T['col'] + n], pg[0:16, :n], [pgr], [R_gaT[ti]])


    def branch_gla2(l, tiles):
        Wl = None
        def parts_for(h):
            return [(Wl[:, OFF['gq'] + 64 * h:OFF['gq'] + 64 * h + 64], KC, 64),
                    (Wl[:, OFF['gk'] + 64 * h:OFF['gk'] + 64 * h + 64], KC, 64),
                    (Wl[:, OFF['gv'] + 128 * h:OFF['gv'] + 128 * h + 128], KC, 128),
                    (Wl[:, OFF['gr'] + 128 * h:OFF['gr'] + 128 * h + 128], KC, 128)]
        get = GETJOB['gla']

        def stage(it):
            h, ti, T = it['h'], it['ti'], it['T']
            (wqk, rqk), (wg, rgt) = get(h)
            n = T['n']
            cs_ = slice(T['col'], T['col'] + n)
            pz, pzr = PF()
            mm(pz[:, :n], wa2[0:16, 128 * h:128 * h + 128], gaT[0:16, cs_], True, True, [R_wa2, R_gaT[ti]], [pzr])
            e, er = tmpf.get()
            act(e[:, :n], pz[:, :n], AF.Exp, [pzr, R_nvec], [er], bias=nvec[:, l, V_BA + h:V_BA + h + 1], scale=-1.0)
            sp_, spr = tmpf.get()
            act(sp_[:, :n], e[:, :n], AF.Ln, [er], [spr], bias=1.0)
            bs, bsr = tmpf.get()
            rm = cst[:, C_RMS:C_RMS + 80] if T['kind'] == 'sp' else cst[:, C_RM:C_RM + 512]
            scan(bs[:, :n], rm, sp_[:, :n], 0.0, ALU.mult, ALU.add, [spr, R_cst], [bsr])
            ebn, ebr = ebp.get()
            act(ebn[:, :n], bs[:, :n], AF.Exp, [bsr, R_cst], [ebr], scale=cst[:, C_ESC:C_ESC + 1])
            yield
            pg, pgr = PF()
            proj_fm(pg, pgr, 0, 128, wg, rgt, ti, T)
            gate, gr_ = gatep.get()
            act(gate[:, :n], pg[:, :n], AF.Silu, [pgr], [gr_])
            yield
            pqk, pqkr = PF()
            proj_fm(pqk, pqkr, 0, 128, wqk, rqk, ti, T)
            qk, qkr = tmpb.get()
            stt(qk[:, :n], pqk[:, :n], cst[:, C_QSC:C_QSC + 1], ebn[:, :n], ALU.mult, ALU.mult, [pqkr, ebr, R_cst], [qkr])
            pf, pfr = PF()
            mm(pf[0:64, :n], selb[:, 0:64], qk[:, :n], True, True, [R_sel, qkr], [pfr])
            kin, kr_ = tmpb.get()
            cp(kin[0:64, :n], pf[0:64, :n], [pfr], [kr_], eng='act')
            yield
            return dict(qin=qk, qr=qkr, kin=kin, kr=kr_, gate=gate, gr=gr_, eb=ebn, ebr=ebr)

        def chA(it, cx, ch):
            h, ti, T = it['h'], it['ti'], it['T']
            c, o0 = ch['c'], ch['off']
            kin, kr_, qin, qr = cx['kin'], cx['kr'], cx['qin'], cx['qr']
            vt, vr = vview(1, ch['g'], h)
            pk, pkr = PB()
            tr(pk[0:c, 0:64], kin[0:64, o0:o0 + c], identb[0:64, 0:64], [kr_, R_idb], [pkr])
            kt, ktr = cbp.get()
            cp(kt[0:c, 0:64], pk[0:c, 0:64], [pkr], [ktr])
            psc, pscr = PF()
            mm(psc[0:c, 0:c], kin[0:64, o0:o0 + c], qin[0:64, o0:o0 + c], True, True, [kr_, qr], [pscr])
            P, Pr = cbp.get()
            tt(P[0:c, 0:c], psc[0:c, 0:c], maskT[0:c, 0:c], ALU.mult, [pscr, R_cst], [Pr])
            return dict(P=P, Pr=Pr, vt=vt, vr=vr, kt=kt, ktr=ktr)

        def chB(it, cx, ch, a):
            h, ti, T = it['h'], it['ti'], it['T']
            c, o0, st = ch['c'], ch['off'], ch['stream']
            qin, qr, eb, ebr = cx['qin'], cx['qr'], cx['eb'], cx['ebr']
            st_action(ch['pre'], l, st, 'gla', h)
            Sb = STB[st]['gla']
            Sr = STR[st]['gla'][h]
            t16, r16 = get_t16(cx, st, 'gla', h)
            pd, pdr = PF()
            mm(pd[0:64, 0:128], a['kt'][0:c, 0:64], a['vt'][0:c, 0:128], True, True, [a['ktr']] + a['vr'], [pdr])
            tt(Sb[0:64, h, :], Sb[0:64, h, :], pd[0:64, 0:128], ALU.add, [Sr, pdr], [Sr])
            ts(Sb[0:64, h, :], Sb[0:64, h, :], eb[0:64, o0 + c - 1:o0 + c], None, ALU.mult, None, [Sr, ebr], [Sr])
            early_refresh(cx, ch, 'gla', h)
            po, por = PF()
            if c <= 64:
                mm(po[0:c, 0:128], a['P'][0:c, 0:c], a['vt'][0:c, 0:128], True, False, [a['Pr']] + a['vr'], [por])
                mm(po[0:c, 0:128], qin[0:64, o0:o0 + c], t16[0:64, 0:128], False, True, [qr, r16], [por])
            else:
                mm(po[0:c, 0:128], qin[0:64, o0:o0 + c], t16[0:64, 0:128], True, False, [qr, r16], [por])
                mm(po[0:c, 0:128], a['P'][0:c, 0:c], a['vt'][0:c, 0:128], False, True, [a['Pr']] + a['vr'], [por])
            hn_part1(cx, T['chunks'].index(ch), po[0:c, 0:128], [por], c)
            st_action(ch['post'], l, st, 'gla', h)

        def finish(it, cx):
            hn_finish(it, cx, cx['gate'], [cx['gr']], 8 + it['h'])

        return (stage, chA, chB, finish)

    def branch_ml2(l, tiles):
        Wl = None

        def parts_for(h):
            return [(Wl[:, OFF['mx'] + 128 * h:OFF['mx'] + 128 * h + 128], KC, 128),
                    (Wl[:, OFF['mz'] + 128 * h:OFF['mz'] + 128 * h + 128], KC, 128),
                    (w_mq[l, h], 1, 128), (w_mk[l, h], 1, 128), (w_mv[l, h], 1, 128)]
        get = GETJOB['ml']

        def stage(it):
            h, ti, T = it['h'], it['ti'], it['T']
            (wx, rx), (wz, rz), (wq, rq), (wk, rk), (wv, rv) = get(h)
            cw = vec[:, l, V_CW + 4 * h:V_CW + 4 * h + 4]
            n = T['n']
            px, pxr = PF()
            proj_fm(px, pxr, 0, 128, wx, rx, ti, T)
            mxf, mxfr = mxfp.get()
            mxb, mxbr = tmpb.get()
            cp(mxb[:, :n], px[:, :n], [pxr], [mxbr], eng='act')
            cT, cTr = ctp.get()
            base = 0
            for sg in T['segs']:
                o0, sn, st = sg['off'], sg['n'], sg['stream']
                ch0 = [ch for ch in T['chunks'] if ch['off'] == o0][0]
                chl = [ch for ch in T['chunks'] if ch['off'] + ch['c'] == o0 + sn][0]
                st_action(ch0['pre'], l, st, 'conv', h)
                cvb = STB[st]['conv']
                cvr = STR[st]['conv'][h]
                cp(mxf[:, base:base + 3], cvb[:, h, 0:3], [cvr], [mxfr])
                cp(mxf[:, base + 3:base + 3 + sn], px[:, o0:o0 + sn], [pxr], [mxfr])
                cp(cvb[:, h, 0:3], mxf[:, base + sn:base + sn + 3], [mxfr], [cvr])
                st_action(chl['post'], l, st, 'conv', h)
                ts(cT[:, o0:o0 + sn], mxf[:, base:base + sn], cw[:, 0:1], vec[:, l, V_CB + h:V_CB + h + 1],
                   ALU.mult, ALU.add, [mxfr, R_vec], [cTr], eng=CONV_ENG)
                for j in range(1, 4):
                    stt(cT[:, o0:o0 + sn], mxf[:, base + j:base + j + sn], cw[:, j:j + 1], cT[:, o0:o0 + sn],
                        ALU.mult, ALU.add, [mxfr, R_vec, cTr], [cTr], eng=CONV_ENG)
                base += sn + 3
            act(cT[:, :n], cT[:, :n], AF.Silu, [cTr], [cTr])
            cb_, cbr_ = tmpb.get()
            cp(cb_[:, :n], cT[:, :n], [cTr], [cbr_])
            yield
            pq, pqr = PF()
            mm(pq[:, :n], wq[:, 0, :], cb_[:, :n], True, True, [rq, cbr_], [pqr])
            qb, qbr = tmpb.get()
            cp(qb[:, :n], pq[:, :n], [pqr], [qbr], eng='act')
            pk_, pkr_ = PF()
            mm(pk_[:, :n], wk[:, 0, :], cb_[:, :n], True, True, [rk, cbr_], [pkr_])
            kb, kbr = tmpb.get()
            ts(kb[:, :n], pk_[:, :n], float(128 ** -0.5), None, ALU.mult, None, [pkr_], [kbr])
            yield
            pz, pzr = PF()
            proj_fm(pz, pzr, 0, 128, wz, rz, ti, T)
            sig, sgr = gatep.get()
            act(sig[:, :n], pz[:, :n], AF.Sigmoid, [pzr], [sgr])
            yield
            return dict(mxb=mxb, mxbr=mxbr, cT=cT, cTr=cTr, qb=qb, qbr=qbr, kb=kb, kbr=kbr, sig=sig, sgr=sgr, wv=wv, rv=rv)

        def chA(it, cx, ch):
            h, ti, T = it['h'], it['ti'], it['T']
            c, o0, g = ch['c'], ch['off'], ch['g']
            col = colsb[0:c, g, :]
            Rc = R_cols[g]
            kb, kbr, qb, qbr = cx['kb'], cx['kbr'], cx['qb'], cx['qbr']
            va, var_ = vaugp.get()
            pv, pvr = PF()
            mm(pv[0:c, 0:128], cx['mxb'][:, o0:o0 + c], cx['wv'][:, 0, :], True, True, [cx['mxbr'], cx['rv']], [pvr])
            cp(va[0:c, 0:128], pv[0:c, 0:128], [pvr], [var_], eng='act')
            pk, pkr = PB()
            tr(pk[0:c, 0:128], kb[:, o0:o0 + c], identb[:, :], [kbr, R_idb], [pkr])
            kea, kear = cbp.get()
            ts(kea[0:c, 0:128], pk[0:c, 0:128], col[:, 12 + h:13 + h], None, ALU.mult, None, [pkr, Rc], [kear])
            psc, pscr = PF()
            mm(psc[0:c, 0:c], kb[:, o0:o0 + c], qb[:, o0:o0 + c], True, True, [kbr, qbr], [pscr])
            P, Pr = cbp.get()
            stt(P[0:c, 0:c], psc[0:c, 0:c], col[:, 12 + h:13 + h], maskT[0:c, 0:c], ALU.mult, ALU.mult,
                [pscr, Rc, R_cst], [Pr])
            return dict(va=va, var=var_, kea=kea, kear=kear, P=P, Pr=Pr)

        def chB(it, cx, ch, a):
            h, ti, T = it['h'], it['ti'], it['T']
            c, o0, st, g = ch['c'], ch['off'], ch['stream'], ch['g']
            k = T['chunks'].index(ch)
            col = colsb[0:c, g, :]
            Rc = R_cols[g]
            qb, qbr = cx['qb'], cx['qbr']
            va, var_ = a['va'], a['var']
            st_action(ch['pre'], l, st, 'ml', h)
            Cb = STB[st]['ml']
            Cr = STR[st]['ml'][h]
            t16, r16 = get_t16(cx, st, 'ml', h)
            pd, pdr = PF()
            mm(pd[:, 0:129], a['kea'][0:c, 0:128], va[0:c, 0:129], True, True, [a['kear'], var_], [pdr])
            tc_, tcr = cfp.get()
            amul(tc_[:, 0:129], pd[:, 0:129], scb[:, ti, h, 2 * k + 1:2 * k + 2], [pdr, R_scb[ti][h]], [tcr])
            stt(Cb[:, h, 0:129], Cb[:, h, 0:129], scb[:, ti, h, 2 * k:2 * k + 1], tc_[:, 0:129], ALU.mult, ALU.add,
                [Cr, tcr, R_scb[ti][h]], [Cr])
            early_refresh(cx, ch, 'ml', h)
            pA, pAr = PF()
            mm(pA[0:c, 0:129], a['P'][0:c, 0:c], va[0:c, 0:129], True, True, [a['Pr'], var_], [pAr])
            pB, pBr = PF()
            mm(pB[0:c, 0:129], qb[:, o0:o0 + c], t16[:, 0:129], True, True, [qbr, r16], [pBr])
            tA, tAr = cfp.get()
            amul(tA[0:c, 0:129], pA[0:c, 0:129], col[:, 0 + h:1 + h], [pAr, Rc], [tAr])
            num, numr = cfp.get()
            stt(num[0:c, 0:129], pB[0:c, 0:129], col[:, 4 + h:5 + h], tA[0:c, 0:129], ALU.mult, ALU.add,
                [pBr, Rc, tAr], [numr])
            sc_, scr_ = csp.get()
            stt(sc_[0:c, 0:1], num[0:c, 128:129], -1.0, num[0:c, 128:129], ALU.mult, ALU.max, [numr], [scr_])
            tt(sc_[0:c, 1:2], sc_[0:c, 0:1], col[:, 8 + h:9 + h], ALU.max, [scr_, Rc], [scr_])
            recip(sc_[0:c, 2:3], sc_[0:c, 1:2], [scr_], [scr_])
            hn_part1(cx, k, num[0:c, 0:128], [numr], c, scale_ap=sc_[0:c, 2:3], scaleR=[scr_])
            st_action(ch['post'], l, st, 'ml', h)

        def finish(it, cx):
            h = it['h']
            hn_finish(it, cx, cx['sig'], [cx['sgr']], 4 + h,
                      skip=(cx['cT'], [cx['cTr']], vec[:, l, V_SK + h:V_SK + h + 1]))

        return (stage, chA, chB, finish)

    def merge_out(l, tiles):
        Wl = None
        for ft in range(KC):
            parts3 = [wjobp('mergeb', 3 * ft + b, l) for b in range(3)]
            parts = [parts3[0][0], parts3[1][0], parts3[2][0], parts3[0][1], parts3[1][1], parts3[2][1]]
            for ti, T in enumerate(tiles):
                n = T['n']
                cs_ = slice(T['col'], T['col'] + n)
                acc, accr = tmpf.get()
                for b in range(3):
                    wz, rz = parts[b]
                    wb, rb = parts[3 + b]
                    pz, pzr = PF()
                    proj_fm(pz, pzr, 0, 128, wz, rz, ti, T)
                    sg, sgr = tmpf.get()
                    act(sg[:, :n], pz[:, :n], AF.Sigmoid, [pzr], [sgr])
                    pp, ppr = PF()
                    for kc in range(4):
                        mm(pp[:, :n], wb[:, kc, :], ob[:, 4 * b + kc, cs_], kc == 0, kc == 3, [rb, Ro[4 * b + kc][ti]], [ppr])
                    if b == 0:
                        tt(acc[:, :n], sg[:, :n], pp[:, :n], ALU.mult, [sgr, ppr], [accr])
                    else:
                        tt(sg[:, :n], sg[:, :n], pp[:, :n], ALU.mult, [sgr, ppr], [sgr])
                        if b == 1:
                            tt(acc[:, :n], acc[:, :n], sg[:, :n], ALU.add, [accr, sgr], [accr])
                        else:
                            tt(mixb[:, ft, cs_], acc[:, :n], sg[:, :n], ALU.add, [accr, sgr], [Rm[ft][ti]])
        for ft in range(KC):
            (wo, ro), = wjobp('out', ft, l)
            for ti, T in enumerate(tiles):
                n = T['n']
                cs_ = slice(T['col'], T['col'] + n)
                pp, ppr = PF()
                for kc in range(KC):
                    mm(pp[:, :n], wo[:, kc, :], mixb[:, kc, cs_], kc == 0, kc == KC - 1, [ro, Rm[kc][ti]], [ppr])
                tt(xT[:, ft, cs_], xT[:, ft, cs_], pp[:, :n], ALU.add, [Rx[ft][ti], ppr], [Rx[ft][ti]])

    def ffn(l, tiles):
        for half in range(2):
            for j in range(11):
                jj = half * 11 + j
                (wg_, rg_), (wv_, rv_) = wjobp('ffi', jj, l)
                for ti, T in enumerate(tiles):
                    n = T['n']
                    cs_ = slice(T['col'], T['col'] + n)
                    pg, pgr = PF()
                    proj_fm(pg, pgr, 0, 128, wg_, rg_, ti, T)
                    sl, slr = tmpf.get()
                    act(sl[:, :n], pg[:, :n], AF.Silu, [pgr], [slr])
                    pv, pvr = PF()
                    proj_fm(pv, pvr, 0, 128, wv_, rv_, ti, T)
                    tt(ob[:, j, cs_], sl[:, :n], pv[:, :n], ALU.mult, [slr, pvr], [Ro[j][ti]])
            for ft in range(KC):
                (wo, ro), = wjobp('ffo', half * 8 + ft, l)
                for ti, T in enumerate(tiles):
                    n = T['n']
                    cs_ = slice(T['col'], T['col'] + n)
                    pp, ppr = PF()
                    for kc in range(11):
                        mm(pp[:, :n], wo[:, kc, :], ob[:, kc, cs_], kc == 0, kc == 10, [ro, Ro[kc][ti]], [ppr])
                    tt(xT[:, ft, cs_], xT[:, ft, cs_], pp[:, :n], ALU.add, [Rx[ft][ti], ppr], [Rx[ft][ti]])

    groups = make_groups()[:ngroups]
    if not all(k in stages for k in ('ret', 'ml', 'gla')):
        allro = [r for rr_ in Ro for r in rr_]
        S.op('dve', lambda: nc.vector.memset(ob[:], 0.0), (), allro)
    for g, tiles in enumerate(groups):
        if 'noload' not in stages:
            load_x(tiles)
        for l in range(depth):
            rmsnorm(tiles, l, V_N1)
            S.mix = FILL
            mixers(l, tiles, stages)
            S.mix = False
            if 'merge' in stages:
                merge_out(l, tiles)
            if 'ffn' in stages:
                rmsnorm(tiles, l, V_N2)
                ffn(l, tiles)
        if 'nofinal' not in stages:
            rmsnorm(tiles, 0, V_NF, dst_bf=False)
    S.emit()
    return nc, S


def host_consts():
    cst = np.zeros((128, C_END), np.float32)
    cst[:, C_ID:C_ID + 128] = np.eye(128, dtype=np.float32)
    s = np.arange(128)[:, None]
    t = np.arange(128)[None, :]
    mask = (s <= t).astype(np.float32)
    cst[:, C_MASK:C_MASK + 128] = mask
    for h in range(4):
        g = np.float64(GAM[h])
        cst[:, C_DR + 128 * h:C_DR + 128 * h + 128] = (mask * (g ** (-(s + 1.0))) * 0.125).astype(np.float32)
        cst[:, C_GQ + 128 * h:C_GQ + 128 * h + 128] = (g ** (t + 1.0)).astype(np.float32) * np.ones((128, 1), np.float32)
        for li, c in enumerate(LENS):
            col = np.where(s[:, 0] < c, g ** (c - 1.0 - s[:, 0]), 0.0) * 0.125
            cst[:, C_KS + 3 * h + li] = col.astype(np.float32)
        cst[h, C_SEL + 128 * h:C_SEL + 128 * h + 128] = 1.0
    rm = np.ones(512, np.float32)
    rm[::128] = 0.0
    ra = np.zeros(512, np.float32)
    ra[::128] = -1e30
    cst[:, C_RM:C_RM + 512] = rm
    cst[:, C_RA:C_RA + 512] = ra
    rms = np.ones(80, np.float32)
    rms[0] = 0.0
    rms[16] = 0.0
    ras = np.zeros(80, np.float32)
    ras[0] = -1e30
    ras[16] = -1e30
    cst[:, C_RMS:C_RMS + 80] = rms
    cst[:, C_RAS:C_RAS + 80] = ras
    cst[:, C_ONE:C_ONE + 128] = 1.0
    cst[0:64, C_ESC] = -1.0 / 16
    cst[64:128, C_ESC] = 1.0 / 16
    cst[0:64, C_QSC] = 0.125
    cst[64:128, C_QSC] = 1.0
    half = 32
    inv = (1.0 / (np.float32(10000.0) ** np.linspace(0.0, 1.0, half, dtype=np.float32))).astype(np.float32)
    pos = np.arange(NPOS, dtype=np.float32)
    ang = (pos[:, None] * inv[None, :]).astype(np.float32)
    cos = np.cos(ang).astype(np.float32).T
    sin = np.sin(ang).astype(np.float32).T
    cs = np.zeros((64, 2, NPOS), np.float32)
    cs[0:32, 0] = cos
    cs[32:64, 0] = cos
    cs[0:32, 1] = -sin
    cs[32:64, 1] = sin
    return cst, cs


_CACHE = {}
NDUMMY = 1
FILL = False
ON_ENG = 'pool'
DUMMY_N = 512
PIPE = True
STAGE_MODE = 1
ML_MODE = 1
CONV_ENG = 'dve'
DBG_H = 0
DBG_OFF = 128
_BUILD_KW = {}
_NCORES = 8


def kernel(x_prompt, x_sample, state_ret, state_mlstm_c, state_mlstm_n, state_mlstm_m, state_mlstm_conv,
           state_gla, meta_tokens, norm1, w_in, b_mlstm_i, b_mlstm_f, conv_w, conv_b, w_mq, w_mk, w_mv,
           m_skip, w_gla_a2, b_gla_a, w_br_ret, w_br_mlstm, w_br_gla, w_out, norm2, w_ffn_in, w_ffn_out, norm_f):
    f = lambda a: np.ascontiguousarray(np.asarray(a, dtype=np.float32))
    if 'nc' not in _CACHE:
        _CACHE['nc'] = build_program(**_BUILD_KW)[0]
    nc = _CACHE['nc']
    cst, cs = host_consts()
    w_in = f(w_in)
    perm = np.concatenate([np.arange(h * 64 + 32, h * 64 + 64).tolist() + np.arange(h * 64, h * 64 + 32).tolist()
                           for h in range(4)]).astype(np.int64)
    w_rs = np.ascontiguousarray(np.concatenate([w_in[:, :, 0:256][:, :, perm], w_in[:, :, 256:512][:, :, perm]], axis=2))
    vecs = np.zeros((128, DEPTH, V_END), np.float32)
    for l in range(DEPTH):
        vecs[:, l, V_N1:V_N1 + 8] = f(norm1)[l].reshape(8, 128).T
        vecs[:, l, V_N2:V_N2 + 8] = f(norm2)[l].reshape(8, 128).T
        vecs[:, l, V_CW:V_CW + 16] = f(conv_w)[l].reshape(4, 4, 128).transpose(2, 1, 0).reshape(128, 16)
        vecs[:, l, V_CB:V_CB + 4] = f(conv_b)[l].reshape(4, 128).T
        vecs[:, l, V_SK:V_SK + 4] = f(m_skip)[l].reshape(4, 128).T
        vecs[0:64, l, V_BA:V_BA + 4] = f(b_gla_a)[l].reshape(4, 64).T
        vecs[64:128, l, V_BA:V_BA + 4] = f(b_gla_a)[l].reshape(4, 64).T
        vecs[0:4, l, V_BI] = f(b_mlstm_i)[l]
        vecs[0:4, l, V_BF] = f(b_mlstm_f)[l]
        vecs[:, l, V_NF:V_NF + 8] = f(norm_f).reshape(8, 128).T
    xp = f(x_prompt)
    xs = f(x_sample)
    srcs = dict(w_in=w_in, w_rs=w_rs, w_mq=f(w_mq).reshape(DEPTH, 512, 128), w_mk=f(w_mk).reshape(DEPTH, 512, 128),
                w_mv=f(w_mv).reshape(DEPTH, 512, 128), w_a2=f(w_gla_a2), w_br0=f(w_br_ret), w_br1=f(w_br_mlstm),
                w_br2=f(w_br_gla), w_out=f(w_out), w_fi=f(w_ffn_in), w_fo=f(w_ffn_out))
    shared = dict(meta=f(meta_tokens), wpack=pack_weights(srcs), cst=cst, vecs=vecs, cs=cs)
    sret, sgla = f(state_ret), f(state_gla)
    sc, sn, sm, scv = f(state_mlstm_c), f(state_mlstm_n), f(state_mlstm_m), f(state_mlstm_conv)
    in_maps = []
    for i in range(8):
        ml = np.concatenate([sc[:, i].transpose(0, 2, 1, 3), sn[:, i].transpose(0, 2, 1)[..., None]], axis=3)
        m = dict(shared)
        m.update(xp=np.ascontiguousarray(xp[2 * i:2 * i + 2]), xs=np.ascontiguousarray(xs[i]),
                 sti_ret=np.ascontiguousarray(sret[:, i].transpose(0, 2, 1, 3))[None],
                 sti_gla=np.ascontiguousarray(sgla[:, i].transpose(0, 2, 1, 3))[None],
                 sti_ml=np.ascontiguousarray(ml)[None],
                 sti_m=np.ascontiguousarray(sm[:, i].reshape(2, 4, 1))[None],
                 sti_conv=np.ascontiguousarray(scv[:, i].reshape(2, 3, 4, 128).transpose(0, 3, 2, 1))[None])
        in_maps.append(m)
    if _NCORES < 8:
        res = run_bass_kernel_spmd(nc, in_maps[:_NCORES], core_ids=list(range(_NCORES)))
        R = list(res.results) + [res.results[0]] * (8 - _NCORES)
    else:
        res = run_bass_kernel_spmd(nc, in_maps, core_ids=list(range(8)))
        R = res.results
    _CACHE["dbg"] = R[0].get("dbg")
    y_prompt = np.concatenate([R[i]["yp"] for i in range(8)], axis=0)
    y_sample = np.stack([R[i]["ys"] for i in range(8)], axis=0)

    def gather(kind, which):
        if which == 'p':
            return np.stack([R[i]["sto_" + kind][s] for i in range(8) for s in range(2)], axis=1)
        return np.stack([R[i]["sto_" + kind][2] for i in range(8)], axis=1)

    outs = [y_prompt, y_sample]
    for which in ('p', 's'):
        ret = gather('ret', which).transpose(0, 1, 3, 2, 4)
        ml = gather('ml', which)
        c = ml[..., 0:128].transpose(0, 1, 3, 2, 4)
        n = ml[..., 128].transpose(0, 1, 3, 2)
        m = gather('m', which)[..., 0]
        cv = gather('conv', which).transpose(0, 1, 4, 3, 2)
        cv = cv.reshape(cv.shape[0], cv.shape[1], 3, 512)
        gla = gather('gla', which).transpose(0, 1, 3, 2, 4)
        outs += [np.ascontiguousarray(a, dtype=np.float32) for a in (ret, c, n, m, cv, gla)]
    return tuple(outs)
```

```python
import numpy as np
import concourse.bass as bass
import concourse.mybir as mybir
from concourse.bass_utils import run_bass_kernel_spmd

F32 = mybir.dt.float32
BF16 = mybir.dt.bfloat16
ALU = mybir.AluOpType
AF = mybir.ActivationFunctionType

D = 1024
KC = 8
SEQ = 2048
NMETA = 16
DSEQ = 64
DEPTH = 2
DFF = 2816
EPS = 1e-6
OFF = dict(rq=0, rk=256, rv=512, rg=1024, mx=1536, mz=2048, mi=2560, mf=2564, gq=2568, gk=2824,
           gv=3080, gr=3592, ga=4104, zr=4120, zm=5144, zg=6168)
DIN = 7192
NPOS = NMETA + SEQ + DSEQ
LENS = (16, 64, 128)
GAM = [1.0 - 2.0 ** (-5.0 - h) for h in range(4)]

C_ESC, C_QSC = 3116, 3117
C_ID, C_MASK, C_DR, C_GQ, C_KS, C_SEL, C_RM, C_RA, C_RMS, C_RAS, C_ONE, C_END = (
    0, 128, 256, 768, 1280, 1292, 1804, 2316, 2828, 2908, 2988, 3120)
V_N1, V_N2, V_CW, V_CB, V_SK, V_BA, V_BI, V_BF, V_NF, V_END = 0, 8, 16, 32, 36, 40, 44, 45, 46, 64


def job_parts(kind, idx):
    h = idx
    W = 'w_in'
    if kind == 'ret':
        return [([(W, OFF['rq'] + 64 * h, 64), ('w_rs', 64 * h, 64)], 0, 8, 128),
                ([(W, OFF['rk'] + 64 * h, 64), ('w_rs', 256 + 64 * h, 64)], 0, 8, 128),
                ([(W, OFF['rg'] + 128 * h, 128)], 0, 8, 128)]
    if kind == 'retv':
        return [([(W, OFF['rv'], 512)], 0, 8, 128)]
    if kind == 'glav':
        return [([(W, OFF['gv'], 512)], 0, 8, 128)]
    if kind == 'gla':
        return [([(W, OFF['gq'] + 64 * h, 64), (W, OFF['gk'] + 64 * h, 64)], 0, 8, 128),
                ([(W, OFF['gr'] + 128 * h, 128)], 0, 8, 128)]
    if kind == 'ml':
        return [([(W, OFF['mx'] + 128 * h, 128)], 0, 8, 128), ([(W, OFF['mz'] + 128 * h, 128)], 0, 8, 128),
                ([('w_mq', 0, 128)], 128 * h, 1, 128), ([('w_mk', 0, 128)], 128 * h, 1, 128),
                ([('w_mv', 0, 128)], 128 * h, 1, 128)]
    if kind == 'merge':
        ft = idx
        return [([(W, OFF['zr'] + 128 * ft, 128)], 0, 8, 128), ([(W, OFF['zm'] + 128 * ft, 128)], 0, 8, 128),
                ([(W, OFF['zg'] + 128 * ft, 128)], 0, 8, 128), ([('w_br0', 128 * ft, 128)], 0, 4, 128),
                ([('w_br1', 128 * ft, 128)], 0, 4, 128), ([('w_br2', 128 * ft, 128)], 0, 4, 128)]
    if kind == 'mergeb':
        ft, b = idx // 3, idx % 3
        zoff = (OFF['zr'], OFF['zm'], OFF['zg'])[b]
        return [([(W, zoff + 128 * ft, 128)], 0, 8, 128), ([('w_br%d' % b, 128 * ft, 128)], 0, 4, 128)]
    if kind == 'out':
        return [([('w_out', 128 * idx, 128)], 0, 8, 128)]
    if kind == 'ffi':
        return [([('w_fi', 128 * idx, 128)], 0, 8, 128), ([('w_fi', DFF + 128 * idx, 128)], 0, 8, 128)]
    if kind == 'ffo':
        half, ft = idx // 8, idx % 8
        return [([('w_fo', 128 * ft, 128)], half * 1408, 11, 128)]
    if kind == 'gaw':
        return [([(W, OFF['ga'], 16)], 0, 8, 128)]
    if kind == 'gwt':
        return [([(W, OFF['mi'], 8)], 0, 8, 128)]
    if kind == 'wa2':
        cols = []
        for hh in range(4):
            cols += [('w_a2', 64 * hh, 64), ('w_a2', 64 * hh, 64)]
        return [(cols, 0, 1, 16)]
    raise KeyError(kind)


def part_ncols(part):
    return sum(c[2] for c in part[0])


JOB_ORDER = ([('retv', 0), ('glav', 0)] + [('ret', h) for h in range(4)] + [('gwt', 0)] + [('ml', h) for h in range(4)] + [('gaw', 0), ('wa2', 0)]
             + [('gla', h) for h in range(4)] + [('mergeb', f) for f in range(24)] + [('out', f) for f in range(8)]
             + [('ffi', j) for j in range(22)] + [('ffo', i) for i in range(16)])
JOB_OFF = {}
_o = 0
for _k in JOB_ORDER:
    _t = sum(pt[2] * part_ncols(pt) for pt in job_parts(*_k))
    JOB_OFF[_k] = (_o, _t)
    _o += _t
WTOT = _o


def pack_weights(srcs):
    wp = np.zeros((DEPTH, 128, WTOT), np.float32)
    for key in JOB_ORDER:
        off, _ = JOB_OFF[key]
        for part in job_parts(*key):
            cols, r0, nk, npart = part
            ncols = part_ncols(part)
            blk = np.concatenate([srcs[sn][:, r0:r0 + nk * npart, c0:c0 + nc_] for (sn, c0, nc_) in cols], axis=2)
            blk = blk.reshape(DEPTH, nk, npart, ncols).transpose(0, 2, 1, 3)
            wp[:, 0:npart, off:off + nk * ncols] = blk.reshape(DEPTH, npart, nk * ncols)
            off += nk * ncols
    return wp


class Rg:
    __slots__ = ("w", "r", "excl")

    def __init__(self, excl=False):
        self.w = None
        self.r = {}
        self.excl = excl


class Sched:
    def __init__(self, nc):
        self.nc = nc
        self.eng = {'pe': nc.tensor, 'act': nc.scalar, 'dve': nc.vector, 'pool': nc.gpsimd, 'sp': nc.sync}
        self.ops = []
        self.cnt = {}
        self.known = {e: {} for e in self.eng}
        self.clock = {}
        self.waited = set()
        self.slots = {'sp': ['dsp%d' % i for i in range(8)], 'pool': ['dpl%d' % i for i in range(6)]}
        self.rr = {'sp': 0, 'pool': 0}
        self.trail = False
        self.mix = False
        self.dummy_fn = None
        self.dummy = nc.alloc_sbuf_tensor("sched_dummy", [1, 16], F32)

    def _deps(self, reads, writes):
        deps = []
        for r in reads:
            if r.w is not None:
                deps.append((r.w[0], r.w[1], True))
            if r.excl:
                for e, i in r.r.items():
                    deps.append((e, i, False))
        for w in writes:
            if w.w is not None:
                deps.append((w.w[0], w.w[1], False))
            for e, i in w.r.items():
                deps.append((e, i, False))
        return deps

    def _resolve(self, E, issuer, deps):
        kn = self.known[issuer]
        need = {}
        for (Fe, i, raw) in deps:
            if Fe == E and (E == 'pe' or not raw):
                continue
            if kn.get(Fe, -1) >= i:
                continue
            if need.get(Fe, -1) < i:
                need[Fe] = i
        waits = []
        for Fe, i in need.items():
            if kn.get(Fe, -1) >= i:
                continue
            waits.append((Fe, i))
            self.waited.add((Fe, i))
            for G, j in self.clock[(Fe, i)].items():
                if kn.get(G, -1) < j:
                    kn[G] = j
            if kn.get(Fe, -1) < i:
                kn[Fe] = i
        return waits

    def _mark(self, key, reads, writes):
        for r in reads:
            if r.r.get(key[0], -1) < key[1]:
                r.r[key[0]] = key[1]
        for w in writes:
            w.w = key
            w.r = {}

    def op(self, E, fn, reads=(), writes=()):
        n = self.cnt.get(E, 0)
        self.cnt[E] = n + 1
        lhs = reads[0].w[0] if (len(reads) > 0 and reads[0].w is not None) else None
        waits = self._resolve(E, E, self._deps(reads, writes))
        waits.sort(key=lambda w: 1 if w[0] == lhs else 0)
        self.clock[(E, n)] = dict(self.known[E])
        self.ops.append(('c', E, n, fn, waits, self.mix))
        self._mark((E, n), reads, writes)

    def dma(self, issuer, out, in_, reads=(), writes=(), **kw):
        sl = self.slots[issuer]
        Dq = sl[self.rr[issuer] % len(sl)]
        self.rr[issuer] += 1
        n = self.cnt.get(Dq, 0)
        self.cnt[Dq] = n + 1
        deps = self._deps(reads, writes)
        if n > 0:
            deps.append((Dq, n - 1, True))
        waits = self._resolve(Dq, issuer, deps)
        self.clock[(Dq, n)] = dict(self.known[issuer])
        self.ops.append(('d', issuer, Dq, n, out, in_, kw, waits))
        self._mark((Dq, n), reads, writes)

    def emit(self):
        nc = self.nc
        val = {}
        c = {}
        for o in self.ops:
            if o[0] == 'c':
                E, n = o[1], o[2]
                if (E, n) in self.waited:
                    c[E] = c.get(E, 0) + 1
                val[(E, n)] = c.get(E, 0)
        names = list(self.eng) + [s for v in self.slots.values() for s in v]
        sems = {}
        for nm in names:
            sems[nm] = nc.semaphore("sem_" + nm).__enter__()

        def v(Fe, i):
            if Fe in self.eng:
                return val[(Fe, i)]
            return 16 * (i + 1)

        for o in self.ops:
            if o[0] == 'c':
                _, E, n, fn, waits, mixf = o
                h = self.eng[E]
                if E == 'pe' and mixf and self.dummy_fn is not None and any(Fe in ('dve', 'act') for (Fe, i) in waits):
                    for _ in range(NDUMMY):
                        self.dummy_fn()
                for (Fe, i) in waits:
                    h.wait_ge(sems[Fe], v(Fe, i))
                ins = fn()
                if (E, n) in self.waited:
                    if E == 'dve' and self.trail:
                        ins = nc.vector.memset(self.dummy[0:1, 0:1], 0.0)
                    elif E == 'act' and self.trail:
                        ins = nc.scalar.copy(self.dummy[0:1, 2:3], self.dummy[0:1, 1:2])
                    ins.then_inc(sems[E], 1)
            else:
                _, issuer, Dq, n, out, in_, kw, waits = o
                h = self.eng[issuer]
                for (Fe, i) in waits:
                    h.wait_ge(sems[Fe], v(Fe, i))
                h.dma_start(out=out, in_=in_, **kw).then_inc(sems[Dq], 16)
        for Dq in self.slots['sp'] + self.slots['pool']:
            if self.cnt.get(Dq, 0) > 0:
                nc.sync.wait_ge(sems[Dq], 16 * self.cnt[Dq])
        self.max_sem = dict(c)


class Pool:
    def __init__(self, nc, name, n, shape, dt):
        self.bufs = [nc.alloc_sbuf_tensor("%s%d" % (name, i), shape, dt) for i in range(n)]
        self.rg = [Rg() for _ in range(n)]
        self.i = 0

    def get(self):
        k = self.i % len(self.bufs)
        self.i += 1
        return self.bufs[k], self.rg[k]


def make_groups():
    def fr(seq, i, pre=None, post=None):
        chunks = []
        import os
        CH = int(os.environ.get('MK_CH', '128'))
        for j in range(512 // CH):
            chunks.append(dict(off=j * CH, c=CH, stream='main', pos0=NMETA + i * 512 + j * CH, li=LENS.index(CH),
                               pre=None, post=None))
        chunks[0]['pre'] = pre
        chunks[-1]['post'] = post
        return dict(kind='fr', n=512, seq=seq, tok0=i * 512, chunks=chunks,
                    segs=[dict(off=0, n=512, stream='main')])
    sp = dict(kind='sp', n=80, chunks=[
        dict(off=0, c=16, stream='main', pos0=0, li=0, pre=('zero',), post=('save', 'scr', 1)),
        dict(off=16, c=64, stream='smp', pos0=NMETA + SEQ, li=1, pre=('load', 'in', 0), post=('save', 'out', 2))],
        segs=[dict(off=0, n=16, stream='main'), dict(off=16, n=64, stream='smp')])
    ST = ('save', 'scr', 0)
    LD = ('load', 'scr', 0)
    groups = [
        [sp, fr(0, 0, None, ST)],
        [fr(0, 1, LD, None), fr(0, 2, None, ST)],
        [fr(0, 3, LD, ('save', 'out', 0)), fr(1, 0, ('load', 'scr', 1), ST)],
        [fr(1, 1, LD, None), fr(1, 2, None, ST)],
        [fr(1, 3, LD, ('save', 'out', 1))],
    ]
    for g in groups:
        col = 0
        gch = 0
        for T in g:
            T['col'] = col
            col += T['n']
            for ch in T['chunks']:
                ch['g'] = gch
                gch += 1
    return groups


def build_program(ngroups=5, stages=('ret', 'ml', 'gla', 'merge', 'ffn'), depth=DEPTH):
    nc = bass.Bass("TRN2", target_bir_lowering=False)
    S = Sched(nc)

    def din(name, shape):
        return nc.dram_tensor(name, list(shape), F32, kind="ExternalInput").ap()

    def dout(name, shape):
        return nc.dram_tensor(name, list(shape), F32, kind="ExternalOutput").ap()

    def dscr(name, shape):
        return nc.dram_tensor(name, list(shape), F32, kind="Internal").ap()

    xp = din("xp", [2, SEQ, D])
    xs = din("xs", [DSEQ, D])
    meta = din("meta", [NMETA, D])
    KSH = dict(ret=[64, 4, 128], gla=[64, 4, 128], ml=[128, 4, 129], m=[4, 1], conv=[128, 4, 3])
    st_in = {k: din("sti_" + k, [1, 2] + s) for k, s in KSH.items()}
    st_out = {k: dout("sto_" + k, [3, 2] + s) for k, s in KSH.items()}
    st_scr = {k: dscr("sts_" + k, [2, 2] + s) for k, s in KSH.items()}
    stt_ = dict(scr=st_scr, out=st_out)
    stt_['in'] = st_in
    scr_rg = {}

    def drg(kind, idx, l, k, h):
        key = (kind, idx, l, k, h)
        if key not in scr_rg:
            scr_rg[key] = Rg()
        return scr_rg[key]

    wpack_d = din("wpack", [DEPTH, 128, WTOT])
    w_in = w_rs = w_mq = w_mk = w_mv = w_a2 = w_out = w_fi = w_fo = None
    w_br = [None, None, None]
    cst_d = din("cst", [128, C_END])
    vec_d = din("vecs", [128, DEPTH, V_END])
    cs_d = din("cs", [64, 2, NPOS])
    yp = dout("yp", [2, SEQ, D])
    ys = dout("ys", [DSEQ, D])

    sb = nc.alloc_sbuf_tensor
    NT = 1024
    DBG = 'dbg' in stages
    dbg_d = dout("dbg", [128, 4096]) if DBG else None
    dbg_state = {'col': 0, 'names': []}

    def dump(name, src, np_, ncol, R):
        if not DBG or dbg_state['col'] + ncol > 4096:
            return
        t, tr_ = tmpf.get()
        cp(t[0:np_, 0:ncol], src, R, [tr_])
        c0 = dbg_state['col']
        S.dma('sp', dbg_d[0:np_, c0:c0 + ncol], t[0:np_, 0:ncol], [tr_], ())
        dbg_state['names'].append((name, np_, c0, ncol))
        dbg_state['col'] += ncol
    build_program.dbg_state = dbg_state
    xT = sb("xT", [128, KC, NT], F32)
    hT = sb("hT", [128, KC, NT], BF16)
    ob = sb("ob", [128, 12, NT], BF16)
    mixb = sb("mixb", [128, KC, NT], BF16)
    cst = sb("cst_sb", [128, C_END], F32)
    vec = sb("vec_sb", [128, DEPTH, V_END], F32)
    nvec = sb("nvec", [128, DEPTH, V_END], F32)
    identb = sb("identb", [128, 128], BF16)
    onesb = sb("onesb", [128, 128], BF16)
    R_cst, R_vec, R_nvec, R_idb, R_oneb = Rg(), Rg(), Rg(), Rg(), Rg()
    Rx = [[Rg() for _ in range(3)] for _ in range(KC)]
    Rh = [[Rg() for _ in range(3)] for _ in range(KC)]
    Ro = [[Rg() for _ in range(3)] for _ in range(12)]
    Rm = [[Rg() for _ in range(3)] for _ in range(KC)]

    STB = {}
    STR = {}
    for s in ('main', 'smp'):
        STB[s] = dict(ret=sb("st_ret_" + s, [128, 4, 128], F32), gla=sb("st_gla_" + s, [128, 4, 128], F32),
                      ml=sb("st_ml_" + s, [128, 4, 144], F32), m=sb("st_m_" + s, [4, 16], F32),
                      conv=sb("st_conv_" + s, [128, 4, 4], F32))
        STR[s] = {k: [Rg() for _ in range(4)] for k in ('ret', 'gla', 'ml', 'conv')}
        STR[s]['m'] = [Rg()]
    s16 = Pool(nc, "s16", 3, [128, 160], BF16)

    WSL = 4608
    warena = sb("warena", [128, 3 * WSL], BF16)
    wlive = []
    DENSE_SLOTS = [(i * WSL, (i + 1) * WSL) for i in range(3)]
    A_SLOTS = [(0, 4096), (4096, 8192)]
    B_SLOTS = [(8192, 8192 + 2432), (8192 + 2432, 8192 + 4864)]
    dctr = [0, 0]
    gaw = sb("gaw", [128, KC, 16], BF16)
    wa2 = sb("wa2", [16, 512], BF16)
    gwt = sb("gwt", [128, KC, 8], BF16)
    R_gaw, R_wa2, R_gwt = Rg(), Rg(), Rg()
    gaT = sb("gaT", [16, NT], BF16)
    R_gaT = [Rg() for _ in range(3)]
    tmpf = Pool(nc, "tmpf", 7, [128, 528], F32)
    gatep = Pool(nc, "gatep", 2, [128, 512], F32)
    tmpb = Pool(nc, "tmpb", 8, [128, 512], BF16)
    mxfp = Pool(nc, "mxfp", 2, [128, 608], F32)
    ctp = Pool(nc, "ctp", 2, [128, 512], F32)
    cfp = Pool(nc, "cfp", 8, [128, 144], F32)
    cbp = Pool(nc, "cbp", 10, [128, 160], BF16)
    vaugp = Pool(nc, "vaugp", 3, [128, 160], BF16)
    csp = Pool(nc, "csp", 12, [128, 16], F32)
    scr8k = sb("scr8k", [128, 2048], F32)
    R_s0, R_s1 = Rg(), Rg()
    rows = scr8k[0:4, :].rearrange("p (k t) -> p k t", t=512)
    colsb = sb("colsb", [128, 8, 16], F32)
    R_cols = [Rg() for _ in range(8)]
    scb = sb("scb", [128, 2, 4, 8], F32)
    R_scb = [[Rg() for _ in range(4)] for _ in range(2)]
    sct = sb("sct", [4, 16], F32)
    R_sct = Rg()
    XINS = [(scr8k[:, 0:1024], R_s0), (scr8k[:, 1024:2048], R_s1)]
    xctr = [0]
    tab = scr8k[0:64, 1024:2048].rearrange("p (k t) -> p k t", t=512)
    tab2 = scr8k[:, 1024:1536]
    R_tab = R_s1
    NPF = 6
    psf = [nc.psum_tensor("psf%d" % i, [128, 512], F32).__enter__() for i in range(NPF)]
    psb = [nc.psum_tensor("psb%d" % i, [128, 1024], BF16).__enter__() for i in range(2)]
    psfr = [Rg(True) for _ in range(NPF)]
    psbr = [Rg(True) for _ in range(2)]
    pctr = [0, 0]

    def PF():
        k = pctr[0] % NPF
        pctr[0] += 1
        return psf[k], psfr[k]

    def PB():
        k = pctr[1] % 2
        pctr[1] += 1
        return psb[k], psbr[k]

    def mm(out, lhsT, rhs, start, stop, R, W):
        S.op('pe', lambda: nc.tensor.matmul(out, lhsT, rhs, start=start, stop=stop), R, W)

    def tr(out, in_, ident, R, W):
        S.op('pe', lambda: nc.tensor.transpose(out, in_, ident), R, W)

    def act(out, in_, func, R, W, bias=None, scale=None, accum=None):
        kw = {}
        if bias is not None:
            kw['bias'] = bias
        if scale is not None:
            kw['scale'] = scale
        if accum is not None:
            kw['accum_out'] = accum
        S.op('act', lambda: nc.scalar.activation(out=out, in_=in_, func=func, **kw), R, W)

    def tt(out, a, b, op, R, W, eng='dve'):
        h = S.eng[eng]
        S.op(eng, lambda: h.tensor_tensor(out, a, b, op), R, W)

    def ts(out, a, s1, s2, op0, op1, R, W, eng='dve'):
        h = S.eng[eng]
        if op1 is None:
            S.op(eng, lambda: h.tensor_scalar(out, a, s1, None, op0), R, W)
        else:
            S.op(eng, lambda: h.tensor_scalar(out, a, s1, s2, op0, op1), R, W)

    def stt(out, in0, scalar, in1, op0, op1, R, W, eng='dve'):
        h = S.eng[eng]
        S.op(eng, lambda: h.scalar_tensor_tensor(out, in0, scalar, in1, op0, op1), R, W)

    def amul(out, in_, sc, R, W):
        S.op('act', lambda: nc.scalar.mul(out, in_, sc), R, W)

    def cp(out, in_, R, W, eng='dve'):
        if eng == 'act':
            S.op('act', lambda: nc.scalar.copy(out, in_), R, W)
        else:
            h = S.eng[eng]
            S.op(eng, lambda: h.tensor_copy(out, in_), R, W)

    def mset(ap, v, W, eng='dve'):
        h = S.eng[eng]
        S.op(eng, lambda: h.memset(ap, v), (), W)

    def scan(out, d0, d1, init, op0, op1, R, W):
        S.op('dve', lambda: nc.vector.tensor_tensor_scan(out, d0, d1, init, op0, op1), R, W)

    def recip(out, in_, R, W):
        S.op('dve', lambda: nc.vector.reciprocal(out, in_), R, W)

    S.dma('sp', cst[:], cst_d, (), [R_cst])
    S.dma('sp', vec[:], vec_d, (), [R_vec])
    S.dma('pool', identb[:], cst_d[:, C_ID:C_ID + 128], (), [R_idb])
    S.dma('pool', onesb[:], cst_d[:, C_ONE:C_ONE + 128], (), [R_oneb])
    ts(nvec[:], vec[:], -1.0, None, ALU.mult, None, [R_vec], [R_nvec])
    epsb = sb("epsb", [128, 16], F32)
    R_eps = Rg()
    mset(epsb[:], EPS, [R_eps])
    foldb = sb("foldb", [128, 64], BF16)
    selb = sb("selb", [128, 64], BF16)
    R_fold, R_sel = Rg(), Rg()
    S.dma('pool', foldb[0:64, :], cst_d[0:64, C_ID:C_ID + 64], (), [R_fold])
    S.dma('pool', foldb[64:128, :], cst_d[64:128, C_ID + 64:C_ID + 128], [R_fold], [R_fold])
    S.dma('pool', selb[:], cst_d[:, C_ID + 64:C_ID + 128], (), [R_sel])
    identf = cst[:, C_ID:C_ID + 128]
    maskT = cst[:, C_MASK:C_MASK + 128]
    for k in range(3):
        mset(vaugp.bufs[k][:, 128:129], 1.0, [vaugp.rg[k]])

    def wjobp(kind, idx, l, slot=None):
        off, tot = JOB_OFF[(kind, idx)]
        if slot is None:
            if tot <= WSL // 2:
                q = dctr[1] % 6
                dctr[1] += 1
                slot = (q * (WSL // 2), (q + 1) * (WSL // 2))
            else:
                slot = DENSE_SLOTS[dctr[0] % 3]
                dctr[0] += 1
        s0, s1 = slot
        assert tot <= s1 - s0
        old_rgs = [rg_ for (a, b, rg_) in wlive if a < s0 + tot and b > s0]
        wlive[:] = [(a, b, rg_) for (a, b, rg_) in wlive if not (a < s0 + tot and b > s0)]
        rg = Rg()
        wlive.append((s0, s0 + tot, rg))
        S.dma('pool', warena[:, s0:s0 + tot], wpack_d[l][:, off:off + tot], (), [rg] + old_rgs)
        res = []
        o = s0
        for part in job_parts(kind, idx):
            nk, ncols = part[2], part_ncols(part)
            res.append((warena[:, o:o + nk * ncols].rearrange("p (k n) -> p k n", n=ncols), rg))
            o += nk * ncols
        return res

    def wsmall(dst_ap, kind, l, rg, npart=128):
        off, tot = JOB_OFF[(kind, 0)]
        S.dma('pool', dst_ap, wpack_d[l][0:npart, off:off + tot], (), [rg])

    def wjob(parts):
        raise RuntimeError('unused')

    def st_action(a, l, stream, kind, h):
        if a is None:
            return
        buf = STB[stream][kind]
        rg = STR[stream][kind][h if kind != 'm' else 0]
        np_ = 64 if kind in ('ret', 'gla') else (4 if kind == 'm' else 128)
        wd = dict(ret=128, gla=128, ml=129, conv=3)
        sv = buf[0:np_, h, 0:wd[kind]] if kind != 'm' else buf[0:4, 0:1]
        if a[0] == 'zero':
            mset(sv, 0.0, [rg])
            return
        which, idx = a[1], a[2]
        dt_ = stt_[which][kind]
        dv = dt_[idx, l][:, h, :] if kind != 'm' else dt_[idx, l]
        drg_ = drg(which, idx, l, kind, h)
        if a[0] == 'load':
            S.dma('sp', sv, dv, [drg_] if which == 'scr' else (), [rg])
        else:
            S.dma('sp', dv, sv, [rg], [drg_])

    def refresh16(stream, kind, h):
        buf = STB[stream][kind]
        np_ = 64 if kind in ('ret', 'gla') else 128
        w = 128 if kind in ('ret', 'gla') else 129
        t16, r16 = s16.get()
        cp(t16[0:np_, 0:w], buf[0:np_, h, 0:w], [STR[stream][kind][h]], [r16], eng='act')
        return t16, r16

    def rmsnorm(tiles, l, vcol, dst_bf=True):
        for ti, T in enumerate(tiles):
            n = T['n']
            cs_ = slice(T['col'], T['col'] + n)
            ps, pr = PF()
            for kc in range(KC):
                sq, sr = tmpb.get()
                act(sq[:, :n], xT[:, kc, cs_], AF.Square, [Rx[kc][ti]], [sr])
                mm(ps[:, :n], onesb[:], sq[:, :n], kc == 0, kc == KC - 1, [R_oneb, sr], [pr])
            lnv, lr = tmpf.get()
            act(lnv[:, :n], ps[:, :n], AF.Ln, [pr, R_eps], [lr], bias=epsb[:, 0:1], scale=1.0 / D)
            rs, rr = gatep.get()
            act(rs[:, :n], lnv[:, :n], AF.Exp, [lr], [rr], scale=-0.5)
            if dst_bf:
                for kc in range(KC):
                    stt(hT[:, kc, cs_], xT[:, kc, cs_], vec[:, l, vcol + kc:vcol + kc + 1], rs[:, :n],
                        ALU.mult, ALU.mult, [Rx[kc][ti], R_vec, rr], [Rh[kc][ti]])
            else:
                yield_final(T, ti, rs, rr, n, cs_, l, vcol)

    def yield_final(T, ti, rs, rr, n, cs_, l, vcol):
        nb = (n + 127) // 128
        for b in range(nb):
            c0 = b * 128
            cn = min(128, n - c0)
            pa, par = PF()
            pb, pbr = PF()
            for kc in range(KC):
                yt, yr = tmpf.get()
                stt(yt[:, :cn], xT[:, kc, T['col'] + c0:T['col'] + c0 + cn], vec[:, l, vcol + kc:vcol + kc + 1],
                    rs[:, c0:c0 + cn], ALU.mult, ALU.mult, [Rx[kc][ti], R_vec, rr], [yr])
                pp, ppr = (pa, par) if kc < 4 else (pb, pbr)
                tr(pp[0:cn, (kc % 4) * 128:(kc % 4) * 128 + 128], yt[:, :cn], identf, [yr, R_cst], [ppr])
            xin, R_xin = XINS[xctr[0] % 2]
            xctr[0] += 1
            cp(xin[0:cn, 0:512], pa[0:cn, :], [par], [R_xin])
            cp(xin[0:cn, 512:1024], pb[0:cn, :], [pbr], [R_xin], eng='act')
            if T['kind'] == 'sp':
                S.dma('sp', ys[:, :], xin[16:80, :], [R_xin], ())
            else:
                r0 = T['tok0'] + c0
                S.dma('sp', yp[T['seq'], r0:r0 + cn, :], xin[0:cn, :], [R_xin], ())

    def load_x(tiles):
        for ti, T in enumerate(tiles):
            n = T['n']
            nb = (n + 127) // 128
            for b in range(nb):
                c0 = b * 128
                cn = min(128, n - c0)
                xin, R_xin = XINS[xctr[0] % 2]
                xctr[0] += 1
                if T['kind'] == 'sp':
                    S.dma('sp', xin[0:16, :], meta, (), [R_xin])
                    S.dma('sp', xin[16:80, :], xs, [R_xin], [R_xin])
                    xin_r = [R_xin]
                else:
                    r0 = T['tok0'] + c0
                    S.dma('sp', xin[0:cn, :], xp[T['seq'], r0:r0 + cn, :], (), [R_xin])
                    xin_r = [R_xin]
                for half in range(2):
                    pp, ppr = PF()
                    for q in range(4):
                        kc = half * 4 + q
                        tr(pp[:, q * 128:q * 128 + cn], xin[0:cn, kc * 128:(kc + 1) * 128], identf[0:cn, 0:cn],
                           xin_r + [R_cst], [ppr])
                    dst = xT[:, half * 4:half * 4 + 4, T['col'] + c0:T['col'] + c0 + cn]
                    src = pp[:, :].rearrange("p (q t) -> p q t", t=128)[:, :, 0:cn]
                    cp(dst, src, [ppr], [Rx[half * 4 + q][ti] for q in range(4)], eng='dve' if half == 0 else 'act')

    def headnorm_gate(src, srcR, c, gate_ap, gateR, oi, ti, colsl, skip=None):
        junk, jr = cfp.get()
        ssb, ssr = csp.get()
        act(junk[0:c, 0:128], src, AF.Square, srcR, [jr, ssr], accum=ssb[0:c, 0:1])
        act(ssb[0:c, 1:2], ssb[0:c, 0:1], AF.Ln, [ssr, R_eps], [ssr], bias=epsb[0:c, 0:1], scale=1.0 / 128)
        act(ssb[0:c, 2:3], ssb[0:c, 1:2], AF.Exp, [ssr], [ssr], scale=-0.5)
        on, onr = cbp.get()
        ts(on[0:c, 0:128], src, ssb[0:c, 2:3], None, ALU.mult, None, srcR + [ssr], [onr])
        pt, ptr = PB()
        tr(pt[:, 0:c], on[0:c, 0:128], identb[0:c, 0:c], [onr, R_idb], [ptr])
        if skip is None:
            tt(ob[:, oi, colsl], pt[:, 0:c], gate_ap, ALU.mult, [ptr] + gateR, [Ro[oi][ti]])
        else:
            cT_ap, cR, sk_ap = skip
            t2, t2r = cfp.get()
            stt(t2[:, 0:c], cT_ap, sk_ap, pt[:, 0:c], ALU.mult, ALU.add, [ptr, R_vec] + cR, [t2r])
            tt(ob[:, oi, colsl], t2[:, 0:c], gate_ap, ALU.mult, [t2r] + gateR, [Ro[oi][ti]])

    def proj_fm(ps, pr, m0, m1, wview, wr, ti, T):
        n = T['n']
        cs_ = slice(T['col'], T['col'] + n)
        for kc in range(KC):
            mm(ps[m0:m1, :n], wview[:, kc, :], hT[:, kc, cs_], kc == 0, kc == KC - 1, [wr, Rh[kc][ti]], [pr])

    def vtok(ch, T, ti, wview, wr, dst, dstr, ncol=128):
        c = ch['c']
        c0 = T['col'] + ch['off']
        ps, pr = PF()
        for kc in range(KC):
            mm(ps[0:c, 0:ncol], hT[:, kc, c0:c0 + c], wview[:, kc, :], kc == 0, kc == KC - 1, [Rh[kc][ti], wr], [pr])
        cp(dst[0:c, 0:ncol], ps[0:c, 0:ncol], [pr], [dstr], eng='act')

    def load_tab2(T):
        n = T['n']
        segs = [(0, 16, 0), (16, 64, NMETA + SEQ)] if T['kind'] == 'sp' else [(0, n, NMETA + T['tok0'])]
        first = True
        for (c0, ln, p0) in segs:
            for half in range(2):
                S.dma('sp', tab2[64 * half:64 * half + 64, c0:c0 + ln], cs_d[:, half, p0:p0 + ln],
                      () if first else [R_tab], [R_tab])
                first = False

    def load_tab(T):
        n = T['n']
        if T['kind'] == 'sp':
            S.dma('sp', tab[:, :, 0:16], cs_d[:, :, 0:16], (), [R_tab])
            S.dma('sp', tab[:, :, 16:80], cs_d[:, :, NMETA + SEQ:NMETA + SEQ + 64], [R_tab], [R_tab])
        else:
            p0 = NMETA + T['tok0']
            S.dma('sp', tab[:, :, 0:n], cs_d[:, :, p0:p0 + n], (), [R_tab])

    def branch_ret(l, tiles):
        for h in range(4):
            Wl = None
            parts = wjob([(Wl[:, OFF['rq'] + 64 * h:OFF['rq'] + 64 * h + 64], KC, 64),
                          (w_rs[l][:, 64 * h:64 * h + 64], KC, 64),
                          (Wl[:, OFF['rk'] + 64 * h:OFF['rk'] + 64 * h + 64], KC, 64),
                          (w_rs[l][:, 256 + 64 * h:256 + 64 * h + 64], KC, 64),
                          (Wl[:, OFF['rv'] + 128 * h:OFF['rv'] + 128 * h + 128], KC, 128),
                          (Wl[:, OFF['rg'] + 128 * h:OFF['rg'] + 128 * h + 128], KC, 128)])
            (wq, rq), (wqs, rqs), (wk, rk), (wks, rks), (wv, rv), (wg, rgt) = parts
            for ti, T in enumerate(tiles):
                n = T['n']
                load_tab(T)
                pg, pgr = PF()
                proj_fm(pg, pgr, 0, 128, wg, rgt, ti, T)
                gate, gr_ = gatep.get()
                act(gate[:, :n], pg[:, :n], AF.Silu, [pgr], [gr_])
                outs = []
                for (wa, ra, wb, rb, isq) in ((wq, rq, wqs, rqs, True), (wk, rk, wks, rks, False)):
                    pa, par = PF()
                    proj_fm(pa, par, 0, 64, wa, ra, ti, T)
                    pb_, pbr_ = PF()
                    proj_fm(pb_, pbr_, 0, 64, wb, rb, ti, T)
                    t1, t1r = tmpf.get()
                    tt(t1[0:64, :n], pa[0:64, :n], tab[:, 0, 0:n], ALU.mult, [par, R_tab], [t1r])
                    t2, t2r = tmpf.get()
                    tt(t2[0:64, :n], pb_[0:64, :n], tab[:, 1, 0:n], ALU.mult, [pbr_, R_tab], [t2r])
                    ob_, obr_ = tmpb.get()
                    if isq:
                        tt(t1[0:64, :n], t1[0:64, :n], t2[0:64, :n], ALU.add, [t1r, t2r], [t1r])
                        for sg in T['segs']:
                            o0, sn = sg['off'], sg['n']
                            if sn >= 128:
                                nch = sn // 128
                                gq = cst[0:64, C_GQ + 128 * h:C_GQ + 128 * h + 128].unsqueeze(1).to_broadcast([64, nch, 128])
                                tt(ob_[0:64, o0:o0 + sn].rearrange("p (c t) -> p c t", t=128),
                                   t1[0:64, o0:o0 + sn].rearrange("p (c t) -> p c t", t=128), gq, ALU.mult,
                                   [t1r, R_cst], [obr_])
                            else:
                                tt(ob_[0:64, o0:o0 + sn], t1[0:64, o0:o0 + sn],
                                   cst[0:64, C_GQ + 128 * h:C_GQ + 128 * h + sn], ALU.mult, [t1r, R_cst], [obr_])
                    else:
                        tt(ob_[0:64, :n], t1[0:64, :n], t2[0:64, :n], ALU.add, [t1r, t2r], [obr_])
                    outs.append((ob_, obr_))
                (qin, qr), (kr, krr) = outs
                if h == 0 and l == 0 and ti == 1:
                    dump('qin', qin[0:64, 0:512], 64, 512, [qr])
                    dump('kr', kr[0:64, 0:512], 64, 512, [krr])
                    dump('gate', gate[:, 0:512], 128, 512, [gr_])
                for ch in T['chunks']:
                    c, o0, st = ch['c'], ch['off'], ch['stream']
                    st_action(ch['pre'], l, st, 'ret', h)
                    Sb = STB[st]['ret']
                    Sr = STR[st]['ret'][h]
                    t16, r16 = refresh16(st, 'ret', h)
                    psc, pscr = PF()
                    mm(psc[0:c, 0:c], kr[0:64, o0:o0 + c], qin[0:64, o0:o0 + c], True, True, [krr, qr], [pscr])
                    P, Pr = cbp.get()
                    tt(P[0:c, 0:c], psc[0:c, 0:c], cst[0:c, C_DR + 128 * h:C_DR + 128 * h + c], ALU.mult,
                       [pscr, R_cst], [Pr])
                    vt, vr = cbp.get()
                    vtok(ch, T, ti, wv, rv, vt, vr)
                    pk, pkr = PB()
                    tr(pk[0:c, 0:64], kr[0:64, o0:o0 + c], identb[0:64, 0:64], [krr, R_idb], [pkr])
                    kst, kstr = cbp.get()
                    ts(kst[0:c, 0:64], pk[0:c, 0:64], cst[0:c, C_KS + 3 * h + ch['li']:C_KS + 3 * h + ch['li'] + 1], None,
                       ALU.mult, None, [pkr, R_cst], [kstr])
                    po, por = PF()
                    if c <= 64:
                        mm(po[0:c, 0:128], P[0:c, 0:c], vt[0:c, 0:128], True, False, [Pr, vr], [por])
                        mm(po[0:c, 0:128], qin[0:64, o0:o0 + c], t16[0:64, 0:128], False, True, [qr, r16], [por])
                    else:
                        mm(po[0:c, 0:128], qin[0:64, o0:o0 + c], t16[0:64, 0:128], True, False, [qr, r16], [por])
                        mm(po[0:c, 0:128], P[0:c, 0:c], vt[0:c, 0:128], False, True, [Pr, vr], [por])
                    pd, pdr = PF()
                    mm(pd[0:64, 0:128], kst[0:c, 0:64], vt[0:c, 0:128], True, True, [kstr, vr], [pdr])
                    if h == DBG_H and l == 0 and ti == 1 and ch['off'] == DBG_OFF:
                        dump('vt', vt[0:128, 0:128], 128, 128, [vr])
                        dump('kst', kst[0:128, 0:64], 128, 64, [kstr])
                        dump('P', P[0:128, 0:128], 128, 128, [Pr])
                        dump('S16', t16[0:64, 0:128], 64, 128, [r16])
                        dump('po', po[0:128, 0:128], 128, 128, [por])
                        dump('pd', pd[0:64, 0:128], 64, 128, [pdr])
                    stt(Sb[0:64, h, :], Sb[0:64, h, :], float(GAM[h] ** c), pd[0:64, 0:128], ALU.mult, ALU.add,
                        [Sr, pdr], [Sr])
                    headnorm_gate(po[0:c, 0:128], [por], c, gate[:, o0:o0 + c], [gr_], h, ti,
                                  slice(T['col'] + o0, T['col'] + o0 + c))
                    st_action(ch['post'], l, st, 'ret', h)
                if h == DBG_H and l == 0 and ti == 1:
                    dump('ob', ob[:, h, T['col']:T['col'] + 512], 128, 512, [Ro[h][ti]])

    def branch_gla(l, tiles):
        import os
        CUT = int(os.environ.get('MK_GLA_CUT', '9'))
        Wl = None
        S.dma('pool', gaw[:], Wl[:, OFF['ga']:OFF['ga'] + 16].rearrange("(k p) n -> p k n", p=128), (), [R_gaw])
        S.dma('pool', wa2[:], w_a2[l], (), [R_wa2])
        for ti, T in enumerate(tiles):
            n = T['n']
            pg, pgr = PF()
            proj_fm(pg, pgr, 0, 16, gaw, R_gaw, ti, T)
            cp(gaT[0:16, T['col']:T['col'] + n], pg[0:16, :n], [pgr], [R_gaT[ti]])
        for h in range(4):
            parts = wjob([(Wl[:, OFF['gq'] + 64 * h:OFF['gq'] + 64 * h + 64], KC, 64),
                          (Wl[:, OFF['gk'] + 64 * h:OFF['gk'] + 64 * h + 64], KC, 64),
                          (Wl[:, OFF['gv'] + 128 * h:OFF['gv'] + 128 * h + 128], KC, 128),
                          (Wl[:, OFF['gr'] + 128 * h:OFF['gr'] + 128 * h + 128], KC, 128)])
            (wq, rq), (wk, rk), (wv, rv), (wg, rgt) = parts
            if CUT < 1:
                continue
            for ti, T in enumerate(tiles):
                n = T['n']
                cs_ = slice(T['col'], T['col'] + n)
                pg, pgr = PF()
                proj_fm(pg, pgr, 0, 128, wg, rgt, ti, T)
                gate, gr_ = gatep.get()
                act(gate[:, :n], pg[:, :n], AF.Silu, [pgr], [gr_])
                if CUT < 2:
                    continue
                pz, pzr = PF()
                mm(pz[0:64, :n], wa2[0:16, 64 * h:64 * h + 64], gaT[0:16, cs_], True, True, [R_wa2, R_gaT[ti]], [pzr])
                e, er = tmpf.get()
                act(e[0:64, :n], pz[0:64, :n], AF.Exp, [pzr, R_nvec], [er], bias=nvec[0:64, l, V_BA + h:V_BA + h + 1],
                    scale=-1.0)
                sp_, spr = tmpf.get()
                act(sp_[0:64, :n], e[0:64, :n], AF.Ln, [er], [spr], bias=1.0)
                if CUT < 3:
                    continue
                bs, bsr = tmpf.get()
                rm = cst[0:64, C_RMS:C_RMS + 80] if T['kind'] == 'sp' else cst[0:64, C_RM:C_RM + 512]
                scan(bs[0:64, :n], rm, sp_[0:64, :n], 0.0, ALU.mult, ALU.add, [spr, R_cst], [bsr])
                if CUT < 4:
                    continue
                eb, ebr = tmpf.get()
                act(eb[0:64, :n], bs[0:64, :n], AF.Exp, [bsr], [ebr], scale=-1.0 / 16)
                enb, enbr = tmpf.get()
                act(enb[0:64, :n], bs[0:64, :n], AF.Exp, [bsr], [enbr], scale=1.0 / 16)
                pq, pqr = PF()
                proj_fm(pq, pqr, 0, 64, wq, rq, ti, T)
                qin, qr = tmpb.get()
                stt(qin[0:64, :n], pq[0:64, :n], 0.125, eb[0:64, :n], ALU.mult, ALU.mult, [pqr, ebr], [qr])
                pk_, pkr_ = PF()
                proj_fm(pk_, pkr_, 0, 64, wk, rk, ti, T)
                kin, kr_ = tmpb.get()
                tt(kin[0:64, :n], pk_[0:64, :n], enb[0:64, :n], ALU.mult, [pkr_, enbr], [kr_])
                if CUT < 6:
                    continue
                for ch in T['chunks']:
                    c, o0, st = ch['c'], ch['off'], ch['stream']
                    st_action(ch['pre'], l, st, 'gla', h)
                    Sb = STB[st]['gla']
                    Sr = STR[st]['gla'][h]
                    t16, r16 = refresh16(st, 'gla', h)
                    vt, vr = cbp.get()
                    vtok(ch, T, ti, wv, rv, vt, vr)
                    pk, pkr = PB()
                    tr(pk[0:c, 0:64], kin[0:64, o0:o0 + c], identb[0:64, 0:64], [kr_, R_idb], [pkr])
                    kt, ktr = cbp.get()
                    cp(kt[0:c, 0:64], pk[0:c, 0:64], [pkr], [ktr])
                    psc, pscr = PF()
                    mm(psc[0:c, 0:c], kin[0:64, o0:o0 + c], qin[0:64, o0:o0 + c], True, True, [kr_, qr], [pscr])
                    P, Pr = cbp.get()
                    tt(P[0:c, 0:c], psc[0:c, 0:c], maskT[0:c, 0:c], ALU.mult, [pscr, R_cst], [Pr])
                    po, por = PF()
                    if c <= 64:
                        mm(po[0:c, 0:128], P[0:c, 0:c], vt[0:c, 0:128], True, False, [Pr, vr], [por])
                        mm(po[0:c, 0:128], qin[0:64, o0:o0 + c], t16[0:64, 0:128], False, True, [qr, r16], [por])
                    else:
                        mm(po[0:c, 0:128], qin[0:64, o0:o0 + c], t16[0:64, 0:128], True, False, [qr, r16], [por])
                        mm(po[0:c, 0:128], P[0:c, 0:c], vt[0:c, 0:128], False, True, [Pr, vr], [por])
                    pd, pdr = PF()
                    mm(pd[0:64, 0:128], kt[0:c, 0:64], vt[0:c, 0:128], True, True, [ktr, vr], [pdr])
                    tt(Sb[0:64, h, :], Sb[0:64, h, :], pd[0:64, 0:128], ALU.add, [Sr, pdr], [Sr])
                    ts(Sb[0:64, h, :], Sb[0:64, h, :], eb[0:64, o0 + c - 1:o0 + c], None, ALU.mult, None, [Sr, ebr], [Sr])
                    headnorm_gate(po[0:c, 0:128], [por], c, gate[:, o0:o0 + c], [gr_], 8 + h, ti,
                                  slice(T['col'] + o0, T['col'] + o0 + c))
                    st_action(ch['post'], l, st, 'gla', h)

    def ml_prepass(l, tiles):
        branch_ml(l, tiles, CUT=3)

    def branch_ml(l, tiles, CUT=9):
        Wl = None
        wsmall(gwt[:].rearrange("p k n -> p (k n)"), 'gwt', l, R_gwt)
        for ti, T in enumerate(tiles):
            n = T['n']
            cs_ = slice(T['col'], T['col'] + n)
            pi_, pir = PF()
            for kc in range(KC):
                mm(pi_[0:4, :n], gwt[:, kc, 0:4], hT[:, kc, cs_], kc == 0, kc == KC - 1, [R_gwt, Rh[kc][ti]], [pir])
            pf_, pfr = PF()
            for kc in range(KC):
                mm(pf_[0:4, :n], gwt[:, kc, 4:8], hT[:, kc, cs_], kc == 0, kc == KC - 1, [R_gwt, Rh[kc][ti]], [pfr])
            a_, ar = tmpf.get()
            ts(a_[0:4, :n], pi_[0:4, :n], vec[0:4, l, V_BI:V_BI + 1], None, ALU.add, None, [pir, R_vec], [ar])
            e, er = tmpf.get()
            act(e[0:4, :n], pf_[0:4, :n], AF.Exp, [pfr, R_nvec], [er], bias=nvec[0:4, l, V_BF:V_BF + 1], scale=-1.0)
            sp_, spr = tmpf.get()
            act(sp_[0:4, :n], e[0:4, :n], AF.Ln, [er], [spr], bias=1.0)
            bs, bsr = tmpf.get()
            sp_tile = T['kind'] == 'sp'
            rm = cst[0:4, C_RMS:C_RMS + 80] if sp_tile else cst[0:4, C_RM:C_RM + 512]
            ra = cst[0:4, C_RAS:C_RAS + 80] if sp_tile else cst[0:4, C_RA:C_RA + 512]
            scan(bs[0:4, :n], rm, sp_[0:4, :n], 0.0, ALU.mult, ALU.add, [spr, R_cst], [bsr])
            tt(a_[0:4, :n], a_[0:4, :n], bs[0:4, :n], ALU.add, [ar, bsr], [ar])
            g_, gr2 = tmpf.get()
            scan(g_[0:4, :n], ra, a_[0:4, :n], -1e30, ALU.add, ALU.max, [ar, R_cst], [gr2])
            act(rows[0:4, 3, :n], a_[0:4, :n], AF.Exp, [ar], [R_s0, R_s1])
            mxt, mxr = tmpf.get()
            tmp_, tmpr = tmpf.get()
            if CUT < 2:
                continue
            for k, ch in enumerate(T['chunks']):
                c, o0, st = ch['c'], ch['off'], ch['stream']
                st_action(ch['pre'], l, st, 'm', 0)
                mrow = STB[st]['m']
                mr = STR[st]['m'][0]
                e1 = o0 + c - 1
                ts(mxt[0:4, o0:o0 + c], g_[0:4, o0:o0 + c], mrow[0:4, 0:1], None, ALU.max, None, [gr2, mr], [mxr])
                act(rows[0:4, 0, o0:o0 + c], mxt[0:4, o0:o0 + c], AF.Exp, [mxr], [R_s0, R_s1], scale=-1.0)
                act(rows[0:4, 1, o0:o0 + c], mxt[0:4, o0:o0 + c], AF.Exp, [mxr, mr], [R_s0, R_s1], scale=-1.0,
                    bias=mrow[0:4, 0:1])
                tt(tmp_[0:4, o0:o0 + c], bs[0:4, o0:o0 + c], mxt[0:4, o0:o0 + c], ALU.subtract, [bsr, mxr], [tmpr])
                act(rows[0:4, 2, o0:o0 + c], tmp_[0:4, o0:o0 + c], AF.Exp, [tmpr], [R_s0, R_s1])
                mmx, mmr = csp.get()
                tt(mmx[0:4, 0:1], mrow[0:4, 0:1], g_[0:4, e1:e1 + 1], ALU.max, [mr, gr2], [mmr])
                act(sct[0:4, 2 * k:2 * k + 1], mmx[0:4, 0:1], AF.Exp, [mmr, mr], [R_sct], scale=-1.0, bias=mrow[0:4, 0:1])
                act(sct[0:4, 2 * k + 1:2 * k + 2], mmx[0:4, 0:1], AF.Exp, [mmr], [R_sct], scale=-1.0)
                tt(mrow[0:4, 0:1], mmx[0:4, 0:1], bs[0:4, e1:e1 + 1], ALU.subtract, [mmr, bsr], [mr])
                st_action(ch['post'], l, st, 'm', 0)
            if CUT < 3:
                continue
            for k, ch in enumerate(T['chunks']):
                c, o0 = ch['c'], ch['off']
                pc, pcr = PF()
                for kind in range(4):
                    mm(pc[0:c, 4 * kind:4 * kind + 4], rows[0:4, kind, o0:o0 + c], identf[0:4, 0:4], True, True,
                       [R_s0, R_s1, R_cst], [pcr])
                cp(colsb[0:c, ch['g'], :], pc[0:c, 0:16], [pcr], [R_cols[ch['g']]])
            nch = len(T['chunks'])
            for h in range(4):
                pc, pcr = PF()
                mm(pc[:, 0:2 * nch], cst[0:4, C_SEL + 128 * h:C_SEL + 128 * h + 128], sct[0:4, 0:2 * nch], True, True,
                   [R_cst, R_sct], [pcr])
                cp(scb[:, ti, h, 0:2 * nch], pc[:, 0:2 * nch], [pcr], [R_scb[ti][h]])
        for h in range(4):
            if CUT < 4:
                continue
            parts = wjob([(Wl[:, OFF['mx'] + 128 * h:OFF['mx'] + 128 * h + 128], KC, 128),
                          (Wl[:, OFF['mz'] + 128 * h:OFF['mz'] + 128 * h + 128], KC, 128),
                          (w_mq[l, h], 1, 128), (w_mk[l, h], 1, 128), (w_mv[l, h], 1, 128)])
            (wx, rx), (wz, rz), (wq, rq), (wk, rk), (wv, rv) = parts
            cw = vec[:, l, V_CW + 4 * h:V_CW + 4 * h + 4]
            for ti, T in enumerate(tiles):
                n = T['n']
                px, pxr = PF()
                proj_fm(px, pxr, 0, 128, wx, rx, ti, T)
                mxf, mxfr = mxfp.get()
                mxb, mxbr = tmpb.get()
                cp(mxb[:, :n], px[:, :n], [pxr], [mxbr], eng='act')
                cT, cTr = ctp.get()
                base = 0
                for sg in T['segs']:
                    o0, sn, st = sg['off'], sg['n'], sg['stream']
                    ch0 = [ch for ch in T['chunks'] if ch['off'] == o0][0]
                    chl = [ch for ch in T['chunks'] if ch['off'] + ch['c'] == o0 + sn][0]
                    st_action(ch0['pre'], l, st, 'conv', h)
                    cvb = STB[st]['conv']
                    cvr = STR[st]['conv'][h]
                    cp(mxf[:, base:base + 3], cvb[:, h, 0:3], [cvr], [mxfr])
                    cp(mxf[:, base + 3:base + 3 + sn], px[:, o0:o0 + sn], [pxr], [mxfr])
                    cp(cvb[:, h, 0:3], mxf[:, base + sn:base + sn + 3], [mxfr], [cvr])
                    st_action(chl['post'], l, st, 'conv', h)
                    ts(cT[:, o0:o0 + sn], mxf[:, base:base + sn], cw[:, 0:1], vec[:, l, V_CB + h:V_CB + h + 1],
                       ALU.mult, ALU.add, [mxfr, R_vec], [cTr])
                    for j in range(1, 4):
                        stt(cT[:, o0:o0 + sn], mxf[:, base + j:base + j + sn], cw[:, j:j + 1], cT[:, o0:o0 + sn],
                            ALU.mult, ALU.add, [mxfr, R_vec, cTr], [cTr])
                    base += sn + 3
                act(cT[:, :n], cT[:, :n], AF.Silu, [cTr], [cTr])
                if CUT < 5:
                    continue
                cb_, cbr_ = tmpb.get()
                cp(cb_[:, :n], cT[:, :n], [cTr], [cbr_])
                pq, pqr = PF()
                mm(pq[:, :n], wq[:, 0, :], cb_[:, :n], True, True, [rq, cbr_], [pqr])
                qb, qbr = tmpb.get()
                cp(qb[:, :n], pq[:, :n], [pqr], [qbr], eng='act')
                pk_, pkr_ = PF()
                mm(pk_[:, :n], wk[:, 0, :], cb_[:, :n], True, True, [rk, cbr_], [pkr_])
                kb, kbr = tmpb.get()
                ts(kb[:, :n], pk_[:, :n], float(128 ** -0.5), None, ALU.mult, None, [pkr_], [kbr])
                pz, pzr = PF()
                proj_fm(pz, pzr, 0, 128, wz, rz, ti, T)
                sig, sgr = gatep.get()
                act(sig[:, :n], pz[:, :n], AF.Sigmoid, [pzr], [sgr])
                if CUT < 6:
                    continue
                for k, ch in enumerate(T['chunks']):
                    c, o0, st, g = ch['c'], ch['off'], ch['stream'], ch['g']
                    st_action(ch['pre'], l, st, 'ml', h)
                    Cb = STB[st]['ml']
                    Cr = STR[st]['ml'][h]
                    t16, r16 = refresh16(st, 'ml', h)
                    col = colsb[0:c, g, :]
                    Rc = R_cols[g]
                    va, var_ = vaugp.get()
                    pv, pvr = PF()
                    mm(pv[0:c, 0:128], mxb[:, o0:o0 + c], wv[:, 0, :], True, True, [mxbr, rv], [pvr])
                    cp(va[0:c, 0:128], pv[0:c, 0:128], [pvr], [var_], eng='act')
                    pk, pkr = PB()
                    tr(pk[0:c, 0:128], kb[:, o0:o0 + c], identb[:, :], [kbr, R_idb], [pkr])
                    kea, kear = cbp.get()
                    ts(kea[0:c, 0:128], pk[0:c, 0:128], col[:, 12 + h:13 + h], None, ALU.mult, None, [pkr, Rc], [kear])
                    psc, pscr = PF()
                    mm(psc[0:c, 0:c], kb[:, o0:o0 + c], qb[:, o0:o0 + c], True, True, [kbr, qbr], [pscr])
                    P, Pr = cbp.get()
                    stt(P[0:c, 0:c], psc[0:c, 0:c], col[:, 12 + h:13 + h], maskT[0:c, 0:c], ALU.mult, ALU.mult,
                        [pscr, Rc, R_cst], [Pr])
                    pA, pAr = PF()
                    mm(pA[0:c, 0:129], P[0:c, 0:c], va[0:c, 0:129], True, True, [Pr, var_], [pAr])
                    pB, pBr = PF()
                    mm(pB[0:c, 0:129], qb[:, o0:o0 + c], t16[:, 0:129], True, True, [qbr, r16], [pBr])
                    tA, tAr = cfp.get()
                    ts(tA[0:c, 0:129], pA[0:c, 0:129], col[:, 0 + h:1 + h], None, ALU.mult, None, [pAr, Rc], [tAr])
                    num, numr = cfp.get()
                    stt(num[0:c, 0:129], pB[0:c, 0:129], col[:, 4 + h:5 + h], tA[0:c, 0:129], ALU.mult, ALU.add,
                        [pBr, Rc, tAr], [numr])
                    sc_, scr_ = csp.get()
                    stt(sc_[0:c, 0:1], num[0:c, 128:129], -1.0, num[0:c, 128:129], ALU.mult, ALU.max, [numr], [scr_])
                    tt(sc_[0:c, 1:2], sc_[0:c, 0:1], col[:, 8 + h:9 + h], ALU.max, [scr_, Rc], [scr_])
                    recip(sc_[0:c, 2:3], sc_[0:c, 1:2], [scr_], [scr_])
                    hh, hhr = cfp.get()
                    ts(hh[0:c, 0:128], num[0:c, 0:128], sc_[0:c, 2:3], None, ALU.mult, None, [numr, scr_], [hhr])
                    pd, pdr = PF()
                    mm(pd[:, 0:129], kea[0:c, 0:128], va[0:c, 0:129], True, True, [kear, var_], [pdr])
                    tc_, tcr = cfp.get()
                    ts(tc_[:, 0:129], pd[:, 0:129], scb[:, ti, h, 2 * k + 1:2 * k + 2], None, ALU.mult, None,
                       [pdr, R_scb[ti][h]], [tcr])
                    stt(Cb[:, h, 0:129], Cb[:, h, 0:129], scb[:, ti, h, 2 * k:2 * k + 1], tc_[:, 0:129], ALU.mult, ALU.add,
                        [Cr, tcr, R_scb[ti][h]], [Cr])
                    headnorm_gate(hh[0:c, 0:128], [hhr], c, sig[:, o0:o0 + c], [sgr], 4 + h, ti,
                                  slice(T['col'] + o0, T['col'] + o0 + c),
                                  skip=(cT[:, o0:o0 + c], [cTr], vec[:, l, V_SK + h:V_SK + h + 1]))
                    st_action(ch['post'], l, st, 'ml', h)


    ssp = Pool(nc, "ssp", 3, [128, 16], F32)
    okp = Pool(nc, "okp", 3, [128, 4, 128], BF16)

    def hn_part1(cx, k, src, srcR, c, scale_ap=None, scaleR=()):
        if 'ss' not in cx:
            cx['ss'] = ssp.get()
            cx['ok'] = okp.get()
            mset(cx['ss'][0][:, 0:16], 1.0, [cx['ss'][1]])
        ss, ssr = cx['ss']
        ok, okr = cx['ok']
        junk, jr = cfp.get()
        if scale_ap is None:
            act(junk[0:c, 0:128], src, AF.Square, srcR, [jr, ssr], accum=ss[0:c, k:k + 1])
            cp(ok[0:c, k, :], src, srcR, [okr], eng='act')
        else:
            act(junk[0:c, 0:128], src, AF.Square, srcR + list(scaleR), [jr, ssr], accum=ss[0:c, k:k + 1], scale=scale_ap)
            ts(ok[0:c, k, :], src, scale_ap, None, ALU.mult, None, srcR + list(scaleR), [okr])

    def hn_finish(it, cx, gate, gateR, oi, skip=None):
        ti, T = it['ti'], it['T']
        chs = T['chunks']
        nch = len(chs)
        ss, ssr = cx['ss']
        ok, okr = cx['ok']
        act(ss[:, 4:4 + nch], ss[:, 0:nch], AF.Ln, [ssr, R_eps], [ssr], bias=epsb[:, 0:1], scale=1.0 / 128)
        act(ss[:, 8:8 + nch], ss[:, 4:4 + nch], AF.Exp, [ssr], [ssr], scale=-0.5)
        for k, ch in enumerate(chs):
            c, o0 = ch['c'], ch['off']
            colsl = slice(T['col'] + o0, T['col'] + o0 + c)
            on, onr = cbp.get()
            amul(on[0:c, 0:128], ok[0:c, k, :], ss[0:c, 8 + k:9 + k], [okr, ssr], [onr])
            pt, ptr = PB()
            tr(pt[:, 0:c], on[0:c, 0:128], identb[0:c, 0:c], [onr, R_idb], [ptr])
            if skip is None:
                tt(ob[:, oi, colsl], pt[:, 0:c], gate[:, o0:o0 + c], ALU.mult, [ptr] + gateR, [Ro[oi][ti]])
            else:
                cT, cR, sk_ap = skip
                t2, t2r = cfp.get()
                stt(t2[:, 0:c], cT[:, o0:o0 + c], sk_ap, pt[:, 0:c], ALU.mult, ALU.add, [ptr, R_vec] + cR, [t2r])
                tt(ob[:, oi, colsl], t2[:, 0:c], gate[:, o0:o0 + c], ALU.mult, [t2r] + gateR, [Ro[oi][ti]])

    ebp = Pool(nc, "ebp", 2, [128, 512], F32)

    def run_to_end(gen):
        try:
            while True:
                next(gen)
        except StopIteration as e:
            return e.value

    def pipeline(items, after_item):
        ctx = run_to_end(items[0]['fns'][0](items[0]))
        for i, it in enumerate(items):
            stage, chA, chB, finish = it['fns']
            nxt = items[i + 1] if i + 1 < len(items) else None
            gen = nxt['fns'][0](nxt) if nxt is not None else None
            nctx = None
            gdone = gen is None
            chs = it['T']['chunks']
            a = chA(it, ctx, chs[0])
            for k, ch in enumerate(chs):
                an = chA(it, ctx, chs[k + 1]) if k + 1 < len(chs) else None
                if not gdone and nxt['mode'] == 1 and k == 1:
                    nctx = run_to_end(gen)
                    gdone = True
                if not gdone and nxt['mode'] != 1:
                    try:
                        next(gen)
                    except StopIteration as e:
                        gdone = True
                        nctx = e.value
                ctx['nxt_ch'] = chs[k + 1] if k + 1 < len(chs) else None
                chB(it, ctx, ch, a)
                a = an
            finish(it, ctx)
            if not gdone:
                nctx = run_to_end(gen)
            after_item(it)
            ctx = nctx

    vbuf = mixb[:].rearrange("p k n -> p (k n)").rearrange("p (b g n) -> p b g n", b=2, g=8)

    def vregs(b, g):
        k = 4 * b + g // 2
        return [Rm[k][0], Rm[k][1], Rm[k][2]]

    def vview(b, g, h):
        return vbuf[:, b, g, 128 * h:128 * h + 128], vregs(b, g)

    def v_prepass(l, tiles, b, kind):
        (wv, rv), = wjobp(kind, 0, l, slot=A_SLOTS[b])
        for ti, T in enumerate(tiles):
            for ch in T['chunks']:
                c = ch['c']
                c0 = T['col'] + ch['off']
                ps, pr = PF()
                for kc in range(KC):
                    mm(ps[0:c, 0:512], hT[:, kc, c0:c0 + c], wv[:, kc, :], kc == 0, kc == KC - 1, [Rh[kc][ti], rv], [pr])
                cp(vbuf[0:c, b, ch['g'], :], ps[0:c, 0:512], [pr], vregs(b, ch['g']), eng='act')

    def get_t16(cx, st, kind, h):
        t = cx.pop('t16next', None)
        if t is not None:
            return t
        return refresh16(st, kind, h)

    def early_refresh(cx, ch, kind, h):
        nx = cx.get('nxt_ch')
        if nx is not None and nx['pre'] is None and nx['stream'] == ch['stream']:
            cx['t16next'] = refresh16(ch['stream'], kind, h)

    GETJOB = {}

    def mixers(l, tiles, stages):
        nt = len(tiles)
        A_jobs = ([('ret', h) for h in range(4)] if 'ret' in stages else []) + ([('gla', h) for h in range(4)] if 'gla' in stages else [])
        B_jobs = [('ml', h) for h in range(4)] if 'ml' in stages else []
        issued = {}

        def issue(stream, j):
            jobs, slots = (A_jobs, A_SLOTS) if stream == 'A' else (B_jobs, B_SLOTS)
            if j < len(jobs) and (stream, j) not in issued:
                issued[(stream, j)] = wjobp(jobs[j][0], jobs[j][1], l, slot=slots[j % 2])

        def getter(stream, base):
            def get(h):
                return issued[(stream, base + h)]
            return get
        GETJOB['ret'] = getter('A', 0)
        GETJOB['gla'] = getter('A', 4 if 'ret' in stages else 0)
        GETJOB['ml'] = getter('B', 0)
        if 'ml' in stages:
            ml_prepass(l, tiles)
        if 'gla' in stages:
            gla_prepass(l, tiles)
        issue('B', 0)
        issue('B', 1)
        if 'ret' in stages:
            v_prepass(l, tiles, 0, 'retv')
        if 'gla' in stages:
            v_prepass(l, tiles, 1, 'glav')
        issue('A', 0)
        issue('A', 1)
        fns = {}
        if 'ret' in stages:
            fns['ret'] = branch_ret2(l, tiles)
        if 'gla' in stages:
            fns['gla'] = branch_gla2(l, tiles)
        if 'ml' in stages:
            fns['ml'] = branch_ml2(l, tiles)
        A_items = []
        for ji, (br, h) in enumerate(A_jobs):
            for ti, T in enumerate(tiles):
                A_items.append(dict(br=br, h=h, ti=ti, T=T, last=(ti == nt - 1), fns=fns[br], stream='A', j=ji, mode=1))
        B_items = []
        for ji, (br, h) in enumerate(B_jobs):
            for ti, T in enumerate(tiles):
                B_items.append(dict(br=br, h=h, ti=ti, T=T, last=(ti == nt - 1), fns=fns[br], stream='B', j=ji, mode=ML_MODE))
        items = []
        ia = ib = 0
        while ia < len(A_items) or ib < len(B_items):
            for _ in range(2):
                if ia < len(A_items):
                    items.append(A_items[ia])
                    ia += 1
            if ib < len(B_items):
                items.append(B_items[ib])
                ib += 1

        def after_item(it):
            if it['last']:
                issue(it['stream'], it['j'] + 2)
        if items:
            pipeline(items, after_item)

    def make_items(tiles):
        return [dict(h=h, ti=ti, T=T, last=(ti == len(tiles) - 1)) for h in range(4) for ti, T in enumerate(tiles)]

    JM_L = [0]

    def job_mgr(parts_for):
        jobs = {}

        def get(h):
            if h not in jobs and h < 4:
                jobs[h] = wjobp(parts_for, h, JM_L[0])
            return jobs.get(h)

        def after_item(it):
            if it['last']:
                get(it['h'] + 3)
        get(0)
        get(1)
        get(2)
        return get, after_item

    def branch_ret2(l, tiles):
        Wl = None

        def parts_for(h):
            return [(Wl[:, OFF['rq'] + 64 * h:OFF['rq'] + 64 * h + 64], KC, 64),
                    (w_rs[l][:, 64 * h:64 * h + 64], KC, 64),
                    (Wl[:, OFF['rk'] + 64 * h:OFF['rk'] + 64 * h + 64], KC, 64),
                    (w_rs[l][:, 256 + 64 * h:256 + 64 * h + 64], KC, 64),
                    (Wl[:, OFF['rv'] + 128 * h:OFF['rv'] + 128 * h + 128], KC, 128),
                    (Wl[:, OFF['rg'] + 128 * h:OFF['rg'] + 128 * h + 128], KC, 128)]
        get = GETJOB['ret']

        def stage(it):
            h, ti, T = it['h'], it['ti'], it['T']
            (wqq, rqq), (wkk, rkk), (wg, rgt) = get(h)
            n = T['n']
            load_tab2(T)
            pas = []
            for (w2, r2) in ((wqq, rqq), (wkk, rkk)):
                pa, par = PF()
                proj_fm(pa, par, 0, 128, w2, r2, ti, T)
                t, t_r = tmpb.get()
                tt(t[:, :n], pa[:, :n], tab2[:, 0:n], ALU.mult, [par, R_tab], [t_r])
                pas.append((t, t_r))
            yield
            pg, pgr = PF()
            proj_fm(pg, pgr, 0, 128, wg, rgt, ti, T)
            gate, gr_ = gatep.get()
            act(gate[:, :n], pg[:, :n], AF.Silu, [pgr], [gr_])
            yield
            outs = []
            for (t, t_r), isq in zip(pas, (True, False)):
                pf, pfr = PF()
                mm(pf[0:64, :n], foldb[:, 0:64], t[:, :n], True, True, [R_fold, t_r], [pfr])
                ob_, obr_ = tmpb.get()
                if isq:
                    for sg in T['segs']:
                        o0, sn = sg['off'], sg['n']
                        if sn >= 128:
                            nch = sn // 128
                            gq = cst[0:64, C_GQ + 128 * h:C_GQ + 128 * h + 128].unsqueeze(1).to_broadcast([64, nch, 128])
                            tt(ob_[0:64, o0:o0 + sn].rearrange("p (c t) -> p c t", t=128),
                               pf[0:64, o0:o0 + sn].rearrange("p (c t) -> p c t", t=128), gq, ALU.mult,
                               [pfr, R_cst], [obr_])
                        else:
                            tt(ob_[0:64, o0:o0 + sn], pf[0:64, o0:o0 + sn],
                               cst[0:64, C_GQ + 128 * h:C_GQ + 128 * h + sn], ALU.mult, [pfr, R_cst], [obr_])
                else:
                    cp(ob_[0:64, :n], pf[0:64, :n], [pfr], [obr_], eng='act')
                outs.append((ob_, obr_))
            yield
            (qin, qr), (kr, krr) = outs
            return dict(qin=qin, qr=qr, kr=kr, krr=krr, gate=gate, gr=gr_)

        def chA(it, cx, ch):
            h, ti, T = it['h'], it['ti'], it['T']
            c, o0 = ch['c'], ch['off']
            kr, krr, qin, qr = cx['kr'], cx['krr'], cx['qin'], cx['qr']
            psc, pscr = PF()
            mm(psc[0:c, 0:c], kr[0:64, o0:o0 + c], qin[0:64, o0:o0 + c], True, True, [krr, qr], [pscr])
            P, Pr = cbp.get()
            tt(P[0:c, 0:c], psc[0:c, 0:c], cst[0:c, C_DR + 128 * h:C_DR + 128 * h + c], ALU.mult, [pscr, R_cst], [Pr])
            vt, vr = vview(0, ch['g'], h)
            pk, pkr = PB()
            tr(pk[0:c, 0:64], kr[0:64, o0:o0 + c], identb[0:64, 0:64], [krr, R_idb], [pkr])
            kst, kstr = cbp.get()
            ts(kst[0:c, 0:64], pk[0:c, 0:64], cst[0:c, C_KS + 3 * h + ch['li']:C_KS + 3 * h + ch['li'] + 1], None,
               ALU.mult, None, [pkr, R_cst], [kstr])
            return dict(P=P, Pr=Pr, vt=vt, vr=vr, kst=kst, kstr=kstr)

        def chB(it, cx, ch, a):
            h, ti, T = it['h'], it['ti'], it['T']
            c, o0, st = ch['c'], ch['off'], ch['stream']
            qin, qr = cx['qin'], cx['qr']
            st_action(ch['pre'], l, st, 'ret', h)
            Sb = STB[st]['ret']
            Sr = STR[st]['ret'][h]
            t16, r16 = get_t16(cx, st, 'ret', h)
            pd, pdr = PF()
            mm(pd[0:64, 0:128], a['kst'][0:c, 0:64], a['vt'][0:c, 0:128], True, True, [a['kstr']] + a['vr'], [pdr])
            stt(Sb[0:64, h, :], Sb[0:64, h, :], float(GAM[h] ** c), pd[0:64, 0:128], ALU.mult, ALU.add, [Sr, pdr], [Sr])
            early_refresh(cx, ch, 'ret', h)
            po, por = PF()
            if c <= 64:
                mm(po[0:c, 0:128], a['P'][0:c, 0:c], a['vt'][0:c, 0:128], True, False, [a['Pr']] + a['vr'], [por])
                mm(po[0:c, 0:128], qin[0:64, o0:o0 + c], t16[0:64, 0:128], False, True, [qr, r16], [por])
            else:
                mm(po[0:c, 0:128], qin[0:64, o0:o0 + c], t16[0:64, 0:128], True, False, [qr, r16], [por])
                mm(po[0:c, 0:128], a['P'][0:c, 0:c], a['vt'][0:c, 0:128], False, True, [a['Pr']] + a['vr'], [por])
            hn_part1(cx, T['chunks'].index(ch), po[0:c, 0:128], [por], c)
            st_action(ch['post'], l, st, 'ret', h)

        def finish(it, cx):
            hn_finish(it, cx, cx['gate'], [cx['gr']], it['h'])

        return (stage, chA, chB, finish)

    def gla_prepass(l, tiles):
        Wl = None
        wsmall(gaw[:].rearrange("p k n -> p (k n)"), 'gaw', l, R_gaw)
        wsmall(wa2[:], 'wa2', l, R_wa2, npart=16)
        for ti, T in enumerate(tiles):
            n = T['n']
            pg, pgr = PF()
            proj_fm(pg, pgr, 0, 16, gaw, R_gaw, ti, T)
            cp(gaT[0:16, T['col']:T['col'] + n], pg[0:16, :n], [pgr], [R_gaT[ti]])


    def branch_gla2(l, tiles):
        Wl = None
        def parts_for(h):
            return [(Wl[:, OFF['gq'] + 64 * h:OFF['gq'] + 64 * h + 64], KC, 64),
                    (Wl[:, OFF['gk'] + 64 * h:OFF['gk'] + 64 * h + 64], KC, 64),
                    (Wl[:, OFF['gv'] + 128 * h:OFF['gv'] + 128 * h + 128], KC, 128),
                    (Wl[:, OFF['gr'] + 128 * h:OFF['gr'] + 128 * h + 128], KC, 128)]
        get = GETJOB['gla']

        def stage(it):
            h, ti, T = it['h'], it['ti'], it['T']
            (wqk, rqk), (wg, rgt) = get(h)
            n = T['n']
            cs_ = slice(T['col'], T['col'] + n)
            pz, pzr = PF()
            mm(pz[:, :n], wa2[0:16, 128 * h:128 * h + 128], gaT[0:16, cs_], True, True, [R_wa2, R_gaT[ti]], [pzr])
            e, er = tmpf.get()
            act(e[:, :n], pz[:, :n], AF.Exp, [pzr, R_nvec], [er], bias=nvec[:, l, V_BA + h:V_BA + h + 1], scale=-1.0)
            sp_, spr = tmpf.get()
            act(sp_[:, :n], e[:, :n], AF.Ln, [er], [spr], bias=1.0)
            bs, bsr = tmpf.get()
            rm = cst[:, C_RMS:C_RMS + 80] if T['kind'] == 'sp' else cst[:, C_RM:C_RM + 512]
            scan(bs[:, :n], rm, sp_[:, :n], 0.0, ALU.mult, ALU.add, [spr, R_cst], [bsr])
            ebn, ebr = ebp.get()
            act(ebn[:, :n], bs[:, :n], AF.Exp, [bsr, R_cst], [ebr], scale=cst[:, C_ESC:C_ESC + 1])
            yield
            pg, pgr = PF()
            proj_fm(pg, pgr, 0, 128, wg, rgt, ti, T)
            gate, gr_ = gatep.get()
            act(gate[:, :n], pg[:, :n], AF.Silu, [pgr], [gr_])
            yield
            pqk, pqkr = PF()
            proj_fm(pqk, pqkr, 0, 128, wqk, rqk, ti, T)
            qk, qkr = tmpb.get()
            stt(qk[:, :n], pqk[:, :n], cst[:, C_QSC:C_QSC + 1], ebn[:, :n], ALU.mult, ALU.mult, [pqkr, ebr, R_cst], [qkr])
            pf, pfr = PF()
            mm(pf[0:64, :n], selb[:, 0:64], qk[:, :n], True, True, [R_sel, qkr], [pfr])
            kin, kr_ = tmpb.get()
            cp(kin[0:64, :n], pf[0:64, :n], [pfr], [kr_], eng='act')
            yield
            return dict(qin=qk, qr=qkr, kin=kin, kr=kr_, gate=gate, gr=gr_, eb=ebn, ebr=ebr)

        def chA(it, cx, ch):
            h, ti, T = it['h'], it['ti'], it['T']
            c, o0 = ch['c'], ch['off']
            kin, kr_, qin, qr = cx['kin'], cx['kr'], cx['qin'], cx['qr']
            vt, vr = vview(1, ch['g'], h)
            pk, pkr = PB()
            tr(pk[0:c, 0:64], kin[0:64, o0:o0 + c], identb[0:64, 0:64], [kr_, R_idb], [pkr])
            kt, ktr = cbp.get()
            cp(kt[0:c, 0:64], pk[0:c, 0:64], [pkr], [ktr])
            psc, pscr = PF()
            mm(psc[0:c, 0:c], kin[0:64, o0:o0 + c], qin[0:64, o0:o0 + c], True, True, [kr_, qr], [pscr])
            P, Pr = cbp.get()
            tt(P[0:c, 0:c], psc[0:c, 0:c], maskT[0:c, 0:c], ALU.mult, [pscr, R_cst], [Pr])
            return dict(P=P, Pr=Pr, vt=vt, vr=vr, kt=kt, ktr=ktr)

        def chB(it, cx, ch, a):
            h, ti, T = it['h'], it['ti'], it['T']
            c, o0, st = ch['c'], ch['off'], ch['stream']
            qin, qr, eb, ebr = cx['qin'], cx['qr'], cx['eb'], cx['ebr']
            st_action(ch['pre'], l, st, 'gla', h)
            Sb = STB[st]['gla']
            Sr = STR[st]['gla'][h]
            t16, r16 = get_t16(cx, st, 'gla', h)
            pd, pdr = PF()
            mm(pd[0:64, 0:128], a['kt'][0:c, 0:64], a['vt'][0:c, 0:128], True, True, [a['ktr']] + a['vr'], [pdr])
            tt(Sb[0:64, h, :], Sb[0:64, h, :], pd[0:64, 0:128], ALU.add, [Sr, pdr], [Sr])
            ts(Sb[0:64, h, :], Sb[0:64, h, :], eb[0:64, o0 + c - 1:o0 + c], None, ALU.mult, None, [Sr, ebr], [Sr])
            early_refresh(cx, ch, 'gla', h)
            po, por = PF()
            if c <= 64:
                mm(po[0:c, 0:128], a['P'][0:c, 0:c], a['vt'][0:c, 0:128], True, False, [a['Pr']] + a['vr'], [por])
                mm(po[0:c, 0:128], qin[0:64, o0:o0 + c], t16[0:64, 0:128], False, True, [qr, r16], [por])
            else:
                mm(po[0:c, 0:128], qin[0:64, o0:o0 + c], t16[0:64, 0:128], True, False, [qr, r16], [por])
                mm(po[0:c, 0:128], a['P'][0:c, 0:c], a['vt'][0:c, 0:128], False, True, [a['Pr']] + a['vr'], [por])
            hn_part1(cx, T['chunks'].index(ch), po[0:c, 0:128], [por], c)
            st_action(ch['post'], l, st, 'gla', h)

        def finish(it, cx):
            hn_finish(it, cx, cx['gate'], [cx['gr']], 8 + it['h'])

        return (stage, chA, chB, finish)

    def branch_ml2(l, tiles):
        Wl = None

        def parts_for(h):
            return [(Wl[:, OFF['mx'] + 128 * h:OFF['mx'] + 128 * h + 128], KC, 128),
                    (Wl[:, OFF['mz'] + 128 * h:OFF['mz'] + 128 * h + 128], KC, 128),
                    (w_mq[l, h], 1, 128), (w_mk[l, h], 1, 128), (w_mv[l, h], 1, 128)]
        get = GETJOB['ml']

        def stage(it):
            h, ti, T = it['h'], it['ti'], it['T']
            (wx, rx), (wz, rz), (wq, rq), (wk, rk), (wv, rv) = get(h)
            cw = vec[:, l, V_CW + 4 * h:V_CW + 4 * h + 4]
            n = T['n']
            px, pxr = PF()
            proj_fm(px, pxr, 0, 128, wx, rx, ti, T)
            mxf, mxfr = mxfp.get()
            mxb, mxbr = tmpb.get()
            cp(mxb[:, :n], px[:, :n], [pxr], [mxbr], eng='act')
            cT, cTr = ctp.get()
            base = 0
            for sg in T['segs']:
                o0, sn, st = sg['off'], sg['n'], sg['stream']
                ch0 = [ch for ch in T['chunks'] if ch['off'] == o0][0]
                chl = [ch for ch in T['chunks'] if ch['off'] + ch['c'] == o0 + sn][0]
                st_action(ch0['pre'], l, st, 'conv', h)
                cvb = STB[st]['conv']
                cvr = STR[st]['conv'][h]
                cp(mxf[:, base:base + 3], cvb[:, h, 0:3], [cvr], [mxfr])
                cp(mxf[:, base + 3:base + 3 + sn], px[:, o0:o0 + sn], [pxr], [mxfr])
                cp(cvb[:, h, 0:3], mxf[:, base + sn:base + sn + 3], [mxfr], [cvr])
                st_action(chl['post'], l, st, 'conv', h)
                ts(cT[:, o0:o0 + sn], mxf[:, base:base + sn], cw[:, 0:1], vec[:, l, V_CB + h:V_CB + h + 1],
                   ALU.mult, ALU.add, [mxfr, R_vec], [cTr], eng=CONV_ENG)
                for j in range(1, 4):
                    stt(cT[:, o0:o0 + sn], mxf[:, base + j:base + j + sn], cw[:, j:j + 1], cT[:, o0:o0 + sn],
                        ALU.mult, ALU.add, [mxfr, R_vec, cTr], [cTr], eng=CONV_ENG)
                base += sn + 3
            act(cT[:, :n], cT[:, :n], AF.Silu, [cTr], [cTr])
            cb_, cbr_ = tmpb.get()
            cp(cb_[:, :n], cT[:, :n], [cTr], [cbr_])
            yield
            pq, pqr = PF()
            mm(pq[:, :n], wq[:, 0, :], cb_[:, :n], True, True, [rq, cbr_], [pqr])
            qb, qbr = tmpb.get()
            cp(qb[:, :n], pq[:, :n], [pqr], [qbr], eng='act')
            pk_, pkr_ = PF()
            mm(pk_[:, :n], wk[:, 0, :], cb_[:, :n], True, True, [rk, cbr_], [pkr_])
            kb, kbr = tmpb.get()
            ts(kb[:, :n], pk_[:, :n], float(128 ** -0.5), None, ALU.mult, None, [pkr_], [kbr])
            yield
            pz, pzr = PF()
            proj_fm(pz, pzr, 0, 128, wz, rz, ti, T)
            sig, sgr = gatep.get()
            act(sig[:, :n], pz[:, :n], AF.Sigmoid, [pzr], [sgr])
            yield
            return dict(mxb=mxb, mxbr=mxbr, cT=cT, cTr=cTr, qb=qb, qbr=qbr, kb=kb, kbr=kbr, sig=sig, sgr=sgr, wv=wv, rv=rv)

        def chA(it, cx, ch):
            h, ti, T = it['h'], it['ti'], it['T']
            c, o0, g = ch['c'], ch['off'], ch['g']
            col = colsb[0:c, g, :]
            Rc = R_cols[g]
            kb, kbr, qb, qbr = cx['kb'], cx['kbr'], cx['qb'], cx['qbr']
            va, var_ = vaugp.get()
            pv, pvr = PF()
            mm(pv[0:c, 0:128], cx['mxb'][:, o0:o0 + c], cx['wv'][:, 0, :], True, True, [cx['mxbr'], cx['rv']], [pvr])
            cp(va[0:c, 0:128], pv[0:c, 0:128], [pvr], [var_], eng='act')
            pk, pkr = PB()
            tr(pk[0:c, 0:128], kb[:, o0:o0 + c], identb[:, :], [kbr, R_idb], [pkr])
            kea, kear = cbp.get()
            ts(kea[0:c, 0:128], pk[0:c, 0:128], col[:, 12 + h:13 + h], None, ALU.mult, None, [pkr, Rc], [kear])
            psc, pscr = PF()
            mm(psc[0:c, 0:c], kb[:, o0:o0 + c], qb[:, o0:o0 + c], True, True, [kbr, qbr], [pscr])
            P, Pr = cbp.get()
            stt(P[0:c, 0:c], psc[0:c, 0:c], col[:, 12 + h:13 + h], maskT[0:c, 0:c], ALU.mult, ALU.mult,
                [pscr, Rc, R_cst], [Pr])
            return dict(va=va, var=var_, kea=kea, kear=kear, P=P, Pr=Pr)

        def chB(it, cx, ch, a):
            h, ti, T = it['h'], it['ti'], it['T']
            c, o0, st, g = ch['c'], ch['off'], ch['stream'], ch['g']
            k = T['chunks'].index(ch)
            col = colsb[0:c, g, :]
            Rc = R_cols[g]
            qb, qbr = cx['qb'], cx['qbr']
            va, var_ = a['va'], a['var']
            st_action(ch['pre'], l, st, 'ml', h)
            Cb = STB[st]['ml']
            Cr = STR[st]['ml'][h]
            t16, r16 = get_t16(cx, st, 'ml', h)
            pd, pdr = PF()
            mm(pd[:, 0:129], a['kea'][0:c, 0:128], va[0:c, 0:129], True, True, [a['kear'], var_], [pdr])
            tc_, tcr = cfp.get()
            amul(tc_[:, 0:129], pd[:, 0:129], scb[:, ti, h, 2 * k + 1:2 * k + 2], [pdr, R_scb[ti][h]], [tcr])
            stt(Cb[:, h, 0:129], Cb[:, h, 0:129], scb[:, ti, h, 2 * k:2 * k + 1], tc_[:, 0:129], ALU.mult, ALU.add,
                [Cr, tcr, R_scb[ti][h]], [Cr])
            early_refresh(cx, ch, 'ml', h)
            pA, pAr = PF()
            mm(pA[0:c, 0:129], a['P'][0:c, 0:c], va[0:c, 0:129], True, True, [a['Pr'], var_], [pAr])
            pB, pBr = PF()
            mm(pB[0:c, 0:129], qb[:, o0:o0 + c], t16[:, 0:129], True, True, [qbr, r16], [pBr])
            tA, tAr = cfp.get()
            amul(tA[0:c, 0:129], pA[0:c, 0:129], col[:, 0 + h:1 + h], [pAr, Rc], [tAr])
            num, numr = cfp.get()
            stt(num[0:c, 0:129], pB[0:c, 0:129], col[:, 4 + h:5 + h], tA[0:c, 0:129], ALU.mult, ALU.add,
                [pBr, Rc, tAr], [numr])
            sc_, scr_ = csp.get()
            stt(sc_[0:c, 0:1], num[0:c, 128:129], -1.0, num[0:c, 128:129], ALU.mult, ALU.max, [numr], [scr_])
            tt(sc_[0:c, 1:2], sc_[0:c, 0:1], col[:, 8 + h:9 + h], ALU.max, [scr_, Rc], [scr_])
            recip(sc_[0:c, 2:3], sc_[0:c, 1:2], [scr_], [scr_])
            hn_part1(cx, k, num[0:c, 0:128], [numr], c, scale_ap=sc_[0:c, 2:3], scaleR=[scr_])
            st_action(ch['post'], l, st, 'ml', h)

        def finish(it, cx):
            h = it['h']
            hn_finish(it, cx, cx['sig'], [cx['sgr']], 4 + h,
                      skip=(cx['cT'], [cx['cTr']], vec[:, l, V_SK + h:V_SK + h + 1]))

        return (stage, chA, chB, finish)

    def merge_out(l, tiles):
        Wl = None
        for ft in range(KC):
            parts3 = [wjobp('mergeb', 3 * ft + b, l) for b in range(3)]
            parts = [parts3[0][0], parts3[1][0], parts3[2][0], parts3[0][1], parts3[1][1], parts3[2][1]]
            for ti, T in enumerate(tiles):
                n = T['n']
                cs_ = slice(T['col'], T['col'] + n)
                acc, accr = tmpf.get()
                for b in range(3):
                    wz, rz = parts[b]
                    wb, rb = parts[3 + b]
                    pz, pzr = PF()
                    proj_fm(pz, pzr, 0, 128, wz, rz, ti, T)
                    sg, sgr = tmpf.get()
                    act(sg[:, :n], pz[:, :n], AF.Sigmoid, [pzr], [sgr])
                    pp, ppr = PF()
                    for kc in range(4):
                        mm(pp[:, :n], wb[:, kc, :], ob[:, 4 * b + kc, cs_], kc == 0, kc == 3, [rb, Ro[4 * b + kc][ti]], [ppr])
                    if b == 0:
                        tt(acc[:, :n], sg[:, :n], pp[:, :n], ALU.mult, [sgr, ppr], [accr])
                    else:
                        tt(sg[:, :n], sg[:, :n], pp[:, :n], ALU.mult, [sgr, ppr], [sgr])
                        if b == 1:
                            tt(acc[:, :n], acc[:, :n], sg[:, :n], ALU.add, [accr, sgr], [accr])
                        else:
                            tt(mixb[:, ft, cs_], acc[:, :n], sg[:, :n], ALU.add, [accr, sgr], [Rm[ft][ti]])
        for ft in range(KC):
            (wo, ro), = wjobp('out', ft, l)
            for ti, T in enumerate(tiles):
                n = T['n']
                cs_ = slice(T['col'], T['col'] + n)
                pp, ppr = PF()
                for kc in range(KC):
                    mm(pp[:, :n], wo[:, kc, :], mixb[:, kc, cs_], kc == 0, kc == KC - 1, [ro, Rm[kc][ti]], [ppr])
                tt(xT[:, ft, cs_], xT[:, ft, cs_], pp[:, :n], ALU.add, [Rx[ft][ti], ppr], [Rx[ft][ti]])

    def ffn(l, tiles):
        for half in range(2):
            for j in range(11):
                jj = half * 11 + j
                (wg_, rg_), (wv_, rv_) = wjobp('ffi', jj, l)
                for ti, T in enumerate(tiles):
                    n = T['n']
                    cs_ = slice(T['col'], T['col'] + n)
                    pg, pgr = PF()
                    proj_fm(pg, pgr, 0, 128, wg_, rg_, ti, T)
                    sl, slr = tmpf.get()
                    act(sl[:, :n], pg[:, :n], AF.Silu, [pgr], [slr])
                    pv, pvr = PF()
                    proj_fm(pv, pvr, 0, 128, wv_, rv_, ti, T)
                    tt(ob[:, j, cs_], sl[:, :n], pv[:, :n], ALU.mult, [slr, pvr], [Ro[j][ti]])
            for ft in range(KC):
                (wo, ro), = wjobp('ffo', half * 8 + ft, l)
                for ti, T in enumerate(tiles):
                    n = T['n']
                    cs_ = slice(T['col'], T['col'] + n)
                    pp, ppr = PF()
                    for kc in range(11):
                        mm(pp[:, :n], wo[:, kc, :], ob[:, kc, cs_], kc == 0, kc == 10, [ro, Ro[kc][ti]], [ppr])
                    tt(xT[:, ft, cs_], xT[:, ft, cs_], pp[:, :n], ALU.add, [Rx[ft][ti], ppr], [Rx[ft][ti]])

    groups = make_groups()[:ngroups]
    if not all(k in stages for k in ('ret', 'ml', 'gla')):
        allro = [r for rr_ in Ro for r in rr_]
        S.op('dve', lambda: nc.vector.memset(ob[:], 0.0), (), allro)
    for g, tiles in enumerate(groups):
        if 'noload' not in stages:
            load_x(tiles)
        for l in range(depth):
            rmsnorm(tiles, l, V_N1)
            S.mix = FILL
            mixers(l, tiles, stages)
            S.mix = False
            if 'merge' in stages:
                merge_out(l, tiles)
            if 'ffn' in stages:
                rmsnorm(tiles, l, V_N2)
                ffn(l, tiles)
        if 'nofinal' not in stages:
            rmsnorm(tiles, 0, V_NF, dst_bf=False)
    S.emit()
    return nc, S


def host_consts():
    cst = np.zeros((128, C_END), np.float32)
    cst[:, C_ID:C_ID + 128] = np.eye(128, dtype=np.float32)
    s = np.arange(128)[:, None]
    t = np.arange(128)[None, :]
    mask = (s <= t).astype(np.float32)
    cst[:, C_MASK:C_MASK + 128] = mask
    for h in range(4):
        g = np.float64(GAM[h])
        cst[:, C_DR + 128 * h:C_DR + 128 * h + 128] = (mask * (g ** (-(s + 1.0))) * 0.125).astype(np.float32)
        cst[:, C_GQ + 128 * h:C_GQ + 128 * h + 128] = (g ** (t + 1.0)).astype(np.float32) * np.ones((128, 1), np.float32)
        for li, c in enumerate(LENS):
            col = np.where(s[:, 0] < c, g ** (c - 1.0 - s[:, 0]), 0.0) * 0.125
            cst[:, C_KS + 3 * h + li] = col.astype(np.float32)
        cst[h, C_SEL + 128 * h:C_SEL + 128 * h + 128] = 1.0
    rm = np.ones(512, np.float32)
    rm[::128] = 0.0
    ra = np.zeros(512, np.float32)
    ra[::128] = -1e30
    cst[:, C_RM:C_RM + 512] = rm
    cst[:, C_RA:C_RA + 512] = ra
    rms = np.ones(80, np.float32)
    rms[0] = 0.0
    rms[16] = 0.0
    ras = np.zeros(80, np.float32)
    ras[0] = -1e30
    ras[16] = -1e30
    cst[:, C_RMS:C_RMS + 80] = rms
    cst[:, C_RAS:C_RAS + 80] = ras
    cst[:, C_ONE:C_ONE + 128] = 1.0
    cst[0:64, C_ESC] = -1.0 / 16
    cst[64:128, C_ESC] = 1.0 / 16
    cst[0:64, C_QSC] = 0.125
    cst[64:128, C_QSC] = 1.0
    half = 32
    inv = (1.0 / (np.float32(10000.0) ** np.linspace(0.0, 1.0, half, dtype=np.float32))).astype(np.float32)
    pos = np.arange(NPOS, dtype=np.float32)
    ang = (pos[:, None] * inv[None, :]).astype(np.float32)
    cos = np.cos(ang).astype(np.float32).T
    sin = np.sin(ang).astype(np.float32).T
    cs = np.zeros((64, 2, NPOS), np.float32)
    cs[0:32, 0] = cos
    cs[32:64, 0] = cos
    cs[0:32, 1] = -sin
    cs[32:64, 1] = sin
    return cst, cs


_CACHE = {}
NDUMMY = 1
FILL = False
ON_ENG = 'pool'
DUMMY_N = 512
PIPE = True
STAGE_MODE = 1
ML_MODE = 0
CONV_ENG = 'dve'
DBG_H = 0
DBG_OFF = 128
_BUILD_KW = {}
_NCORES = 8


def kernel(x_prompt, x_sample, state_ret, state_mlstm_c, state_mlstm_n, state_mlstm_m, state_mlstm_conv,
           state_gla, meta_tokens, norm1, w_in, b_mlstm_i, b_mlstm_f, conv_w, conv_b, w_mq, w_mk, w_mv,
           m_skip, w_gla_a2, b_gla_a, w_br_ret, w_br_mlstm, w_br_gla, w_out, norm2, w_ffn_in, w_ffn_out, norm_f):
    f = lambda a: np.ascontiguousarray(np.asarray(a, dtype=np.float32))
    if 'nc' not in _CACHE:
        _CACHE['nc'] = build_program(**_BUILD_KW)[0]
    nc = _CACHE['nc']
    cst, cs = host_consts()
    w_in = f(w_in)
    perm = np.concatenate([np.arange(h * 64 + 32, h * 64 + 64).tolist() + np.arange(h * 64, h * 64 + 32).tolist()
                           for h in range(4)]).astype(np.int64)
    w_rs = np.ascontiguousarray(np.concatenate([w_in[:, :, 0:256][:, :, perm], w_in[:, :, 256:512][:, :, perm]], axis=2))
    vecs = np.zeros((128, DEPTH, V_END), np.float32)
    for l in range(DEPTH):
        vecs[:, l, V_N1:V_N1 + 8] = f(norm1)[l].reshape(8, 128).T
        vecs[:, l, V_N2:V_N2 + 8] = f(norm2)[l].reshape(8, 128).T
        vecs[:, l, V_CW:V_CW + 16] = f(conv_w)[l].reshape(4, 4, 128).transpose(2, 1, 0).reshape(128, 16)
        vecs[:, l, V_CB:V_CB + 4] = f(conv_b)[l].reshape(4, 128).T
        vecs[:, l, V_SK:V_SK + 4] = f(m_skip)[l].reshape(4, 128).T
        vecs[0:64, l, V_BA:V_BA + 4] = f(b_gla_a)[l].reshape(4, 64).T
        vecs[64:128, l, V_BA:V_BA + 4] = f(b_gla_a)[l].reshape(4, 64).T
        vecs[0:4, l, V_BI] = f(b_mlstm_i)[l]
        vecs[0:4, l, V_BF] = f(b_mlstm_f)[l]
        vecs[:, l, V_NF:V_NF + 8] = f(norm_f).reshape(8, 128).T
    xp = f(x_prompt)
    xs = f(x_sample)
    srcs = dict(w_in=w_in, w_rs=w_rs, w_mq=f(w_mq).reshape(DEPTH, 512, 128), w_mk=f(w_mk).reshape(DEPTH, 512, 128),
                w_mv=f(w_mv).reshape(DEPTH, 512, 128), w_a2=f(w_gla_a2), w_br0=f(w_br_ret), w_br1=f(w_br_mlstm),
                w_br2=f(w_br_gla), w_out=f(w_out), w_fi=f(w_ffn_in), w_fo=f(w_ffn_out))
    shared = dict(meta=f(meta_tokens), wpack=pack_weights(srcs), cst=cst, vecs=vecs, cs=cs)
    sret, sgla = f(state_ret), f(state_gla)
    sc, sn, sm, scv = f(state_mlstm_c), f(state_mlstm_n), f(state_mlstm_m), f(state_mlstm_conv)
    in_maps = []
    for i in range(8):
        ml = np.concatenate([sc[:, i].transpose(0, 2, 1, 3), sn[:, i].transpose(0, 2, 1)[..., None]], axis=3)
        m = dict(shared)
        m.update(xp=np.ascontiguousarray(xp[2 * i:2 * i + 2]), xs=np.ascontiguousarray(xs[i]),
                 sti_ret=np.ascontiguousarray(sret[:, i].transpose(0, 2, 1, 3))[None],
                 sti_gla=np.ascontiguousarray(sgla[:, i].transpose(0, 2, 1, 3))[None],
                 sti_ml=np.ascontiguousarray(ml)[None],
                 sti_m=np.ascontiguousarray(sm[:, i].reshape(2, 4, 1))[None],
                 sti_conv=np.ascontiguousarray(scv[:, i].reshape(2, 3, 4, 128).transpose(0, 3, 2, 1))[None])
        in_maps.append(m)
    if _NCORES < 8:
        res = run_bass_kernel_spmd(nc, in_maps[:_NCORES], core_ids=list(range(_NCORES)))
        R = list(res.results) + [res.results[0]] * (8 - _NCORES)
    else:
        res = run_bass_kernel_spmd(nc, in_maps, core_ids=list(range(8)))
        R = res.results
    _CACHE["dbg"] = R[0].get("dbg")
    y_prompt = np.concatenate([R[i]["yp"] for i in range(8)], axis=0)
    y_sample = np.stack([R[i]["ys"] for i in range(8)], axis=0)

    def gather(kind, which):
        if which == 'p':
            return np.stack([R[i]["sto_" + kind][s] for i in range(8) for s in range(2)], axis=1)
        return np.stack([R[i]["sto_" + kind][2] for i in range(8)], axis=1)

    outs = [y_prompt, y_sample]
    for which in ('p', 's'):
        ret = gather('ret', which).transpose(0, 1, 3, 2, 4)
        ml = gather('ml', which)
        c = ml[..., 0:128].transpose(0, 1, 3, 2, 4)
        n = ml[..., 128].transpose(0, 1, 3, 2)
        m = gather('m', which)[..., 0]
        cv = gather('conv', which).transpose(0, 1, 4, 3, 2)
        cv = cv.reshape(cv.shape[0], cv.shape[1], 3, 512)
        gla = gather('gla', which).transpose(0, 1, 3, 2, 4)
        outs += [np.ascontiguousarray(a, dtype=np.float32) for a in (ret, c, n, m, cv, gla)]
    return tuple(outs)
```
